# Optimizing a Trainium2 kernel written in Bass

```python
import jax, jax.numpy as jnp
from jax import lax
import numpy as np

D_MODEL = 1024
BATCH = 8
SEQ = 2048
DEPTH = 1

CHUNK = 64
CONV_WIDTH = 4
NORM_EPS = 1e-5
SSD_INNER = 2 * D_MODEL
SSD_HEAD_DIM = 64
SSD_HEADS = SSD_INNER // SSD_HEAD_DIM
SSD_GROUPS = 4
SSD_HEADS_PER_GROUP = SSD_HEADS // SSD_GROUPS
SSD_STATE = 128
SSD_XBC = SSD_INNER + 2 * SSD_GROUPS * SSD_STATE
MLSTM_INNER = D_MODEL
MLSTM_HEADS = 4
MLSTM_HEAD_DIM = MLSTM_INNER // MLSTM_HEADS
D_FF = 4 * D_MODEL
SPLIT_SIZES = (SSD_INNER, SSD_XBC, SSD_HEADS,
               MLSTM_INNER, MLSTM_INNER, MLSTM_INNER, MLSTM_INNER,
               MLSTM_HEADS, MLSTM_HEADS, 2 * D_MODEL)
IN_PROJ_WIDTH = sum(SPLIT_SIZES)

kernel_name = 'hybrid_ssd_mlstm_gated_block'


def rms_norm(x, w):
    xf = x.astype(jnp.float32)
    xf = xf * lax.rsqrt(jnp.mean(xf * xf, axis=-1, keepdims=True) + NORM_EPS)
    return (xf * w.astype(jnp.float32)).astype(x.dtype)


def causal_dwconv(u, w, b):
    S = u.shape[1]
    K = w.shape[0]
    up = jnp.pad(u, ((0, 0), (K - 1, 0), (0, 0)))
    out = b
    for tap in range(K):
        out = out + up[:, tap:tap + S] * w[tap]
    return out


def ssd_chunked(xs, dt, A, Bm, Cm):
    Bsz, S, G, R, P = xs.shape
    N = Bm.shape[-1]
    nc, L = S // CHUNK, CHUNK
    xc = xs.reshape(Bsz, nc, L, G, R, P)
    dtc = dt.reshape(Bsz, nc, L, G, R)
    Bc = Bm.reshape(Bsz, nc, L, G, N)
    Cc = Cm.reshape(Bsz, nc, L, G, N)
    acum = jnp.cumsum(dtc * A, axis=2)
    causal = jnp.tril(jnp.ones((L, L), dtype=bool))
    seg = acum[:, :, :, None] - acum[:, :, None]
    decay = jnp.exp(jnp.where(causal[:, :, None, None], seg, -jnp.inf))
    cb = jnp.einsum('bctgn,bcsgn->bctsg', Cc, Bc)
    w = cb[..., None] * decay * dtc[:, :, None]
    y_intra = jnp.einsum('bctsgr,bcsgrp->bctgrp', w, xc)
    to_end = jnp.exp(acum[:, :, -1:] - acum) * dtc
    states = jnp.einsum('bclgn,bclgr,bclgrp->bcgrpn', Bc, to_end, xc)
    chunk_decay = jnp.exp(acum[:, :, -1])

    def step(hst, inp):
        s, dcy = inp
        return hst * dcy[..., None, None] + s, hst

    h0 = jnp.zeros((Bsz, G, R, P, N), jnp.float32)
    _, h_prev = lax.scan(step, h0, (jnp.moveaxis(states, 1, 0), jnp.moveaxis(chunk_decay, 1, 0)))
    h_prev = jnp.moveaxis(h_prev, 0, 1)
    y_inter = jnp.einsum('bctgn,bcgrpn->bctgrp', Cc, h_prev) * jnp.exp(acum)[..., None]
    return (y_intra + y_inter).reshape(Bsz, S, G, R, P)


def mlstm_chunkwise(q, k, v, i_pre, f_pre):
    Bsz, S, H, dh = q.shape
    nc, L = S // CHUNK, CHUNK
    qc = (q * dh ** -0.5).reshape(Bsz, nc, L, H, dh)
    kc = k.reshape(Bsz, nc, L, H, dh)
    vc = v.reshape(Bsz, nc, L, H, dh)
    li = i_pre.reshape(Bsz, nc, L, H)
    lf = jax.nn.log_sigmoid(f_pre).reshape(Bsz, nc, L, H)
    b = jnp.cumsum(lf, axis=2)
    b_end = b[:, :, -1]
    causal = jnp.tril(jnp.ones((L, L), dtype=bool))
    d_log = b[:, :, :, None] - b[:, :, None] + li[:, :, None]
    d_log = jnp.where(causal[:, :, None], d_log, -jnp.inf)
    g = b_end[:, :, None] - b + li
    m_loc = jnp.max(g, axis=2)
    wg = jnp.exp(g - m_loc[:, :, None])
    c_loc = jnp.einsum('bclh,bclhk,bclhv->bchkv', wg, kc, vc)
    n_loc = jnp.einsum('bclh,bclhk->bchk', wg, kc)

    def step(carry, inp):
        c_st, n_st, m_st = carry
        cl, nl, ml, bl = inp
        m_new = jnp.maximum(bl + m_st, ml)
        a_old = jnp.exp(bl + m_st - m_new)
        a_loc = jnp.exp(ml - m_new)
        c_new = a_old[..., None, None] * c_st + a_loc[..., None, None] * cl
        n_new = a_old[..., None] * n_st + a_loc[..., None] * nl
        return (c_new, n_new, m_new), (c_st, n_st, m_st)

    init = (jnp.zeros((Bsz, H, dh, dh), jnp.float32),
            jnp.zeros((Bsz, H, dh), jnp.float32),
            jnp.zeros((Bsz, H), jnp.float32))
    xs_in = tuple(jnp.moveaxis(t, 1, 0) for t in (c_loc, n_loc, m_loc, b_end))
    _, (c_prev, n_prev, m_prev) = lax.scan(step, init, xs_in)
    c_prev = jnp.moveaxis(c_prev, 0, 1)
    n_prev = jnp.moveaxis(n_prev, 0, 1)
    m_prev = jnp.moveaxis(m_prev, 0, 1)
    inter_log = b + m_prev[:, :, None]
    m_t = jnp.maximum(inter_log, jnp.max(d_log, axis=3))
    w_inter = jnp.exp(inter_log - m_t)
    w_intra = jnp.exp(d_log - m_t[:, :, :, None])
    qk = jnp.einsum('bcthk,bcshk->bctsh', qc, kc) * w_intra
    num = (jnp.einsum('bctsh,bcshv->bcthv', qk, vc)
           + w_inter[..., None] * jnp.einsum('bcthk,bchkv->bcthv', qc, c_prev))
    den = jnp.sum(qk, axis=3) + w_inter * jnp.einsum('bcthk,bchk->bcth', qc, n_prev)
    h = num / jnp.maximum(jnp.abs(den), jnp.exp(-m_t))[..., None]
    return h.reshape(Bsz, S, H, dh)


def setup_inputs(seed: int = 0) -> dict:
    key = jax.random.key(seed)
    ks = jax.random.split(key, 24)
    f32 = jnp.float32

    def nrm(k, shape, scale):
        return jax.random.normal(k, shape, f32) * scale

    dt0 = jnp.exp(jax.random.uniform(ks[5], (DEPTH, SSD_HEADS), f32, np.log(1e-3), np.log(1e-1)))
    return {
        'x': nrm(ks[0], (BATCH, SEQ, D_MODEL), 1.0),
        'norm_mix_w': 1.0 + nrm(ks[1], (DEPTH, D_MODEL), 0.02),
        'w_in': nrm(ks[2], (DEPTH, D_MODEL, IN_PROJ_WIDTH), D_MODEL ** -0.5),
        'conv_ssd_w': nrm(ks[3], (DEPTH, CONV_WIDTH, SSD_XBC), CONV_WIDTH ** -0.5),
        'conv_ssd_b': nrm(ks[4], (DEPTH, SSD_XBC), 0.01),
        'dt_bias': dt0 + jnp.log(-jnp.expm1(-dt0)),
        'a_log': jnp.log(jax.random.uniform(ks[6], (DEPTH, SSD_HEADS), f32, 1.0, 16.0)),
        'd_skip': 1.0 + nrm(ks[7], (DEPTH, SSD_HEADS), 0.1),
        'ssd_norm_w': 1.0 + nrm(ks[8], (DEPTH, SSD_INNER), 0.02),
        'conv_qk_w': nrm(ks[9], (DEPTH, CONV_WIDTH, 2 * MLSTM_INNER), CONV_WIDTH ** -0.5),
        'conv_qk_b': nrm(ks[10], (DEPTH, 2 * MLSTM_INNER), 0.01),
        'i_bias': nrm(ks[11], (DEPTH, MLSTM_HEADS), 0.1),
        'f_bias': jnp.linspace(3.0, 6.0, MLSTM_HEADS, dtype=f32)[None] + nrm(ks[12], (DEPTH, MLSTM_HEADS), 0.1),
        'mlstm_norm_w': 1.0 + nrm(ks[13], (DEPTH, MLSTM_INNER), 0.02),
        'w_br_ssd': nrm(ks[14], (DEPTH, SSD_INNER, D_MODEL), SSD_INNER ** -0.5),
        'w_br_mlstm': nrm(ks[15], (DEPTH, MLSTM_INNER, D_MODEL), MLSTM_INNER ** -0.5),
        'w_out': nrm(ks[16], (DEPTH, D_MODEL, D_MODEL), D_MODEL ** -0.5),
        'norm_mlp_w': 1.0 + nrm(ks[17], (DEPTH, D_MODEL), 0.02),
        'w_up': nrm(ks[18], (DEPTH, D_MODEL, D_FF), D_MODEL ** -0.5),
        'w_down': nrm(ks[19], (DEPTH, D_FF, D_MODEL), D_FF ** -0.5),
        'norm_final_w': 1.0 + nrm(ks[20], (D_MODEL,), 0.02),
    }


def reference(x, norm_mix_w, w_in, conv_ssd_w, conv_ssd_b, dt_bias, a_log, d_skip, ssd_norm_w,
              conv_qk_w, conv_qk_b, i_bias, f_bias, mlstm_norm_w, w_br_ssd, w_br_mlstm, w_out,
              norm_mlp_w, w_up, w_down, norm_final_w):
    f32 = jnp.float32
    Bsz, S, _ = x.shape
    G, R, P, N = SSD_GROUPS, SSD_HEADS_PER_GROUP, SSD_HEAD_DIM, SSD_STATE
    H, dh = MLSTM_HEADS, MLSTM_HEAD_DIM
    offsets = [int(o) for o in np.cumsum(SPLIT_SIZES)[:-1]]
    h = x
    for layer in range(DEPTH):
        u = rms_norm(h, norm_mix_w[layer])
        proj = u @ w_in[layer]
        z, xbc, dt_raw, q, k, v, o, i_pre, f_pre, gates = jnp.split(proj, offsets, axis=-1)

        xbc = jax.nn.silu(causal_dwconv(xbc, conv_ssd_w[layer], conv_ssd_b[layer]))
        xs, bm, cm = jnp.split(xbc, [SSD_INNER, SSD_INNER + G * N], axis=-1)
        dt = jax.nn.softplus(dt_raw.astype(f32) + dt_bias[layer].astype(f32))
        A = -jnp.exp(a_log[layer].astype(f32))
        xs_h = xs.astype(f32).reshape(Bsz, S, G, R, P)
        y = ssd_chunked(xs_h, dt.reshape(Bsz, S, G, R), A.reshape(G, R),
                        bm.astype(f32).reshape(Bsz, S, G, N), cm.astype(f32).reshape(Bsz, S, G, N))
        y = y + d_skip[layer].astype(f32).reshape(G, R)[:, :, None] * xs_h
        y = y.reshape(Bsz, S, SSD_INNER) * jax.nn.silu(z.astype(f32))
        y = rms_norm(y.reshape(Bsz, S, G, SSD_INNER // G),
                     ssd_norm_w[layer].reshape(G, SSD_INNER // G)).reshape(Bsz, S, SSD_INNER)

        qk = jax.nn.silu(causal_dwconv(jnp.concatenate([q, k], axis=-1), conv_qk_w[layer], conv_qk_b[layer]))
        q_c, k_c = jnp.split(qk, 2, axis=-1)
        hm = mlstm_chunkwise(q_c.astype(f32).reshape(Bsz, S, H, dh), k_c.astype(f32).reshape(Bsz, S, H, dh),
                             v.astype(f32).reshape(Bsz, S, H, dh),
                             i_pre.astype(f32) + i_bias[layer].astype(f32),
                             f_pre.astype(f32) + f_bias[layer].astype(f32))
        hm = rms_norm(hm, mlstm_norm_w[layer].reshape(H, dh)).reshape(Bsz, S, MLSTM_INNER)
        hm = jax.nn.sigmoid(o.astype(f32)) * hm

        g_ssd, g_ml = jnp.split(gates, 2, axis=-1)
        mixed = (jax.nn.sigmoid(g_ssd) * (y.astype(h.dtype) @ w_br_ssd[layer])
                 + jax.nn.sigmoid(g_ml) * (hm.astype(h.dtype) @ w_br_mlstm[layer]))
        h = h + mixed @ w_out[layer]

        u = rms_norm(h, norm_mlp_w[layer])
        h = h + jnp.square(jax.nn.relu(u @ w_up[layer])) @ w_down[layer]
    return rms_norm(h, norm_final_w)
```

```python
import contextlib
import math
import numpy as np
import concourse.bass as bass
import concourse.mybir as mybir
from concourse.bass_utils import run_bass_kernel_spmd

F32 = mybir.dt.float32
BF16 = mybir.dt.bfloat16
AF = mybir.ActivationFunctionType
ALU = mybir.AluOpType

ENGS = ["pe", "act", "dve", "pool", "sp"]
NDMASEM = 6
EPS = 1e-5

S = 2048
TB = 1024
NB = S // TB
NT = TB // 128
OFF_Z, OFF_X, OFF_B, OFF_C, OFF_DT = 0, 2048, 4096, 4608, 5120
OFF_Q, OFF_K, OFF_V, OFF_O, OFF_I, OFF_F, OFF_G = 5152, 6176, 7200, 8224, 9248, 9252, 9256
LNSCALE = math.log(256 ** -0.5)


class Buf:
    __slots__ = ("name", "w", "r")

    def __init__(self, name):
        self.name = name
        self.w = None
        self.r = []


class Unit:
    __slots__ = ("eng", "fns", "deps", "cost", "kind", "tag", "idx", "start", "end", "count", "tok")

    def __init__(self, eng, kind, tag, idx):
        self.eng = eng
        self.fns = []
        self.deps = set()
        self.cost = 0.0
        self.kind = kind
        self.tag = tag
        self.idx = idx
        self.start = None
        self.end = None
        self.count = None
        self.tok = None


XLAT = 0.5
WINDOW = 600
LOOKAHEAD = 0.3


class Prog:
    def __init__(self, nc):
        self.nc = nc
        self.units = []
        self.open = {e: None for e in ENGS}
        self.last_barrier = {e: None for e in ENGS}
        self.since_barrier = []
        self.semnames = list(ENGS) + ["d%s%d" % (e, i) for e in ("sp", "pool") for i in range(NDMASEM)]
        self.tag = ""
        self.annot = False
        self.schedule = True

    def _unit(self, eng, kind):
        u = self.open[eng]
        if u is None or kind != "op":
            u = Unit(eng, kind, self.tag, len(self.units))
            self.units.append(u)
            if self.last_barrier[eng] is not None:
                u.deps.add(self.last_barrier[eng])
            self.since_barrier.append(u.idx)
        return u

    def _deps(self, u, reads, writes):
        for b in reads:
            if b.w is not None:
                u.deps.add(b.w)
        for b in writes:
            if b.w is not None:
                u.deps.add(b.w)
            u.deps.update(b.r)
        u.deps.discard(u.idx)
        for b in reads:
            if not b.r or b.r[-1] != u.idx:
                b.r.append(u.idx)
        for b in writes:
            b.w = u.idx
            b.r = []

    def op(self, eng, fn, reads=(), writes=(), inc=True, cost=0.3):
        u = self._unit(eng, "op")
        u.fns.append(fn)
        u.cost += cost
        self._deps(u, reads, writes)
        self.open[eng] = None if inc else u
        return u.idx

    def dma(self, eng, fn, reads=(), writes=(), cost=3.0):
        assert self.open[eng] is None
        u = self._unit(eng, "dma")
        u.fns.append(fn)
        u.cost = cost
        self._deps(u, reads, writes)
        return u.idx

    def barrier(self):
        prev = list(self.since_barrier)
        self.since_barrier = []
        news = {}
        for e in ENGS:
            assert self.open[e] is None
            if e == "pe":
                continue
            u = Unit(e, "bar", "", len(self.units))
            self.units.append(u)
            u.deps.update(prev)
            if self.last_barrier[e] is not None:
                u.deps.add(self.last_barrier[e])
            news[e] = u.idx
        news["pe"] = None
        self.last_barrier = news
        self.since_barrier = [v for v in news.values() if v is not None]

    def wait_all(self, eng, toks):
        u = Unit(eng, "bar", "", len(self.units))
        self.units.append(u)
        u.deps.update(toks)
        if self.last_barrier[eng] is not None:
            u.deps.add(self.last_barrier[eng])

    def _schedule(self):
        import heapq
        units = self.units
        byeng = {e: [u for u in units if u.eng == e] for e in ENGS}
        if not self.schedule:
            return byeng
        nun = len(units)
        succ = [[] for _ in range(nun)]
        ndep = [0] * nun
        for u in units:
            ndep[u.idx] = len(u.deps)
            for d in u.deps:
                succ[d].append(u.idx)
        blev = [0.0] * nun
        for u in reversed(units):
            m = 0.0
            for sidx in succ[u.idx]:
                su = units[sidx]
                t = blev[sidx] + (XLAT if su.eng != u.eng or u.kind == "dma" else 0.0)
                if t > m:
                    m = t
            blev[u.idx] = m + u.cost
        inwin = [False] * nun
        nxt = {e: 0 for e in ENGS}
        nin = {e: 0 for e in ENGS}
        blocked = {e: False for e in ENGS}
        hA = {e: [] for e in ENGS}
        hB = {e: [] for e in ENGS}
        etime = {e: 0.0 for e in ENGS}
        order = {e: [] for e in ENGS}

        def ready_time(u):
            r = 0.0
            for d in u.deps:
                du = units[d]
                t = du.end + (XLAT if du.eng != u.eng or du.kind == "dma" else 0.0)
                if t > r:
                    r = t
            return r

        def admit(e):
            lst = byeng[e]
            while nxt[e] < len(lst) and nin[e] < WINDOW and not blocked[e]:
                u = lst[nxt[e]]
                nxt[e] += 1
                nin[e] += 1
                inwin[u.idx] = True
                if u.kind == "bar":
                    blocked[e] = True
                if ndep[u.idx] == 0:
                    heapq.heappush(hA[e], (ready_time(u), u.idx))

        for e in ENGS:
            admit(e)
        remaining = nun
        while remaining:
            best = None
            for e in ENGS:
                A = hA[e]; B = hB[e]
                while A and A[0][0] <= etime[e]:
                    ii = heapq.heappop(A)[1]
                    heapq.heappush(B, (-blev[ii], ii))
                if B:
                    key = (etime[e], B[0][1])
                    if A and LOOKAHEAD > 0:
                        rA, iA = A[0]
                        ub = units[B[0][1]]
                        if blev[iA] > blev[ub.idx] + 1.0 and rA - etime[e] < min(ub.cost * 0.6, LOOKAHEAD):
                            key = (rA, iA)
                elif A:
                    key = (A[0][0], A[0][1])
                else:
                    continue
                if best is None or key < best[0]:
                    best = (key, e)
            assert best is not None, "scheduler stuck"
            (st, idx), e = best
            if hB[e] and hB[e][0][1] == idx:
                heapq.heappop(hB[e])
            else:
                heapq.heappop(hA[e])
            u = units[idx]
            u.start = st
            if u.kind == "dma":
                etime[e] = st + 0.15
                u.end = st + u.cost
            else:
                etime[e] = st + u.cost
                u.end = etime[e]
            order[e].append(u)
            nin[e] -= 1
            if u.kind == "bar":
                blocked[e] = False
            remaining -= 1
            for sidx in succ[idx]:
                ndep[sidx] -= 1
                if ndep[sidx] == 0 and inwin[sidx]:
                    su = units[sidx]
                    heapq.heappush(hA[su.eng], (ready_time(su), sidx))
            admit(e)
        self.makespan = max(u.end for u in units)
        return order

    def emit(self, sems):
        nc = self.nc
        units = self.units
        order = self._schedule()
        for e in ENGS:
            c = 0
            k = 0
            dval = {}
            for u in order[e]:
                if u.kind == "op":
                    c += 1
                    u.tok = (e, c)
                elif u.kind == "dma":
                    sem = "d%s%d" % (e, k)
                    k = (k + 1) % NDMASEM
                    prev = dval.get(sem, 0)
                    dval[sem] = prev + 16
                    u.tok = (sem, prev + 16)
                    u.count = (sem, prev)
                else:
                    u.tok = None
        K = {e: {} for e in ENGS}
        snap = {}
        queues = {e: [] for e in ENGS}
        ptr = {e: 0 for e in ENGS}
        processed = set()
        total = sum(len(order[e]) for e in ENGS)
        ndone = 0
        while ndone < total:
            progressed = False
            for e in ENGS:
                lst = order[e]
                while ptr[e] < len(lst):
                    u = lst[ptr[e]]
                    if any(d not in processed for d in u.deps):
                        break
                    known = K[e]
                    need = {}
                    srcs = []
                    for d in u.deps:
                        du = units[d]
                        if du.tok is None:
                            continue
                        sname, v = du.tok
                        if du.eng == e and du.kind == "op":
                            if e in ("pe", "sp") and u.kind != "dma":
                                continue
                        if known.get(sname, 0) < v:
                            if need.get(sname, 0) < v:
                                need[sname] = v
                            srcs.append(d)
                    if u.kind == "dma" and u.count[1] > 0:
                        sname, v = u.count
                        if known.get(sname, 0) < v and need.get(sname, 0) < v:
                            need[sname] = v
                    for d in srcs:
                        for sname, v in snap[d].items():
                            if known.get(sname, 0) < v:
                                known[sname] = v
                    final = []
                    for sname, v in need.items():
                        implied = False
                        for d in srcs:
                            du = units[d]
                            if du.tok[0] != sname and snap[d].get(sname, 0) >= v:
                                implied = True
                                break
                        if not implied:
                            final.append((sname, v))
                        if known.get(sname, 0) < v:
                            known[sname] = v
                    queues[e].append((final, u))
                    sn = dict(known)
                    if u.tok is not None:
                        if sn.get(u.tok[0], 0) < u.tok[1]:
                            sn[u.tok[0]] = u.tok[1]
                    snap[u.idx] = sn
                    processed.add(u.idx)
                    ptr[e] += 1
                    ndone += 1
                    progressed = True
            assert progressed, "wait derivation stuck"
        self._check(queues)
        self.queues = queues

        def replay(eobj, name):
            own = sems[name]
            for waits, u in queues[name]:
                for s, v in waits:
                    eobj.wait_ge(sems[s], v)
                n = len(u.fns)
                for j, fn in enumerate(u.fns):
                    ins = fn(eobj)
                    if self.annot:
                        ins.annotate(u.tag)
                    if j == n - 1:
                        if u.kind == "op":
                            ins.then_inc(own, 1)
                        else:
                            ins.then_inc(sems[u.tok[0]], 16)

        with nc.Block() as block:
            @block.tensor
            def _(e):
                replay(e, "pe")

            @block.scalar
            def _(e):
                replay(e, "act")

            @block.vector
            def _(e):
                replay(e, "dve")

            @block.gpsimd
            def _(e):
                replay(e, "pool")

            @block.sync
            def _(e):
                replay(e, "sp")

    def _check(self, queues):
        sem = {}
        ptr = {e: 0 for e in ENGS}
        total = sum(len(q) for q in queues.values())
        donecount = 0
        progress = True
        while progress:
            progress = False
            for e in ENGS:
                q = queues[e]
                while ptr[e] < len(q):
                    waits, u = q[ptr[e]]
                    if all(sem.get(s, 0) >= v for s, v in waits):
                        if u.tok is not None:
                            s, v = u.tok
                            sem[s] = sem.get(s, 0) + (1 if u.kind == "op" else 16)
                            assert sem[s] == v, (s, v, sem[s])
                        ptr[e] += 1
                        donecount += 1
                        progress = True
                    else:
                        break
        assert donecount == total, ("deadlock in emitted program", {e: (ptr[e], len(queues[e])) for e in ENGS})


def build(stop_after=None, dumps=(), annot=False):
    nc = bass.Bass("TRN2", target_bir_lowering=False)
    P = Prog(nc)
    P.annot = annot
    dumps = set(dumps)

    def din(name, shape):
        return nc.dram_tensor(name, shape, F32, kind="ExternalInput")

    x_d = din("x", [S, 1024])
    w_in = din("w_in", [1024, 11304])
    w_brs = din("w_br_ssd", [2048, 1024])
    w_brm = din("w_br_mlstm", [1024, 1024])
    w_out = din("w_out", [1024, 1024])
    w_up = din("w_up", [1024, 4096])
    w_dn = din("w_down", [4096, 1024])
    conv_ssd_w = din("conv_ssd_w", [4, 3072])
    conv_ssd_b = din("conv_ssd_b", [1, 3072])
    conv_qk_w = din("conv_qk_w", [4, 2048])
    conv_qk_b = din("conv_qk_b", [1, 2048])
    vecs = {n: din(n, [1, m]) for n, m in [("norm_mix_w", 1024), ("norm_mlp_w", 1024), ("norm_final_w", 1024),
                                           ("ssd_norm_w", 2048), ("mlstm_norm_w", 1024), ("dt_bias", 32),
                                           ("a_log", 32), ("d_skip", 32), ("i_bias", 4), ("f_bias", 4)]}
    y_d = nc.dram_tensor("y", [S, 1024], F32, kind="ExternalOutput")
    dump_t = {}

    es = contextlib.ExitStack()
    ARENA_BASE, ARENA_TOP = 16512, 229376
    arena = {"p": ARENA_BASE, "n": 0, "hi": 0}
    arena_cache = {}

    def sb(name, shape, dt=F32, stack=None):
        nbytes = (2 if dt == BF16 else 4)
        for d in shape[1:]:
            nbytes *= d
        nbytes = (nbytes + 63) // 64 * 64
        off = arena["p"]
        arena["p"] = off + nbytes
        arena["hi"] = max(arena["hi"], arena["p"])
        assert arena["p"] <= ARENA_TOP, ("SBUF overflow", name, arena["p"])
        key = (name, tuple(shape), str(dt), off)
        if key not in arena_cache:
            arena["n"] += 1
            arena_cache[key] = nc.alloc_sbuf_tensor_at("%s_%d" % (name, arena["n"]), list(shape), dt, offset=off)
        return arena_cache[key]

    class Phase:
        def __enter__(self):
            self.mark = arena["p"]
            return self

        def __exit__(self, *a):
            P.barrier()
            arena["p"] = self.mark
            return False

    with es:

        sems = {n: es.enter_context(nc.semaphore(n)) for n in P.semnames}
        psb = [es.enter_context(nc.psum_tensor("ps%d" % i, [128, 512], F32)) for i in range(8)]
        psbuf = [Buf("ps%d" % i) for i in range(8)]
        pscnt = [0]

        pscnt2 = [0]

        regbuf = {}
        rings = {}
        ringcnt = {}

        def set_rings(layout):
            rings.clear()
            for cls, regs in layout.items():
                rings[cls] = regs
                ringcnt.setdefault(cls, 0)

        def full(banks):
            return [(k, 0, 512) for k in banks]

        LAY_DEFAULT = {"conv": full([0, 1]), "core": full([2, 3, 4, 5, 6, 7])}
        LAY_SSD = {"conv": full([0, 1]), "core": full([2, 3, 4]), "s": full([5]), "s2": full([6]),
                   "q": full([7]), "t": full([7])}
        LAY_ML = {"conv": full([0, 1]), "core": full([2, 3, 4]), "c": full([5]), "n": full([6]),
                  "m64": full([7]), "q": full([7]), "t": full([7])}
        set_rings(LAY_DEFAULT)

        def PS(cls="core"):
            regs = rings[cls]
            r = regs[ringcnt[cls] % len(regs)]
            ringcnt[cls] += 1
            if r not in regbuf:
                regbuf[r] = Buf("ps%d_%d" % (r[0], r[1]))
            k, c0, w = r
            return psb[k][:, c0:c0 + w], regbuf[r]

        def fsz(ap):
            n = 1
            for d in ap.shape[1:]:
                n *= d
            return n

        def mm(out, lhsT, rhs, start=True, stop=True, reads=(), writes=(), inc=True):
            c = 0.03 + max(fsz(rhs), 64) / 2400.0 * (4.0 if rhs.dtype == F32 else 1.0)
            P.op("pe", lambda e: e.matmul(out, lhsT=lhsT, rhs=rhs, start=start, stop=stop), reads, writes, inc, cost=c)

        def tr(out, in_, ident, reads=(), writes=(), inc=True):
            P.op("pe", lambda e: e.transpose(out, in_, ident), reads, writes, inc, cost=0.1)

        def act(out, in_, func, reads=(), writes=(), bias=None, scale=None, accum=None):
            kw = {}
            if bias is not None:
                kw["bias"] = bias
            if scale is not None:
                kw["scale"] = scale
            if accum is not None:
                kw["accum_out"] = accum
            c = 0.2 + max(fsz(out), 64) / 1400.0 + (0.1 if accum is not None else 0.0)
            P.op("act", lambda e: e.activation(out, in_, func, **kw), reads, writes, cost=c)

        def tt(out, in0, in1, op, reads=(), writes=(), eng="dve"):
            c = (0.08 if eng == "dve" else 0.35) + max(fsz(out), 64) / 960.0
            if op == ALU.pow:
                c = 1.0
            P.op(eng, lambda e: e.tensor_tensor(out, in0, in1, op), reads, writes, cost=c)

        def ts(out, in0, s1, s2, op0, op1=None, reads=(), writes=(), eng="dve"):
            c = (0.08 if eng == "dve" else 0.35) + max(fsz(out), 64) / 960.0
            if op1 is None:
                P.op(eng, lambda e: e.tensor_scalar(out, in0, s1, None, op0), reads, writes, cost=c)
            else:
                P.op(eng, lambda e: e.tensor_scalar(out, in0, s1, s2, op0, op1), reads, writes, cost=c)

        def stt(out, in0, scalar, in1, op0, op1, reads=(), writes=()):
            c = 0.08 + max(fsz(out), 64) / 960.0
            P.op("dve", lambda e: e.scalar_tensor_tensor(out, in0, scalar, in1, op0, op1), reads, writes, cost=c)

        def cp(out, in_, reads=(), writes=(), eng="dve"):
            if eng == "act":
                act(out, in_, AF.Copy, reads, writes)
            else:
                c = (0.08 if eng == "dve" else 0.35) + max(fsz(out), 64) / 960.0
                P.op(eng, lambda e: e.tensor_copy(out, in_), reads, writes, cost=c)

        def memset(ap, val, writes=(), eng="pool"):
            P.op(eng, lambda e: e.memset(ap, val), (), writes, cost=0.35 + fsz(ap) / 960.0)

        def bc_row(handle, off, n):
            return bass.AP(handle, off, [[0, 128], [1, n]])

        def dump(name, ap, bufs, shape, dt=F32):
            if name not in dumps:
                return
            t = nc.dram_tensor("dbg_" + name, list(shape), dt, kind="ExternalOutput")
            dump_t[name] = t
            tok = P.dma("sp", lambda e: e.dma_start(out=t.ap(), in_=ap), reads=bufs)
            dumptoks.append(tok)

        dumptoks = []
        outtoks = []

        cb = Buf("const")
        identf = sb("identf", [128, 128]); identb = sb("identb", [128, 128], BF16)
        Lmask = sb("Lmask", [128, 128]); Umask = sb("Umask", [128, 128]); Bd = sb("Bd", [128, 128])
        sel0 = sb("sel0", [128, 128]); sel1 = sb("sel1", [128, 128])
        tri = sb("tri", [128, 64]); Idm = sb("Idm", [128, 64])
        mhalf = sb("mhalf", [128, 1])
        memset(identf[:], 0.0, [cb])
        P.op("pool", lambda e: e.affine_select(identf[:], identf[:], pattern=[[-1, 128]], compare_op=ALU.not_equal,
                                               fill=1.0, base=0, channel_multiplier=1), [cb], [cb])
        cp(identb[:], identf[:], [cb], [cb], eng="pool")
        memset(Lmask[:], 1.0, [cb])
        P.op("pool", lambda e: e.affine_select(Lmask[:], Lmask[:], pattern=[[1, 128]], compare_op=ALU.is_ge,
                                               fill=0.0, base=0, channel_multiplier=-1), [cb], [cb])
        memset(Lmask[0:64, 64:128], 0.0, [cb])
        memset(Umask[:], 1.0, [cb])
        P.op("pool", lambda e: e.affine_select(Umask[:], Umask[:], pattern=[[-1, 128]], compare_op=ALU.is_gt,
                                               fill=0.0, base=0, channel_multiplier=1), [cb], [cb])
        memset(Umask[64:128, 0:64], 0.0, [cb])
        memset(Bd[:], 0.0, [cb]); memset(Bd[0:64, 0:64], 1.0, [cb]); memset(Bd[64:128, 64:128], 1.0, [cb])
        memset(sel0[:], 0.0, [cb]); memset(sel0[0:64, :], 1.0, [cb])
        memset(sel1[:], 0.0, [cb]); memset(sel1[64:128, :], 1.0, [cb])
        cp(tri[0:64, :], Lmask[0:64, 0:64], [cb], [cb], eng="pool")
        cp(tri[64:128, :], Lmask[64:128, 64:128], [cb], [cb], eng="pool")
        cp(Idm[0:64, :], identf[0:64, 0:64], [cb], [cb], eng="pool")
        cp(Idm[64:128, :], identf[64:128, 64:128], [cb], [cb], eng="pool")
        memset(mhalf[:], -0.5, [cb])

        cw = sb("cw", [128, 40, 5])
        cwb = Buf("cw")
        with Phase():
            cwT = sb("cwT", [5, 5120])
            P.dma("sp", lambda e: e.dma_start(out=cwT[0:4, 0:3072], in_=conv_ssd_w.ap()), writes=[cwb])
            P.dma("sp", lambda e: e.dma_start(out=cwT[4:5, 0:3072], in_=conv_ssd_b.ap()), writes=[cwb])
            P.dma("sp", lambda e: e.dma_start(out=cwT[0:4, 3072:5120], in_=conv_qk_w.ap()), writes=[cwb])
            P.dma("sp", lambda e: e.dma_start(out=cwT[4:5, 3072:5120], in_=conv_qk_b.ap()), writes=[cwb])
            ps, pb = PS()
            for g in range(40):
                mm(ps[:, g * 5:(g + 1) * 5], cwT[0:5, g * 128:(g + 1) * 128], identf[0:5, 0:5],
                   reads=[cwb, cb], writes=[pb], inc=(g == 39))
            cp(cw[:].rearrange("p g f -> p (g f)"), ps[:, 0:200], [pb], [cwb])

        dtb_bc = sb("dtb_bc", [128, 32]); A_bc = sb("A_bc", [128, 32]); dsk_bc = sb("dsk_bc", [128, 32])
        fb4 = sb("fb4", [4, 1]); ib4 = sb("ib4", [4, 1]); negfb = sb("negfb", [4, 1])
        P.dma("sp", lambda e: e.dma_start(out=dtb_bc[:], in_=bc_row(vecs["dt_bias"], 0, 32)), writes=[cb])
        P.dma("sp", lambda e: e.dma_start(out=A_bc[:], in_=bc_row(vecs["a_log"], 0, 32)), writes=[cb])
        P.dma("sp", lambda e: e.dma_start(out=dsk_bc[:], in_=bc_row(vecs["d_skip"], 0, 32)), writes=[cb])
        P.dma("sp", lambda e: e.dma_start(out=fb4[:], in_=bass.AP(vecs["f_bias"], 0, [[1, 4], [1, 1]])), writes=[cb])
        P.dma("sp", lambda e: e.dma_start(out=ib4[:], in_=bass.AP(vecs["i_bias"], 0, [[1, 4], [1, 1]])), writes=[cb])
        act(A_bc[:], A_bc[:], AF.Exp, [cb], [cb])
        ts(A_bc[:], A_bc[:], -1.0, None, ALU.mult, reads=[cb], writes=[cb])
        ts(negfb[:], fb4[:], -1.0, None, ALU.mult, reads=[cb], writes=[cb])
        dI = sb("dI", [128, 32, 64], BF16)
        tt(dI[:], dsk_bc[:, :].unsqueeze(2).broadcast_to([128, 32, 64]),
           Idm[:, :].unsqueeze(1).broadcast_to([128, 32, 64]), ALU.mult, [cb], [cb])
        I4x = sb("I4x", [4, 4, 32])
        tt(I4x[:], identf[0:4, 0:4].unsqueeze(2).broadcast_to([4, 4, 32]),
           identf[0:4, 0:4].unsqueeze(2).broadcast_to([4, 4, 32]), ALU.mult, [cb], [cb])
        ones4 = sb("ones4", [4, 128])
        memset(ones4[:], 1.0, [cb])
        stb = Buf("state")
        hst = sb("hst", [128, 4, 512]); hbf = sb("hbf", [128, 4, 512], BF16)
        Cst = sb("Cst", [128, 4, 2, 258]); Cbf = sb("Cbf", [128, 4, 2, 258], BF16)
        rawc = sb("rawc", [128, 40, 3])
        mcar = sb("mcar", [4, 1])
        memset(hst[:], 0.0, [stb]); memset(hbf[:], 0.0, [stb])
        memset(Cst[:], 0.0, [stb]); memset(Cbf[:], 0.0, [stb])
        memset(rawc[:], 0.0, [stb]); memset(mcar[:], 0.0, [stb])
        hgb = [Buf("h%d" % g) for g in range(4)]
        hbfb = [Buf("hbf%d" % g) for g in range(4)]
        Chb = [Buf("C%d" % h) for h in range(4)]
        Cbfb = [Buf("Cbf%d" % h) for h in range(4)]
        rawcb = [Buf("rawc%d" % g) for g in range(40)]
        for b in hgb + hbfb + Chb + Cbfb + rawcb:
            b.w = stb.w
        mcarb = Buf("mcar"); mcarb.w = stb.w

        NST = 6
        sst = sb("sst", [128, NST, 4]); sstb = [Buf("sst%d" % i) for i in range(NST)]
        stc = [0]

        def STAT():
            k = stc[0] % NST
            stc[0] += 1
            return sst[:, k, :], sstb[k]

        junkt = [sb("junk%d" % i, [128, 1024], BF16) for i in range(2)]; junkbs = [Buf("junk%d" % i) for i in range(2)]
        jc = [0]

        def JUNK():
            k = jc[0] % 2
            jc[0] += 1
            return junkt[k], junkbs[k]

        def rstd_from_ss(st, stbuf, n, eps):
            ts(st[:, 1:2], st[:, 0:1], 1.0 / n, eps, ALU.mult, ALU.add, [stbuf], [stbuf], eng="pool")
            tt(st[:, 1:2], st[:, 1:2], mhalf[:, 0:1], ALU.pow, [stbuf, cb], [stbuf], eng="pool")

        NSLAB = 2
        slab_t = [sb("slab%d" % i, [128, 8192], BF16) for i in range(NSLAB)]
        slab_b = [Buf("slab%d" % i) for i in range(NSLAB)]

        def piece(handle, r0, k, c0, n):
            return (handle, r0, k, c0, n)

        sched = []
        for blk in range(NB):
            def s_xbc(g):
                return [piece(w_in, 0, 8, OFF_X + g * 512, 512), piece(w_in, 0, 8, OFF_B + g * 128, 128),
                        piece(w_in, 0, 8, OFF_C + g * 128, 128)]

            def s_z(g):
                return [piece(w_in, 0, 8, OFF_Z + g * 512, 512)]
            sched.extend([s_xbc(0), s_xbc(1), s_z(0), s_xbc(2), s_z(1), s_xbc(3), s_z(2), s_z(3)])
            sched.append([piece(w_in, 0, 8, OFF_G, 1024)])
            sched.append([piece(w_brs, 0, 16, 0, 512)])
            sched.append([piece(w_brs, 0, 16, 512, 512)])
            for h in range(4):
                sched.append([piece(w_in, 0, 8, OFF_Q + h * 256, 256), piece(w_in, 0, 8, OFF_K + h * 256, 256),
                              piece(w_in, 0, 8, OFF_V + h * 256, 256), piece(w_in, 0, 8, OFF_O + h * 256, 256)])
            sched.append([piece(w_in, 0, 8, OFF_G + 1024, 1024)])
            sched.append([piece(w_brm, 0, 8, 0, 1024)])
            sched.append([piece(w_out, 0, 8, 0, 1024)])
            for qf in range(4):
                sched.append([piece(w_up, 0, 8, qf * 1024, 1024)])
                sched.append([piece(w_dn, qf * 1024, 8, 0, 1024)])
        slab_issued = [0]
        slab_cur = [0]

        def issue_slab(i):
            t = slab_t[i % NSLAB]; b = slab_b[i % NSLAB]
            off = 0
            for (handle, r0, k, c0, n) in sched[i]:
                dst = t[:, off:off + k * n].rearrange("p (k n) -> p k n", k=k)
                src = handle.ap()[r0:r0 + k * 128, c0:c0 + n].rearrange("(k p) n -> p k n", p=128)
                P.dma("pool", lambda e, dst=dst, src=src: e.dma_start(out=dst, in_=src), writes=[b], cost=2.5 + k * n * 512.0 / 300e3)
                off += k * n

        def next_slab():
            i = slab_cur[0]
            slab_cur[0] += 1
            while slab_issued[0] < min(len(sched), i + NSLAB):
                issue_slab(slab_issued[0])
                slab_issued[0] += 1
            t = slab_t[i % NSLAB]; b = slab_b[i % NSLAB]
            views = []
            off = 0
            for (handle, r0, k, c0, n) in sched[i]:
                views.append(t[:, off:off + k * n].rearrange("p (k n) -> p k n", k=k))
                off += k * n
            return views, b

        wsm = sb("wsm", [128, 8, 40], BF16); wsmb = Buf("wsm")
        P.dma("pool", lambda e: e.dma_start(out=wsm[:, :, 0:32],
                                            in_=w_in.ap()[:, OFF_DT:OFF_DT + 32].rearrange("(k p) n -> p k n", p=128)),
              writes=[wsmb])
        P.dma("pool", lambda e: e.dma_start(out=wsm[:, :, 32:40],
                                            in_=w_in.ap()[:, OFF_I:OFF_I + 8].rearrange("(k p) n -> p k n", p=128)),
              writes=[wsmb])

        uT = sb("uT", [128, 8, TB], BF16); uTb = [Buf("uT%d" % i) for i in range(NT)]
        big = sb("big", [128, 16, TB], BF16)
        bigb = [Buf("big%d" % i) for i in range(NT)]
        mixb = [Buf("mix%d" % i) for i in range(NT)]
        h1b = [Buf("h1_%d" % i) for i in range(NT)]
        xinb = [Buf("xin%d" % i) for i in range(4)]
        ubfb = [Buf("ubf%d" % i) for i in range(4)]
        nwb = Buf("nw")
        rawb = [Buf("raw%d" % i) for i in range(2)]
        accb = [Buf("acc%d" % i) for i in range(2)]
        fmTb = [Buf("fmT%d" % i) for i in range(2)]
        M = {}

        def alloc_norm_bufs(nb=2):
            M["nb"] = nb
            M["xin"] = [sb("xin%d" % i, [128, 1024]) for i in range(nb)]
            M["ubf"] = [sb("ubf%d" % i, [128, 1024], BF16) for i in range(nb)]
            M["nw"] = sb("nw", [128, 1024])

        def alloc_conv_bufs():
            M["raw"] = [sb("raw%d" % i, [128, 4 + TB], BF16) for i in range(2)]
            M["dg"] = [sb("dg%d" % i, [128, 4, 128], BF16) for i in range(2)]
            M["fmT"] = [sb("fmT%d" % i, [128, TB], BF16) for i in range(2)]
        cvc = [0]

        def load_nw(name, off, n):
            nw = M["nw"]
            P.dma("sp", lambda e: e.dma_start(out=nw[:, 0:n], in_=bc_row(vecs[name], off, n)), writes=[nwb])

        def norm_to_uT(src_fn, srcb_fn, t0):
            for i in range(NT):
                src = src_fn(i); sbuf_ = srcb_fn(i)
                st, stbuf = STAT()
                jk, jkb = JUNK()
                act(jk[:], src, AF.Square, [sbuf_], [stbuf, jkb], accum=st[:, 0:1])
                rstd_from_ss(st, stbuf, 1024.0, EPS)
                u = M["ubf"][i % M["nb"]]; ub_ = ubfb[i % M["nb"]]
                stt(u[:], src, st[:, 1:2], M["nw"][:], ALU.mult, ALU.mult, [sbuf_, stbuf, nwb], [ub_])
                ps, pb = PS()
                pbf = ps.bitcast(BF16)
                for k in range(8):
                    tr(pbf[:, k * 128:(k + 1) * 128], u[:, k * 128:(k + 1) * 128], identb[:], [ub_, cb], [pb], inc=(k == 7))
                cp(uT[:, :, i * 128:(i + 1) * 128], pbf[:, 0:1024].rearrange("p (k t) -> p k t", k=8), [pb], [uTb[i]], eng="act")

        def fm_conv(wp, wbuf, col0, cwg, dst, dstb):
            ri = cvc[0] % 2
            cvc[0] += 1
            raw = M["raw"][ri]; rb = rawb[ri]; dg = M["dg"][ri]; db = accb[ri]
            tt(dg[:], identf[:, :].unsqueeze(1).broadcast_to([128, 4, 128]),
               cw[:, cwg, 0:4].unsqueeze(2).broadcast_to([128, 4, 128]), ALU.mult, [cb, cwb], [db])
            cp(raw[:, 0:3], rawc[:, cwg, :], [rawcb[cwg]], [rb])
            for tb in range(TB // 512):
                ps, pb = PS("conv")
                for k in range(8):
                    mm(ps[:], wp[:, k, col0:col0 + 128], uT[:, k, tb * 512:(tb + 1) * 512], k == 0, k == 7,
                       [wbuf] + uTb[tb * 4:(tb + 1) * 4], [pb], inc=(k in (3, 7)))
                cp(raw[:, 3 + tb * 512:3 + (tb + 1) * 512], ps[:], [pb], [rb], eng="act")
            cp(rawc[:, cwg, :], raw[:, TB:TB + 3], [rb], [rawcb[cwg]])
            for tb in range(TB // 512):
                ps, pb = PS("conv")
                for tap in range(4):
                    mm(ps[:], dg[:, tap, :], raw[:, tap + tb * 512:tap + tb * 512 + 512], tap == 0, tap == 3, [db, rb], [pb], inc=(tap in (1, 3)))
                act(dst[:, tb * 512:(tb + 1) * 512], ps[:], AF.Silu, [pb, cwb], [dstb], bias=cw[:, cwg, 4:5])

        def transpose_blocks(src, srcb, n128, dst3, dstbs, eng="act"):
            ps, pb = PS("conv")
            pbf = ps.bitcast(BF16)
            for j in range(NT):
                tr(pbf[:, j * 128:(j + 1) * 128], src[:, j * 128:(j + 1) * 128], identb[:], [srcb, cb], [pb], inc=(j == NT - 1))
            cp(dst3, pbf[:, 0:NT * 128].rearrange("p (a b) -> p a b", a=NT), [pb], dstbs, eng=eng)

        for blk in range(NB):
            t0 = blk * TB
            P.tag = "S0"
            ph0 = Phase(); ph0.__enter__()
            alloc_norm_bufs(4)
            load_nw("norm_mix_w", 0, 1024)

            def xsrc(i, t0=t0):
                xt = M["xin"][i % 4]
                src = x_d.ap()[t0 + i * 128:t0 + (i + 1) * 128, :]
                P.dma("sp", lambda e: e.dma_start(out=xt[:], in_=src), writes=[xinb[i % 4]])
                return xt[:]
            norm_to_uT(xsrc, lambda i: xinb[i % 4], t0)
            if blk == 0:
                dump("uT", uT[:], uTb, [128, 8, TB], BF16)
            ph0.__exit__()
            if stop_after == "S0":
                break

            P.tag = "ssd_pre"
            with Phase():
                psb_ = sb
                set_rings(LAY_SSD)
                alloc_conv_bufs()
                dt_tok = psb_("dt_tok", [128, NT, 32]); a_tok = psb_("a_tok", [128, NT, 32])
                sm = psb_("sm", [128, NT, 4, 32]); dte = psb_("dte", [128, NT, 32])
                smb = [Buf("sm%d" % i) for i in range(NT)]
                for i in range(NT):
                    ps, pb = PS()
                    for k in range(8):
                        mm(ps[:, 0:32], uT[:, k, i * 128:(i + 1) * 128], wsm[:, k, 0:32], k == 0, k == 7,
                           [uTb[i], wsmb], [pb], inc=(k == 7))
                    tt(dt_tok[:, i, :], ps[:, 0:32], dtb_bc[:], ALU.add, [pb, cb], [smb[i]])
                    act(dt_tok[:, i, :], dt_tok[:, i, :], AF.Exp, [smb[i]], [smb[i]])
                for i in range(NT):
                    act(dt_tok[:, i, :], dt_tok[:, i, :], AF.Ln, [smb[i]], [smb[i]], bias=1.0)
                    tt(a_tok[:, i, :], dt_tok[:, i, :], A_bc[:], ALU.mult, [smb[i], cb], [smb[i]])
                for i in range(NT):
                    ps, pb = PS()
                    for j, msk in enumerate([Lmask, Umask, sel0, sel1]):
                        mm(ps[:, j * 32:(j + 1) * 32], msk[:], a_tok[:, i, :], True, True, [smb[i], cb], [pb], inc=(j == 3))
                    act(sm[:, i, :, :].rearrange("p a b -> p (a b)"), ps[:, 0:128], AF.Exp, [pb], [smb[i]])
                    tt(dte[:, i, :], dt_tok[:, i, :], sm[:, i, 1, :], ALU.mult, [smb[i]], [smb[i]])
                if blk == 0:
                    dump("dt", dt_tok[:], smb, [128, NT, 32])

                x_tok2 = [psb_("x_tok", [128, NT, 512], BF16) for _ in range(2)]
                x_tokb2 = [[Buf("xtok%d" % i) for i in range(NT)] for _ in range(2)]
                BT2 = [psb_("BT", [128, TB], BF16) for _ in range(2)]; BTb2 = [Buf("BT") for _ in range(2)]
                CT2 = [psb_("CT", [128, TB], BF16) for _ in range(2)]; CTb2 = [Buf("CT") for _ in range(2)]
                B_tok2 = [psb_("B_tok", [128, NT, 128], BF16) for _ in range(2)]
                B_tokb2 = [[Buf("Btok%d" % i) for i in range(NT)] for _ in range(2)]
                rseg = [psb_("rseg%d" % i, [128, 8, 64]) for i in range(3)]
                esg = [psb_("esg%d" % i, [128, 8, 64]) for i in range(3)]
                cbm = [psb_("cbm%d" % i, [128, 64]) for i in range(3)]
                wpr = [psb_("wpr%d" % i, [128, 8, 64], BF16) for i in range(3)]
                xdt = [psb_("xdt%d" % i, [128, 512], BF16) for i in range(3)]
                xw = [psb_("xw%d" % i, [128, 512], BF16) for i in range(3)]
                indb = [Buf("ind%d" % i) for i in range(3)]
                yis2 = [psb_("yis", [128, 512]) for _ in range(2)]; th2 = [psb_("th", [128, 512]) for _ in range(2)]
                ysn2 = [psb_("ysn", [128, 512], BF16) for _ in range(2)]
                seqb2 = [Buf("seq%d" % i) for i in range(2)]
                htmp = psb_("htmp", [128, 512]); htb = Buf("htmp")
                nws = psb_("nws", [128, 512]); nwsb = Buf("nws")

                def ssd_pre(g):
                    par = g % 2
                    x_tok = x_tok2[par]; x_tokb = x_tokb2[par]; BT = BT2[par]; BTb = BTb2[par]
                    CT = CT2[par]; CTb = CTb2[par]; B_tok = B_tok2[par]; B_tokb = B_tokb2[par]
                    (wx, wB, wC), wb1 = next_slab()
                    P.tag = "ssd_conv"
                    for c4 in range(4):
                        f = M["fmT"][c4 % 2]; fb_ = fmTb[c4 % 2]
                        fm_conv(wx, wb1, c4 * 128, g * 4 + c4, f[:], fb_)
                        transpose_blocks(f, fb_, 1, x_tok[:, :, c4 * 128:(c4 + 1) * 128], x_tokb, eng="dve")
                    fm_conv(wB, wb1, 0, 16 + g, BT[:], BTb)
                    transpose_blocks(BT, BTb, 1, B_tok[:, :, :], B_tokb, eng="dve")
                    fm_conv(wC, wb1, 0, 20 + g, CT[:], CTb)

                def ssd_core(g):
                    par = g % 2
                    x_tok = x_tok2[par]; x_tokb = x_tokb2[par]; BT = BT2[par]; BTb = BTb2[par]
                    CT = CT2[par]; CTb = CTb2[par]; B_tok = B_tok2[par]; B_tokb = B_tokb2[par]
                    (wz,), wb2 = next_slab()
                    P.dma("sp", lambda e, g=g: e.dma_start(out=nws[:], in_=bc_row(vecs["ssd_norm_w"], g * 512, 512)), writes=[nwsb])
                    if blk == 0 and g == 0:
                        dump("x_tok0", x_tok[:], x_tokb, [128, NT, 512], BF16)
                        dump("BT0", BT[:], [BTb], [128, TB], BF16)

                    def indep(i, g=g):
                        P.tag = "ssd_indep"
                        q = i % 3
                        ib = indb[q]
                        tt(rseg[q][:], a_tok[:, i, g * 8:(g + 1) * 8].unsqueeze(2).broadcast_to([128, 8, 64]),
                           tri[:, :].unsqueeze(1).broadcast_to([128, 8, 64]), ALU.mult, [smb[i], cb], [ib])
                        ps, pb = PS("s")
                        mm(ps[:], Umask[:], rseg[q][:].rearrange("p a b -> p (a b)"), True, True, [ib, cb], [pb])
                        act(esg[q][:].rearrange("p a b -> p (a b)"), ps[:], AF.Exp, [pb], [ib])
                        ps2, pb2 = PS("q")
                        mm(ps2[:, 0:128], BT[:, i * 128:(i + 1) * 128], CT[:, i * 128:(i + 1) * 128], True, True,
                           [BTb, CTb], [pb2])
                        tt(cbm[q][0:64, :], ps2[0:64, 0:64], tri[0:64, :], ALU.mult, [pb2, cb], [ib])
                        tt(cbm[q][64:128, :], ps2[64:128, 64:128], tri[64:128, :], ALU.mult, [pb2, cb], [ib])
                        tt(wpr[q][:], esg[q][:], cbm[q][:, :].unsqueeze(1).broadcast_to([128, 8, 64]), ALU.mult, [ib], [ib])
                        tt(xdt[q][:].rearrange("p (a b) -> p a b", a=8), x_tok[:, i, :].rearrange("p (a b) -> p a b", a=8),
                           dt_tok[:, i, g * 8:(g + 1) * 8].unsqueeze(2).broadcast_to([128, 8, 64]), ALU.mult,
                           [x_tokb[i], smb[i]], [ib], eng="pool")
                        tt(xw[q][:].rearrange("p (a b) -> p a b", a=8), x_tok[:, i, :].rearrange("p (a b) -> p a b", a=8),
                           dte[:, i, g * 8:(g + 1) * 8].unsqueeze(2).broadcast_to([128, 8, 64]), ALU.mult,
                           [x_tokb[i], smb[i]], [ib], eng="pool")

                    def seq(i, g=g):
                        P.tag = "ssd_seq"
                        q = i % 3
                        ib = indb[q]
                        q3 = i % 2
                        yis = yis2[q3]; yv = yis2[q3]; th = th2[q3]; t1 = th2[q3]; ysn = ysn2[q3]; seqb = seqb2[q3]
                        psy, pby = PS()
                        psi, pbi = PS()
                        psz, pbz = PS()
                        for k in range(8):
                            mm(psz[:], uT[:, k, i * 128:(i + 1) * 128], wz[:, k, :], k == 0, k == 7, [uTb[i], wb2], [pbz], inc=(k in (3, 7)))
                        for r in range(8):
                            for j in range(2):
                                sl = slice(64 * j, 64 * j + 64)
                                mm(psy[sl, r * 64:(r + 1) * 64], wpr[q][sl, r, :], xdt[q][sl, r * 64:(r + 1) * 64], True, False,
                                   [ib], [pby], inc=False)
                                mm(psy[sl, r * 64:(r + 1) * 64], dI[sl, g * 8 + r, :], x_tok[sl, i, r * 64:(r + 1) * 64], False, True,
                                   [cb, x_tokb[i]], [pby], inc=(r == 7 and j == 1))
                        for j in range(2):
                            sl = slice(64 * j, 64 * j + 64)
                            mm(psi[sl, :], CT[:, i * 128 + 64 * j:i * 128 + 64 * j + 64], hbf[:, g, :], True, True, [CTb, hbfb[g]], [pbi])
                            pss, pbs = PS("s2")
                            mm(pss[:], B_tok[sl, i, :], xw[q][sl, :], True, True, [B_tokb[i], ib], [pbs])
                            tt(htmp[:].rearrange("p (a b) -> p a b", a=8), hst[:, g, :].rearrange("p (a b) -> p a b", a=8),
                               sm[:, i, 2 + j, g * 8:(g + 1) * 8].unsqueeze(2).broadcast_to([128, 8, 64]), ALU.mult,
                               [hgb[g], smb[i]], [htb])
                            tt(hst[:, g, :], htmp[:], pss[:], ALU.add, [htb, pbs], [hgb[g]])
                            cp(hbf[:, g, :], hst[:, g, :], [hgb[g]], [hbfb[g]], eng="act")
                        tt(yis[:].rearrange("p (a b) -> p a b", a=8), psi[:].rearrange("p (a b) -> p a b", a=8),
                           sm[:, i, 0, g * 8:(g + 1) * 8].unsqueeze(2).broadcast_to([128, 8, 64]), ALU.mult, [pbi, smb[i]], [seqb])
                        tt(yv[:], psy[:], yis[:], ALU.add, [pby, seqb], [seqb])
                        if blk == 0 and g == 0 and i == 0:
                            dump("y_pre00", yv[:], [seqb], [128, 512])
                        act(th[:], psz[:], AF.Tanh, [pbz], [seqb], scale=0.5)
                        stt(t1[:], th[:], 1.0, psz[:], ALU.add, ALU.mult, [seqb, pbz], [seqb])
                        tt(t1[:], t1[:], yv[:], ALU.mult, [seqb], [seqb])
                        st, stbuf = STAT()
                        jk, jkb = JUNK()
                        act(jk[:, 0:512], t1[:], AF.Square, [seqb], [stbuf, jkb], accum=st[:, 0:1])
                        rstd_from_ss(st, stbuf, 512.0, 4.0 * EPS)
                        stt(ysn[:], t1[:], st[:, 1:2], nws[:], ALU.mult, ALU.mult, [seqb, stbuf, nwsb], [seqb])
                        pst, pbt = PS("t")
                        pbf = pst.bitcast(BF16)
                        for c in range(4):
                            tr(pbf[:, c * 128:(c + 1) * 128], ysn[:, c * 128:(c + 1) * 128], identb[:], [seqb, cb], [pbt], inc=(c == 3))
                        cp(big[:, g * 4:(g + 1) * 4, i * 128:(i + 1) * 128], pbf[:, 0:512].rearrange("p (c t) -> p c t", c=4),
                           [pbt], [bigb[i]], eng="act")

                    for i in range(NT + 1):
                        if i < NT:
                            indep(i)
                        if i > 0:
                            seq(i - 1)

                ssd_pre(0)
                for g in range(4):
                    if g < 3:
                        ssd_pre(g + 1)
                    ssd_core(g)
                if blk == 0:
                    dump("yT", big[:], bigb, [128, 16, TB], BF16)
            if stop_after == "SSD":
                break

            set_rings(LAY_DEFAULT)
            phA = Phase(); phA.__enter__()
            mixed = sb("mixed", [128, NT, 1024], BF16)
            gtok = sb("gtok", [128, NT, 5, 4]); gtokb = Buf("gtok")
            aold_bc = sb("aold_bc", [128, 4, 16]); aoldb = Buf("aold")
            P.tag = "post_ssd"
            with Phase():
                P.tag = "ml_gates"
                rmask = sb("rmask", [4, TB]); negbig = sb("negbig", [4, TB])
                G_l1 = sb("G_l1", [4, TB]); G_cs = sb("G_cs", [4, TB]); G_e = sb("G_e", [4, TB])
                G_Ml = sb("G_Ml", [4, TB]); G_M = sb("G_M", [4, TB]); G_t = sb("G_t", [4, TB])
                gs = sb("gs", [4, 8, 16])
                gb = Buf("gates")
                memset(rmask[:], 1.0, [gb])
                memset(rmask[:].rearrange("p (c t) -> p c t", t=64)[:, :, 0:1], 0.0, [gb])
                memset(negbig[:], 0.0, [gb])
                memset(negbig[:].rearrange("p (c t) -> p c t", t=64)[:, :, 0:1], -1e30, [gb])
                for tb in range(TB // 512):
                    ps, pb = PS(); ps2, pb2 = PS()
                    for k in range(8):
                        mm(ps[0:4, :], wsm[:, k, 32:36], uT[:, k, tb * 512:(tb + 1) * 512], k == 0, k == 7,
                           [wsmb] + uTb[tb * 4:(tb + 1) * 4], [pb], inc=(k == 7))
                    for k in range(8):
                        mm(ps2[0:4, :], wsm[:, k, 36:40], uT[:, k, tb * 512:(tb + 1) * 512], k == 0, k == 7,
                           [wsmb] + uTb[tb * 4:(tb + 1) * 4], [pb2], inc=(k == 7))
                    act(G_e[:, tb * 512:(tb + 1) * 512], ps[0:4, :], AF.Identity, [pb, cb], [gb], bias=ib4[:, 0:1])
                    act(G_l1[:, tb * 512:(tb + 1) * 512], ps2[0:4, :], AF.Exp, [pb2, cb], [gb], bias=negfb[:, 0:1], scale=-1.0)
                act(G_l1[:], G_l1[:], AF.Ln, [gb], [gb], bias=1.0)
                P.op("dve", lambda e: e.tensor_tensor_scan(G_cs[:], rmask[:], G_l1[:], 0.0, ALU.mult, ALU.add), [gb], [gb], cost=2.3)
                tt(G_e[:], G_e[:], G_cs[:], ALU.add, [gb], [gb])
                P.op("dve", lambda e: e.tensor_tensor_scan(G_Ml[:], negbig[:], G_e[:], 0.0, ALU.add, ALU.max), [gb], [gb], cost=2.3)

                def v3(t):
                    return t[:].rearrange("p (c t) -> p c t", t=64)
                csend = v3(G_cs)[:, :, 63]; emax = v3(G_Ml)[:, :, 63]
                mloc = gs[:, 0, :]; bend = gs[:, 1, :]; maft = gs[:, 2, :]; mprev = gs[:, 3, :]; dd = gs[:, 4, :]; ao = gs[:, 5, :]
                tt(mloc, emax, csend, ALU.subtract, [gb], [gb])
                ts(bend, csend, -1.0, None, ALU.mult, reads=[gb], writes=[gb])
                P.op("dve", lambda e: e.tensor_tensor_scan(maft, bend, mloc, mcar[:, 0:1], ALU.add, ALU.max), [gb, mcarb], [gb])
                cp(mprev[:, 0:1], mcar[:, 0:1], [mcarb, gb], [gb])
                cp(mprev[:, 1:16], maft[:, 0:15], [gb], [gb])
                cp(mcar[:, 0:1], maft[:, 15:16], [gb], [mcarb])
                tt(v3(G_M), v3(G_Ml), mprev.unsqueeze(2).broadcast_to([4, 16, 64]), ALU.max, [gb], [gb])
                psT, pbT = PS()

                def to_tok(src, qi):
                    for i in range(NT):
                        c0 = (i * 5 + qi) * 4
                        mm(psT[:, c0:c0 + 4], src[0:4, i * 128:(i + 1) * 128], identf[0:4, 0:4], True, True, [gb, cb], [pbT],
                           inc=(i == NT - 1))
                ts(G_t[:], G_e[:], LNSCALE, None, ALU.add, reads=[gb], writes=[gb])
                to_tok(G_t, 0)
                to_tok(G_M, 1)
                tt(v3(G_t), mprev.unsqueeze(2).broadcast_to([4, 16, 64]), v3(G_M), ALU.subtract, [gb, pbT], [gb])
                act(G_t[:], G_t[:], AF.Exp, [gb], [gb], bias=LNSCALE)
                to_tok(G_t, 2)
                tt(G_t[:], G_cs[:], G_M[:], ALU.subtract, [gb, pbT], [gb])
                act(G_t[:], G_t[:], AF.Exp, [gb], [gb])
                to_tok(G_t, 3)
                tt(dd, bend, maft, ALU.subtract, [gb], [gb])
                tt(v3(G_t), v3(G_e), dd.unsqueeze(2).broadcast_to([4, 16, 64]), ALU.add, [gb, pbT], [gb])
                act(G_t[:], G_t[:], AF.Exp, [gb], [gb])
                to_tok(G_t, 4)
                cp(gtok[:].rearrange("p a b c -> p (a b c)"), psT[:, 0:NT * 20], [pbT], [gtokb])
                tt(ao, dd, mprev, ALU.add, [gb], [gb])
                act(ao, ao, AF.Exp, [gb], [gb])
                Rx = sb("Rx", [4, 4, 16])
                tt(Rx[:], ao.unsqueeze(1).broadcast_to([4, 4, 16]), I4x[:, :, 0:16], ALU.mult, [gb, cb], [gb])
                psA, pbA = PS()
                mm(psA[:, 0:64], ones4[:], Rx[:].rearrange("p a b -> p (a b)"), True, True, [gb, cb], [pbA])
                cp(aold_bc[:].rearrange("p a b -> p (a b)"), psA[:, 0:64], [pbA], [aoldb])


                P.tag = "post_ssd"
                sgt = sb("sgt", [128, NT, 1024], BF16); sgtb = [Buf("sgt%d" % i) for i in range(NT)]
                (wg,), wbg = next_slab()
                for i in range(NT):
                    for cbk in range(2):
                        psg, pbg = PS()
                        for k in range(8):
                            mm(psg[:], uT[:, k, i * 128:(i + 1) * 128], wg[:, k, cbk * 512:(cbk + 1) * 512], k == 0, k == 7,
                               [uTb[i], wbg], [pbg], inc=(k == 7))
                        act(sgt[:, i, cbk * 512:(cbk + 1) * 512], psg[:], AF.Sigmoid, [pbg], [sgtb[i]])
                for cbk in range(2):
                    (wbr,), wbb = next_slab()
                    for i in range(NT):
                        psr, pbr = PS()
                        for k in range(16):
                            mm(psr[:], big[:, k, i * 128:(i + 1) * 128], wbr[:, k, :], k == 0, k == 15, [bigb[i], wbb], [pbr], inc=(k == 15))
                        tt(mixed[:, i, cbk * 512:(cbk + 1) * 512], sgt[:, i, cbk * 512:(cbk + 1) * 512], psr[:], ALU.mult,
                           [sgtb[i], pbr], [mixb[i]])

            P.tag = "ml_gates"
            with Phase():
                with Phase():
                    set_rings(LAY_ML)
                    alloc_conv_bufs()
                    qT2 = [sb("qT", [128, 2, TB], BF16) for _ in range(2)]; qTb2 = [Buf("qT") for _ in range(2)]
                    kT2 = [sb("kT", [128, 2, TB], BF16) for _ in range(2)]; kTb2 = [Buf("kT") for _ in range(2)]
                    k_tok2 = [sb("k_tok", [128, NT, 256], BF16) for _ in range(2)]
                    k_tokb2 = [[Buf("ktok%d" % i) for i in range(NT)] for _ in range(2)]
                    vext2 = [sb("vext", [128, NT, 258], BF16) for _ in range(2)]
                    vextb2 = [[Buf("vext%d" % i) for i in range(NT)] for _ in range(2)]
                    osig2 = [sb("osig", [128, NT, 256], BF16) for _ in range(2)]
                    osigb2 = [[Buf("osig%d" % i) for i in range(NT)] for _ in range(2)]
                    nwm2 = [sb("nwm", [128, 256]) for _ in range(2)]; nwmb2 = [Buf("nwm") for _ in range(2)]
                    Md = [sb("Md%d" % i, [128, 64]) for i in range(2)]
                    Dm = [sb("Dm%d" % i, [128, 64]) for i in range(2)]
                    Sx = [sb("Sx%d" % i, [128, 64], BF16) for i in range(2)]
                    vw = [sb("vw%d" % i, [128, 258], BF16) for i in range(2)]
                    mib = [Buf("mind%d" % i) for i in range(2)]
                    num2 = [sb("num", [128, 258]) for _ in range(2)]
                    hb2 = [sb("hb", [128, 256], BF16) for _ in range(2)]
                    msb2 = [Buf("mseq%d" % i) for i in range(2)]
                    for par_ in range(2):
                        vinit = Buf("vinit")
                        memset(vext2[par_][:, :, 256:257], 1.0, [vinit]); memset(vext2[par_][:, :, 257:258], 0.0, [vinit])
                        for b_ in vextb2[par_]:
                            b_.w = vinit.w

                    def sel(h):
                        p_ = h % 2
                        return (qT2[p_], qTb2[p_], kT2[p_], kTb2[p_], k_tok2[p_], k_tokb2[p_], vext2[p_], vextb2[p_],
                                osig2[p_], osigb2[p_], nwm2[p_], nwmb2[p_])

                    def ml_pre(h):
                        qT, qTb, kT, kTb, k_tok, k_tokb, vext, vextb, osig, osigb, nwm, nwmb = sel(h)
                        (wq, wk, wv, wo), wbh = next_slab()
                        wvo = slab_t[(slab_cur[0] - 1) % NSLAB][:, 4096:8192].rearrange("p (s k n) -> p s k n", s=2, k=8)
                        P.tag = "ml_conv"
                        P.dma("sp", lambda e, h=h: e.dma_start(out=nwm[:], in_=bc_row(vecs["mlstm_norm_w"], h * 256, 256)), writes=[nwmb])
                        for half in range(2):
                            fm_conv(wq, wbh, half * 128, 24 + h * 2 + half, qT[:, half, :], qTb)
                        for half in range(2):
                            fm_conv(wk, wbh, half * 128, 32 + h * 2 + half, kT[:, half, :], kTb)
                        for half in range(2):
                            transpose_blocks(kT[:, half, :], kTb, 1, k_tok[:, :, half * 128:(half + 1) * 128], k_tokb)
                        for i in range(NT):
                            ps, pb = PS("conv")
                            for k in range(8):
                                mm(ps[:, 0:512].rearrange("p (s n) -> p s n", s=2), uT[:, k, i * 128:(i + 1) * 128], wvo[:, :, k, :], k == 0, k == 7,
                                   [uTb[i], wbh], [pb], inc=(k in (3, 7)))
                            cp(vext[:, i, 0:256], ps[:, 0:256], [pb], [vextb[i]], eng="act")
                            act(osig[:, i, :], ps[:, 256:512], AF.Sigmoid, [pb], [osigb[i]])

                    def ml_core(h):
                        qT, qTb, kT, kTb, k_tok, k_tokb, vext, vextb, osig, osigb, nwm, nwmb = sel(h)

                        def mindep(i, h=h):
                            P.tag = "ml_indep"
                            q = i % 2
                            ib = mib[q]
                            ts(Md[q][:], Idm[:], gtok[:, i, 1, h:h + 1], None, ALU.mult, reads=[gtokb, cb], writes=[ib])
                            psM, pbM = PS("m64")
                            mm(psM[:, 0:64], Bd[:], Md[q][:], True, True, [ib, cb], [pbM])
                            act(Dm[q][:], psM[:, 0:64], AF.Exp, [pbM, gtokb], [ib], bias=gtok[:, i, 0, h:h + 1], scale=-1.0)
                            tt(Dm[q][:], Dm[q][:], tri[:], ALU.mult, [ib, cb], [ib])
                            psq, pbq = PS("q")
                            for half in range(2):
                                mm(psq[:, 0:128], kT[:, half, i * 128:(i + 1) * 128], qT[:, half, i * 128:(i + 1) * 128], half == 0, half == 1,
                                   [kTb, qTb], [pbq], inc=(half == 1))
                            tt(Sx[q][0:64, :], psq[0:64, 0:64], Dm[q][0:64, :], ALU.mult, [pbq, ib], [ib])
                            tt(Sx[q][64:128, :], psq[64:128, 64:128], Dm[q][64:128, :], ALU.mult, [pbq, ib], [ib])
                            ts(vw[q][:], vext[:, i, :], gtok[:, i, 4, h:h + 1], None, ALU.mult, reads=[vextb[i], gtokb], writes=[ib])

                        def mseq(i, h=h):
                            P.tag = "ml_seq"
                            q = i % 2
                            ib = mib[q]
                            tmpi = num2[q]; num = num2[q]; hn = num2[q]; hb = hb2[q]; msb = msb2[q]
                            psn, pbn = PS(); psi, pbi = PS()
                            for j in range(2):
                                sl = slice(64 * j, 64 * j + 64)
                                c = 2 * i + j
                                mm(psn[sl, 0:258], Sx[q][sl, :], vext[sl, i, :], True, True, [ib, vextb[i]], [pbn])
                                for half in range(2):
                                    mm(psi[sl, 0:258], qT[:, half, i * 128 + 64 * j:i * 128 + 64 * j + 64], Cbf[:, h, half, :], half == 0, half == 1,
                                       [qTb, Cbfb[h]], [pbi], inc=(half == 1))
                                psc, pbc = PS("c")
                                pcn, pbcn = PS("n")
                                for half in range(2):
                                    mm(psc[:, half * 256:(half + 1) * 256], k_tok[sl, i, half * 128:(half + 1) * 128], vw[q][sl, 0:256], True, True,
                                       [k_tokb[i], ib], [pbc], inc=(half == 1))
                                for half in range(2):
                                    mm(pcn[:, half * 2:half * 2 + 2], k_tok[sl, i, half * 128:(half + 1) * 128], vw[q][sl, 256:258], True, True,
                                       [k_tokb[i], ib], [pbcn], inc=(half == 1))
                                stt(Cst[:, h, :, 0:256], Cst[:, h, :, 0:256], aold_bc[:, h, c:c + 1], psc[:, :].rearrange("p (a b) -> p a b", a=2),
                                    ALU.mult, ALU.add, [Chb[h], aoldb, pbc], [Chb[h]])
                                stt(Cst[:, h, :, 256:258], Cst[:, h, :, 256:258], aold_bc[:, h, c:c + 1],
                                    pcn[:, 0:4].rearrange("p (a b) -> p a b", a=2), ALU.mult, ALU.add, [Chb[h], aoldb, pbcn], [Chb[h]])
                                cp(Cbf[:, h, :, :], Cst[:, h, :, :], [Chb[h]], [Cbfb[h]], eng="act")
                            act(tmpi[:], psi[:, 0:258], AF.Identity, [pbi, gtokb], [msb], scale=gtok[:, i, 2, h:h + 1])
                            tt(num[:], tmpi[:], psn[:, 0:258], ALU.add, [msb, pbn], [msb])
                            st, stbuf = STAT()
                            ts(st[:, 2:3], num[:, 256:257], -1.0, None, ALU.mult, reads=[msb], writes=[stbuf])
                            tt(st[:, 2:3], st[:, 2:3], num[:, 256:257], ALU.max, [msb, stbuf], [stbuf])
                            tt(st[:, 2:3], st[:, 2:3], gtok[:, i, 3, h:h + 1], ALU.max, [stbuf, gtokb], [stbuf])
                            P.op("dve", lambda e: e.reciprocal(st[:, 3:4], st[:, 2:3]), [stbuf], [stbuf])
                            jk, jkb = JUNK()
                            act(jk[:, 0:256], num[:, 0:256], AF.Square, [msb, stbuf], [stbuf, jkb], scale=st[:, 3:4], accum=st[:, 0:1])
                            rstd_from_ss(st, stbuf, 256.0, EPS)
                            tt(st[:, 2:3], st[:, 1:2], st[:, 3:4], ALU.mult, [stbuf], [stbuf])
                            stt(hn[:, 0:256], num[:, 0:256], st[:, 2:3], nwm[:], ALU.mult, ALU.mult, [msb, stbuf, nwmb], [msb])
                            tt(hb[:], hn[:, 0:256], osig[:, i, :], ALU.mult, [msb, osigb[i]], [msb])
                            pst, pbt = PS("t")
                            pbf = pst.bitcast(BF16)
                            for c2 in range(2):
                                tr(pbf[:, c2 * 128:(c2 + 1) * 128], hb[:, c2 * 128:(c2 + 1) * 128], identb[:], [msb, cb], [pbt], inc=(c2 == 1))
                            cp(big[:, h * 2:(h + 1) * 2, i * 128:(i + 1) * 128], pbf[:, 0:256].rearrange("p (c t) -> p c t", c=2),
                               [pbt], [bigb[i]], eng="act")

                        for i in range(NT + 1):
                            if i < NT:
                                mindep(i)
                            if i > 0:
                                mseq(i - 1)

                    ml_pre(0)
                    for h in range(4):
                        if h < 3:
                            ml_pre(h + 1)
                        ml_core(h)
                if blk == 0:
                    dump("hT", big[:, 0:8, :], bigb, [128, 8, TB], BF16)
            if stop_after == "ML":
                break

            set_rings(LAY_DEFAULT)
            P.tag = "post_ml"
            with Phase():
                h1 = sb("h1", [128, NT, 1024])
                alloc_norm_bufs()
                sgt = sb("sgt", [128, NT, 1024], BF16); sgtb = [Buf("sgt%d" % i) for i in range(NT)]
                tmpm = [sb("tmpm%d" % i, [128, 512]) for i in range(2)]; sgb = [Buf("tmpm%d" % i) for i in range(2)]
                (wg2,), wbg = next_slab()
                for i in range(NT):
                    for cbk in range(2):
                        psg, pbg = PS()
                        for k in range(8):
                            mm(psg[:], uT[:, k, i * 128:(i + 1) * 128], wg2[:, k, cbk * 512:(cbk + 1) * 512], k == 0, k == 7,
                               [uTb[i], wbg], [pbg], inc=(k == 7))
                        act(sgt[:, i, cbk * 512:(cbk + 1) * 512], psg[:], AF.Sigmoid, [pbg], [sgtb[i]])
                (wbm,), wbb = next_slab()
                cnt = 0
                for i in range(NT):
                    for cbk in range(2):
                        psr, pbr = PS()
                        for k in range(8):
                            mm(psr[:], big[:, k, i * 128:(i + 1) * 128], wbm[:, k, cbk * 512:(cbk + 1) * 512], k == 0, k == 7,
                               [bigb[i], wbb], [pbr], inc=(k == 7))
                        q = cnt % 2; cnt += 1
                        tt(tmpm[q][:], sgt[:, i, cbk * 512:(cbk + 1) * 512], psr[:], ALU.mult, [sgtb[i], pbr], [sgb[q]])
                        tt(mixed[:, i, cbk * 512:(cbk + 1) * 512], mixed[:, i, cbk * 512:(cbk + 1) * 512], tmpm[q][:], ALU.add,
                           [sgb[q], mixb[i]], [mixb[i]])
                if blk == 0:
                    dump("mixed", mixed[:], mixb, [128, NT, 1024], BF16)
                for i in range(NT):
                    ps, pb = PS()
                    pbf = ps.bitcast(BF16)
                    for k in range(8):
                        tr(pbf[:, k * 128:(k + 1) * 128], mixed[:, i, k * 128:(k + 1) * 128], identb[:], [mixb[i], cb], [pb], inc=(k == 7))
                    cp(big[:, 8:16, i * 128:(i + 1) * 128], pbf[:, 0:1024].rearrange("p (k t) -> p k t", k=8), [pb], [bigb[i]], eng="act")
                (wo_,), wbo = next_slab()
                for i in range(NT):
                    xt = M["xin"][i % 2]
                    src = x_d.ap()[t0 + i * 128:t0 + (i + 1) * 128, :]
                    P.dma("sp", lambda e, xt=xt, src=src: e.dma_start(out=xt[:], in_=src), writes=[xinb[i % 2]])
                    for cbk in range(2):
                        ps, pb = PS()
                        for k in range(8):
                            mm(ps[:], big[:, 8 + k, i * 128:(i + 1) * 128], wo_[:, k, cbk * 512:(cbk + 1) * 512], k == 0, k == 7,
                               [bigb[i], wbo], [pb], inc=(k == 7))
                        tt(h1[:, i, cbk * 512:(cbk + 1) * 512], ps[:], xt[:, cbk * 512:(cbk + 1) * 512], ALU.add, [pb, xinb[i % 2]], [h1b[i]])
                if blk == 0:
                    dump("h1", h1[:], h1b, [128, NT, 1024])
                P.tag = "mlp"
                load_nw("norm_mlp_w", 0, 1024)
                norm_to_uT(lambda i: h1[:, i, :], lambda i: h1b[i], t0)
                hidb2 = [[Buf("hid%d" % i) for i in range(TB // 512)] for _ in range(2)]
                rl = tmpm; rlb = sgb
                cnt = 0
                for qf in range(4):
                    (wu,), wbu = next_slab()
                    for e2 in range(2):
                        hidb = hidb2[e2]
                        for f4 in range(4):
                            fg = e2 * 4 + f4
                            for tb in range(TB // 512):
                                ps, pb = PS()
                                for k in range(8):
                                    mm(ps[:], wu[:, k, fg * 128:(fg + 1) * 128], uT[:, k, tb * 512:(tb + 1) * 512], k == 0, k == 7,
                                       [wbu] + uTb[tb * 4:(tb + 1) * 4], [pb], inc=(k == 7))
                                q = cnt % 2; cnt += 1
                                act(rl[q][:], ps[:], AF.Relu, [pb], [rlb[q]])
                                tt(sgt[:, fg, tb * 512:(tb + 1) * 512], rl[q][:], rl[q][:], ALU.mult, [rlb[q]], [hidb[tb]])
                    (wd,), wbd = next_slab()
                    for e2 in range(2):
                        hidb = hidb2[e2]
                        for i in range(NT):
                            for cbk in range(2):
                                ps, pb = PS()
                                for c4 in range(4):
                                    c = e2 * 4 + c4
                                    mm(ps[:], sgt[:, c, i * 128:(i + 1) * 128], wd[:, c, cbk * 512:(cbk + 1) * 512], c4 == 0, c4 == 3,
                                       [hidb[i // 4], wbd], [pb], inc=(c4 == 3))
                                tt(h1[:, i, cbk * 512:(cbk + 1) * 512], h1[:, i, cbk * 512:(cbk + 1) * 512], ps[:], ALU.add, [pb, h1b[i]], [h1b[i]])
                if blk == 0:
                    dump("h2", h1[:], h1b, [128, NT, 1024])
                load_nw("norm_final_w", 0, 1024)
                for i in range(NT):
                    st, stbuf = STAT()
                    jk, jkb = JUNK()
                    act(jk[:], h1[:, i, :], AF.Square, [h1b[i]], [stbuf, jkb], accum=st[:, 0:1])
                    rstd_from_ss(st, stbuf, 1024.0, EPS)
                    ot = M["xin"][i % 2]
                    stt(ot[:], h1[:, i, :], st[:, 1:2], M["nw"][:], ALU.mult, ALU.mult, [h1b[i], stbuf, nwb], [xinb[i % 2]])
                    dst = y_d.ap()[t0 + i * 128:t0 + (i + 1) * 128, :]
                    tok = P.dma("sp", lambda e, ot=ot, dst=dst: e.dma_start(out=dst, in_=ot[:]), reads=[xinb[i % 2]])
                    outtoks.append(tok)
            phA.__exit__()

        P.wait_all("sp", dumptoks + outtoks)
        P.emit(sems)
    return nc, dump_t


def make_in_map(inputs, b):
    m = {"x": np.ascontiguousarray(np.asarray(inputs["x"])[b], dtype=np.float32)}
    for n in ("w_in", "w_br_ssd", "w_br_mlstm", "w_out", "w_up", "w_down", "conv_ssd_w", "conv_qk_w"):
        m[n] = np.ascontiguousarray(np.asarray(inputs[n])[0], dtype=np.float32)
    for n in ("conv_ssd_b", "conv_qk_b", "norm_mix_w", "norm_mlp_w", "ssd_norm_w", "mlstm_norm_w", "dt_bias",
              "a_log", "d_skip", "i_bias", "f_bias"):
        m[n] = np.ascontiguousarray(np.asarray(inputs[n]).reshape(1, -1), dtype=np.float32)
    m["norm_final_w"] = np.ascontiguousarray(np.asarray(inputs["norm_final_w"]).reshape(1, -1), dtype=np.float32)
    return m


def kernel(**inputs):
    nc, _ = build()
    in_maps = [make_in_map(inputs, b) for b in range(8)]
    res = run_bass_kernel_spmd(nc, in_maps, core_ids=list(range(8)))
    return np.stack([np.asarray(r["y"], dtype=np.float32) for r in res.results], axis=0)
```

```python
import contextlib
import math
import numpy as np
import concourse.bass as bass
import concourse.mybir as mybir
from concourse.bass_utils import run_bass_kernel_spmd

F32 = mybir.dt.float32
BF16 = mybir.dt.bfloat16
AF = mybir.ActivationFunctionType
ALU = mybir.AluOpType

ENGS = ["pe", "act", "dve", "pool", "sp"]
NDMASEM = 6
EPS = 1e-5

S = 2048
TB = 1024
NB = S // TB
NT = TB // 128
OFF_Z, OFF_X, OFF_B, OFF_C, OFF_DT = 0, 2048, 4096, 4608, 5120
OFF_Q, OFF_K, OFF_V, OFF_O, OFF_I, OFF_F, OFF_G = 5152, 6176, 7200, 8224, 9248, 9252, 9256
LNSCALE = math.log(256 ** -0.5)


class Buf:
    __slots__ = ("name", "w", "r")

    def __init__(self, name):
        self.name = name
        self.w = None
        self.r = []


class Unit:
    __slots__ = ("eng", "fns", "deps", "cost", "kind", "tag", "idx", "start", "end", "count", "tok")

    def __init__(self, eng, kind, tag, idx):
        self.eng = eng
        self.fns = []
        self.deps = set()
        self.cost = 0.0
        self.kind = kind
        self.tag = tag
        self.idx = idx
        self.start = None
        self.end = None
        self.count = None
        self.tok = None


XLAT = 0.5
WINDOW = 600
LOOKAHEAD = 0.3


class Prog:
    def __init__(self, nc):
        self.nc = nc
        self.units = []
        self.open = {e: None for e in ENGS}
        self.last_barrier = {e: None for e in ENGS}
        self.since_barrier = []
        self.semnames = list(ENGS) + ["d%s%d" % (e, i) for e in ("sp", "pool") for i in range(NDMASEM)]
        self.tag = ""
        self.annot = False
        self.schedule = True

    def _unit(self, eng, kind):
        u = self.open[eng]
        if u is None or kind != "op":
            u = Unit(eng, kind, self.tag, len(self.units))
            self.units.append(u)
            if self.last_barrier[eng] is not None:
                u.deps.add(self.last_barrier[eng])
            self.since_barrier.append(u.idx)
        return u

    def _deps(self, u, reads, writes):
        for b in reads:
            if b.w is not None:
                u.deps.add(b.w)
        for b in writes:
            if b.w is not None:
                u.deps.add(b.w)
            u.deps.update(b.r)
        u.deps.discard(u.idx)
        for b in reads:
            if not b.r or b.r[-1] != u.idx:
                b.r.append(u.idx)
        for b in writes:
            b.w = u.idx
            b.r = []

    def op(self, eng, fn, reads=(), writes=(), inc=True, cost=0.3):
        u = self._unit(eng, "op")
        u.fns.append(fn)
        u.cost += cost
        self._deps(u, reads, writes)
        self.open[eng] = None if inc else u
        return u.idx

    def dma(self, eng, fn, reads=(), writes=(), cost=3.0):
        assert self.open[eng] is None
        u = self._unit(eng, "dma")
        u.fns.append(fn)
        u.cost = cost
        self._deps(u, reads, writes)
        return u.idx

    def barrier(self):
        prev = list(self.since_barrier)
        self.since_barrier = []
        news = {}
        for e in ENGS:
            assert self.open[e] is None
            if e == "pe":
                continue
            u = Unit(e, "bar", "", len(self.units))
            self.units.append(u)
            u.deps.update(prev)
            if self.last_barrier[e] is not None:
                u.deps.add(self.last_barrier[e])
            news[e] = u.idx
        news["pe"] = None
        self.last_barrier = news
        self.since_barrier = [v for v in news.values() if v is not None]

    def wait_all(self, eng, toks):
        u = Unit(eng, "bar", "", len(self.units))
        self.units.append(u)
        u.deps.update(toks)
        if self.last_barrier[eng] is not None:
            u.deps.add(self.last_barrier[eng])

    def _schedule(self):
        import heapq
        units = self.units
        byeng = {e: [u for u in units if u.eng == e] for e in ENGS}
        if not self.schedule:
            return byeng
        nun = len(units)
        succ = [[] for _ in range(nun)]
        ndep = [0] * nun
        for u in units:
            ndep[u.idx] = len(u.deps)
            for d in u.deps:
                succ[d].append(u.idx)
        blev = [0.0] * nun
        for u in reversed(units):
            m = 0.0
            for sidx in succ[u.idx]:
                su = units[sidx]
                t = blev[sidx] + (XLAT if su.eng != u.eng or u.kind == "dma" else 0.0)
                if t > m:
                    m = t
            blev[u.idx] = m + u.cost
        inwin = [False] * nun
        nxt = {e: 0 for e in ENGS}
        nin = {e: 0 for e in ENGS}
        blocked = {e: False for e in ENGS}
        hA = {e: [] for e in ENGS}
        hB = {e: [] for e in ENGS}
        etime = {e: 0.0 for e in ENGS}
        order = {e: [] for e in ENGS}

        def ready_time(u):
            r = 0.0
            for d in u.deps:
                du = units[d]
                t = du.end + ((XLAT - (0.15 if u.eng == "dve" else 0.0)) if du.eng != u.eng or du.kind == "dma" else 0.0)
                if t > r:
                    r = t
            return r

        def admit(e):
            lst = byeng[e]
            while nxt[e] < len(lst) and nin[e] < WINDOW and not blocked[e]:
                u = lst[nxt[e]]
                nxt[e] += 1
                nin[e] += 1
                inwin[u.idx] = True
                if u.kind == "bar":
                    blocked[e] = True
                if ndep[u.idx] == 0:
                    heapq.heappush(hA[e], (ready_time(u), u.idx))

        for e in ENGS:
            admit(e)
        remaining = nun
        while remaining:
            best = None
            for e in ENGS:
                A = hA[e]; B = hB[e]
                while A and A[0][0] <= etime[e]:
                    ii = heapq.heappop(A)[1]
                    heapq.heappush(B, (-blev[ii], ii))
                if B:
                    key = (etime[e], B[0][1])
                    if A and LOOKAHEAD > 0:
                        rA, iA = A[0]
                        ub = units[B[0][1]]
                        if blev[iA] > blev[ub.idx] + 1.0 and rA - etime[e] < min(ub.cost * 0.6, LOOKAHEAD):
                            key = (rA, iA)
                elif A:
                    key = (A[0][0], A[0][1])
                else:
                    continue
                if best is None or key < best[0]:
                    best = (key, e)
            assert best is not None, "scheduler stuck"
            (st, idx), e = best
            if hB[e] and hB[e][0][1] == idx:
                heapq.heappop(hB[e])
            else:
                heapq.heappop(hA[e])
            u = units[idx]
            u.start = st
            if u.kind == "dma":
                etime[e] = st + 0.15
                u.end = st + u.cost
            else:
                etime[e] = st + u.cost
                u.end = etime[e]
            order[e].append(u)
            nin[e] -= 1
            if u.kind == "bar":
                blocked[e] = False
            remaining -= 1
            for sidx in succ[idx]:
                ndep[sidx] -= 1
                if ndep[sidx] == 0 and inwin[sidx]:
                    su = units[sidx]
                    heapq.heappush(hA[su.eng], (ready_time(su), sidx))
            admit(e)
        self.makespan = max(u.end for u in units)
        return order

    def emit(self, sems):
        nc = self.nc
        units = self.units
        order = self._schedule()
        for e in ENGS:
            c = 0
            k = 0
            dval = {}
            for u in order[e]:
                if u.kind == "op":
                    c += 1
                    u.tok = (e, c)
                elif u.kind == "dma":
                    sem = "d%s%d" % (e, k)
                    k = (k + 1) % NDMASEM
                    prev = dval.get(sem, 0)
                    dval[sem] = prev + 16
                    u.tok = (sem, prev + 16)
                    u.count = (sem, prev)
                else:
                    u.tok = None
        K = {e: {} for e in ENGS}
        snap = {}
        queues = {e: [] for e in ENGS}
        ptr = {e: 0 for e in ENGS}
        processed = set()
        total = sum(len(order[e]) for e in ENGS)
        ndone = 0
        while ndone < total:
            progressed = False
            for e in ENGS:
                lst = order[e]
                while ptr[e] < len(lst):
                    u = lst[ptr[e]]
                    if any(d not in processed for d in u.deps):
                        break
                    known = K[e]
                    need = {}
                    srcs = []
                    for d in u.deps:
                        du = units[d]
                        if du.tok is None:
                            continue
                        sname, v = du.tok
                        if du.eng == e and du.kind == "op":
                            if e in ("pe", "sp") and u.kind != "dma":
                                continue
                        if known.get(sname, 0) < v:
                            if need.get(sname, 0) < v:
                                need[sname] = v
                            srcs.append(d)
                    if u.kind == "dma" and u.count[1] > 0:
                        sname, v = u.count
                        if known.get(sname, 0) < v and need.get(sname, 0) < v:
                            need[sname] = v
                    for d in srcs:
                        for sname, v in snap[d].items():
                            if known.get(sname, 0) < v:
                                known[sname] = v
                    final = []
                    for sname, v in need.items():
                        implied = False
                        for d in srcs:
                            du = units[d]
                            if du.tok[0] != sname and snap[d].get(sname, 0) >= v:
                                implied = True
                                break
                        if not implied:
                            final.append((sname, v))
                        if known.get(sname, 0) < v:
                            known[sname] = v
                    queues[e].append((final, u))
                    sn = dict(known)
                    if u.tok is not None:
                        if sn.get(u.tok[0], 0) < u.tok[1]:
                            sn[u.tok[0]] = u.tok[1]
                    snap[u.idx] = sn
                    processed.add(u.idx)
                    ptr[e] += 1
                    ndone += 1
                    progressed = True
            assert progressed, "wait derivation stuck"
        self._check(queues)
        self.queues = queues

        def replay(eobj, name):
            own = sems[name]
            for waits, u in queues[name]:
                for s, v in waits:
                    eobj.wait_ge(sems[s], v)
                n = len(u.fns)
                for j, fn in enumerate(u.fns):
                    ins = fn(eobj)
                    if self.annot:
                        ins.annotate(u.tag)
                    if j == n - 1:
                        if u.kind == "op":
                            ins.then_inc(own, 1)
                        else:
                            ins.then_inc(sems[u.tok[0]], 16)

        with nc.Block() as block:
            @block.tensor
            def _(e):
                replay(e, "pe")

            @block.scalar
            def _(e):
                replay(e, "act")

            @block.vector
            def _(e):
                replay(e, "dve")

            @block.gpsimd
            def _(e):
                replay(e, "pool")

            @block.sync
            def _(e):
                replay(e, "sp")

    def _check(self, queues):
        sem = {}
        ptr = {e: 0 for e in ENGS}
        total = sum(len(q) for q in queues.values())
        donecount = 0
        progress = True
        while progress:
            progress = False
            for e in ENGS:
                q = queues[e]
                while ptr[e] < len(q):
                    waits, u = q[ptr[e]]
                    if all(sem.get(s, 0) >= v for s, v in waits):
                        if u.tok is not None:
                            s, v = u.tok
                            sem[s] = sem.get(s, 0) + (1 if u.kind == "op" else 16)
                            assert sem[s] == v, (s, v, sem[s])
                        ptr[e] += 1
                        donecount += 1
                        progress = True
                    else:
                        break
        assert donecount == total, ("deadlock in emitted program", {e: (ptr[e], len(queues[e])) for e in ENGS})


def build(stop_after=None, dumps=(), annot=False):
    nc = bass.Bass("TRN2", target_bir_lowering=False)
    P = Prog(nc)
    P.annot = annot
    dumps = set(dumps)

    def din(name, shape):
        return nc.dram_tensor(name, shape, F32, kind="ExternalInput")

    x_d = din("x", [S, 1024])
    w_in = din("w_in", [1024, 11304])
    w_brs = din("w_br_ssd", [2048, 1024])
    w_brm = din("w_br_mlstm", [1024, 1024])
    w_out = din("w_out", [1024, 1024])
    w_up = din("w_up", [1024, 4096])
    w_dn = din("w_down", [4096, 1024])
    conv_ssd_w = din("conv_ssd_w", [4, 3072])
    conv_ssd_b = din("conv_ssd_b", [1, 3072])
    conv_qk_w = din("conv_qk_w", [4, 2048])
    conv_qk_b = din("conv_qk_b", [1, 2048])
    vecs = {n: din(n, [1, m]) for n, m in [("norm_mix_w", 1024), ("norm_mlp_w", 1024), ("norm_final_w", 1024),
                                           ("ssd_norm_w", 2048), ("mlstm_norm_w", 1024), ("dt_bias", 32),
                                           ("a_log", 32), ("d_skip", 32), ("i_bias", 4), ("f_bias", 4)]}
    y_d = nc.dram_tensor("y", [S, 1024], F32, kind="ExternalOutput")
    dump_t = {}

    es = contextlib.ExitStack()
    ARENA_BASE, ARENA_TOP = 16512, 229376
    arena = {"p": ARENA_BASE, "n": 0, "hi": 0}
    arena_cache = {}

    def sb(name, shape, dt=F32, stack=None):
        nbytes = (2 if dt == BF16 else 4)
        for d in shape[1:]:
            nbytes *= d
        nbytes = (nbytes + 63) // 64 * 64
        off = arena["p"]
        arena["p"] = off + nbytes
        arena["hi"] = max(arena["hi"], arena["p"])
        assert arena["p"] <= ARENA_TOP, ("SBUF overflow", name, arena["p"])
        key = (name, tuple(shape), str(dt), off)
        if key not in arena_cache:
            arena["n"] += 1
            arena_cache[key] = nc.alloc_sbuf_tensor_at("%s_%d" % (name, arena["n"]), list(shape), dt, offset=off)
        return arena_cache[key]

    class Phase:
        def __enter__(self):
            self.mark = arena["p"]
            return self

        def __exit__(self, *a):
            P.barrier()
            arena["p"] = self.mark
            return False

    with es:

        sems = {n: es.enter_context(nc.semaphore(n)) for n in P.semnames}
        psb = [es.enter_context(nc.psum_tensor("ps%d" % i, [128, 512], F32)) for i in range(8)]
        psbuf = [Buf("ps%d" % i) for i in range(8)]
        pscnt = [0]

        pscnt2 = [0]

        regbuf = {}
        rings = {}
        ringcnt = {}

        def set_rings(layout):
            rings.clear()
            for cls, regs in layout.items():
                rings[cls] = regs
                ringcnt.setdefault(cls, 0)

        def full(banks):
            return [(k, 0, 512) for k in banks]

        LAY_DEFAULT = {"conv": full([0, 1]), "core": full([2, 3, 4, 5, 6, 7])}
        LAY_SSD = {"conv": full([0, 1]), "core": full([2, 3, 4]), "s": full([5]), "s2": full([6]),
                   "q": full([7]), "t": full([7])}
        LAY_ML = {"conv": full([0, 1]), "core": full([2, 3, 4]), "c": full([5]), "n": full([6]),
                  "m64": full([7]), "q": full([7]), "t": full([7])}
        set_rings(LAY_DEFAULT)

        def PS(cls="core"):
            regs = rings[cls]
            r = regs[ringcnt[cls] % len(regs)]
            ringcnt[cls] += 1
            if r not in regbuf:
                regbuf[r] = Buf("ps%d_%d" % (r[0], r[1]))
            k, c0, w = r
            return psb[k][:, c0:c0 + w], regbuf[r]

        def fsz(ap):
            n = 1
            for d in ap.shape[1:]:
                n *= d
            return n

        def mm(out, lhsT, rhs, start=True, stop=True, reads=(), writes=(), inc=True):
            c = 0.03 + max(fsz(rhs), 64) / 2400.0 * (4.0 if rhs.dtype == F32 else 1.0)
            P.op("pe", lambda e: e.matmul(out, lhsT=lhsT, rhs=rhs, start=start, stop=stop), reads, writes, inc, cost=c)

        def tr(out, in_, ident, reads=(), writes=(), inc=True):
            P.op("pe", lambda e: e.transpose(out, in_, ident), reads, writes, inc, cost=0.1)

        def act(out, in_, func, reads=(), writes=(), bias=None, scale=None, accum=None):
            kw = {}
            if bias is not None:
                kw["bias"] = bias
            if scale is not None:
                kw["scale"] = scale
            if accum is not None:
                kw["accum_out"] = accum
            c = 0.2 + max(fsz(out), 64) / 1400.0 + (0.1 if accum is not None else 0.0)
            P.op("act", lambda e: e.activation(out, in_, func, **kw), reads, writes, cost=c)

        def tt(out, in0, in1, op, reads=(), writes=(), eng="dve"):
            c = (0.08 if eng == "dve" else 0.35) + max(fsz(out), 64) / 960.0
            if op == ALU.pow:
                c = 1.0
            P.op(eng, lambda e: e.tensor_tensor(out, in0, in1, op), reads, writes, cost=c)

        def ts(out, in0, s1, s2, op0, op1=None, reads=(), writes=(), eng="dve"):
            c = (0.08 if eng == "dve" else 0.35) + max(fsz(out), 64) / 960.0
            if op1 is None:
                P.op(eng, lambda e: e.tensor_scalar(out, in0, s1, None, op0), reads, writes, cost=c)
            else:
                P.op(eng, lambda e: e.tensor_scalar(out, in0, s1, s2, op0, op1), reads, writes, cost=c)

        def stt(out, in0, scalar, in1, op0, op1, reads=(), writes=()):
            c = 0.08 + max(fsz(out), 64) / 960.0
            P.op("dve", lambda e: e.scalar_tensor_tensor(out, in0, scalar, in1, op0, op1), reads, writes, cost=c)

        def cp(out, in_, reads=(), writes=(), eng="dve"):
            if eng == "act":
                act(out, in_, AF.Copy, reads, writes)
            else:
                c = (0.08 if eng == "dve" else 0.35) + max(fsz(out), 64) / 960.0
                P.op(eng, lambda e: e.tensor_copy(out, in_), reads, writes, cost=c)

        def memset(ap, val, writes=(), eng="pool"):
            P.op(eng, lambda e: e.memset(ap, val), (), writes, cost=0.35 + fsz(ap) / 960.0)

        def bc_row(handle, off, n):
            return bass.AP(handle, off, [[0, 128], [1, n]])

        def dump(name, ap, bufs, shape, dt=F32):
            if name not in dumps:
                return
            t = nc.dram_tensor("dbg_" + name, list(shape), dt, kind="ExternalOutput")
            dump_t[name] = t
            tok = P.dma("sp", lambda e: e.dma_start(out=t.ap(), in_=ap), reads=bufs)
            dumptoks.append(tok)

        dumptoks = []
        outtoks = []

        cb = Buf("const")
        identf = sb("identf", [128, 128]); identb = sb("identb", [128, 128], BF16)
        Lmask = sb("Lmask", [128, 128]); Umask = sb("Umask", [128, 128]); Bd = sb("Bd", [128, 128])
        sel0 = sb("sel0", [128, 128]); sel1 = sb("sel1", [128, 128])
        tri = sb("tri", [128, 64]); Idm = sb("Idm", [128, 64])
        mhalf = sb("mhalf", [128, 1])
        memset(identf[:], 0.0, [cb])
        P.op("pool", lambda e: e.affine_select(identf[:], identf[:], pattern=[[-1, 128]], compare_op=ALU.not_equal,
                                               fill=1.0, base=0, channel_multiplier=1), [cb], [cb])
        cp(identb[:], identf[:], [cb], [cb], eng="pool")
        memset(Lmask[:], 1.0, [cb])
        P.op("pool", lambda e: e.affine_select(Lmask[:], Lmask[:], pattern=[[1, 128]], compare_op=ALU.is_ge,
                                               fill=0.0, base=0, channel_multiplier=-1), [cb], [cb])
        memset(Lmask[0:64, 64:128], 0.0, [cb])
        memset(Umask[:], 1.0, [cb])
        P.op("pool", lambda e: e.affine_select(Umask[:], Umask[:], pattern=[[-1, 128]], compare_op=ALU.is_gt,
                                               fill=0.0, base=0, channel_multiplier=1), [cb], [cb])
        memset(Umask[64:128, 0:64], 0.0, [cb])
        memset(Bd[:], 0.0, [cb]); memset(Bd[0:64, 0:64], 1.0, [cb]); memset(Bd[64:128, 64:128], 1.0, [cb])
        memset(sel0[:], 0.0, [cb]); memset(sel0[0:64, :], 1.0, [cb])
        memset(sel1[:], 0.0, [cb]); memset(sel1[64:128, :], 1.0, [cb])
        cp(tri[0:64, :], Lmask[0:64, 0:64], [cb], [cb], eng="pool")
        cp(tri[64:128, :], Lmask[64:128, 64:128], [cb], [cb], eng="pool")
        cp(Idm[0:64, :], identf[0:64, 0:64], [cb], [cb], eng="pool")
        cp(Idm[64:128, :], identf[64:128, 64:128], [cb], [cb], eng="pool")
        memset(mhalf[:], -0.5, [cb])

        cw = sb("cw", [128, 40, 5])
        cwb = Buf("cw")
        with Phase():
            cwT = sb("cwT", [5, 5120])
            P.dma("sp", lambda e: e.dma_start(out=cwT[0:4, 0:3072], in_=conv_ssd_w.ap()), writes=[cwb])
            P.dma("sp", lambda e: e.dma_start(out=cwT[4:5, 0:3072], in_=conv_ssd_b.ap()), writes=[cwb])
            P.dma("sp", lambda e: e.dma_start(out=cwT[0:4, 3072:5120], in_=conv_qk_w.ap()), writes=[cwb])
            P.dma("sp", lambda e: e.dma_start(out=cwT[4:5, 3072:5120], in_=conv_qk_b.ap()), writes=[cwb])
            ps, pb = PS()
            for g in range(40):
                mm(ps[:, g * 5:(g + 1) * 5], cwT[0:5, g * 128:(g + 1) * 128], identf[0:5, 0:5],
                   reads=[cwb, cb], writes=[pb], inc=(g == 39))
            cp(cw[:].rearrange("p g f -> p (g f)"), ps[:, 0:200], [pb], [cwb])

        dtb_bc = sb("dtb_bc", [128, 32]); A_bc = sb("A_bc", [128, 32]); dsk_bc = sb("dsk_bc", [128, 32])
        fb4 = sb("fb4", [4, 1]); ib4 = sb("ib4", [4, 1]); negfb = sb("negfb", [4, 1])
        P.dma("sp", lambda e: e.dma_start(out=dtb_bc[:], in_=bc_row(vecs["dt_bias"], 0, 32)), writes=[cb])
        P.dma("sp", lambda e: e.dma_start(out=A_bc[:], in_=bc_row(vecs["a_log"], 0, 32)), writes=[cb])
        P.dma("sp", lambda e: e.dma_start(out=dsk_bc[:], in_=bc_row(vecs["d_skip"], 0, 32)), writes=[cb])
        P.dma("sp", lambda e: e.dma_start(out=fb4[:], in_=bass.AP(vecs["f_bias"], 0, [[1, 4], [1, 1]])), writes=[cb])
        P.dma("sp", lambda e: e.dma_start(out=ib4[:], in_=bass.AP(vecs["i_bias"], 0, [[1, 4], [1, 1]])), writes=[cb])
        act(A_bc[:], A_bc[:], AF.Exp, [cb], [cb])
        ts(A_bc[:], A_bc[:], -1.0, None, ALU.mult, reads=[cb], writes=[cb])
        ts(negfb[:], fb4[:], -1.0, None, ALU.mult, reads=[cb], writes=[cb])
        dI = sb("dI", [128, 32, 64], BF16)
        tt(dI[:], dsk_bc[:, :].unsqueeze(2).broadcast_to([128, 32, 64]),
           Idm[:, :].unsqueeze(1).broadcast_to([128, 32, 64]), ALU.mult, [cb], [cb])
        I4x = sb("I4x", [4, 4, 32])
        tt(I4x[:], identf[0:4, 0:4].unsqueeze(2).broadcast_to([4, 4, 32]),
           identf[0:4, 0:4].unsqueeze(2).broadcast_to([4, 4, 32]), ALU.mult, [cb], [cb])
        ones4 = sb("ones4", [4, 128])
        memset(ones4[:], 1.0, [cb])
        stb = Buf("state")
        hst = sb("hst", [128, 4, 512]); hbf = sb("hbf", [128, 4, 512], BF16)
        Cst = sb("Cst", [128, 4, 2, 258]); Cbf = sb("Cbf", [128, 4, 2, 258], BF16)
        rawc = sb("rawc", [128, 40, 3])
        mcar = sb("mcar", [4, 1])
        memset(hst[:], 0.0, [stb]); memset(hbf[:], 0.0, [stb])
        memset(Cst[:], 0.0, [stb]); memset(Cbf[:], 0.0, [stb])
        memset(rawc[:], 0.0, [stb]); memset(mcar[:], 0.0, [stb])
        hgb = [Buf("h%d" % g) for g in range(4)]
        hbfb = [Buf("hbf%d" % g) for g in range(4)]
        Chb = [Buf("C%d" % h) for h in range(4)]
        Cbfb = [Buf("Cbf%d" % h) for h in range(4)]
        rawcb = [Buf("rawc%d" % g) for g in range(40)]
        for b in hgb + hbfb + Chb + Cbfb + rawcb:
            b.w = stb.w
        mcarb = Buf("mcar"); mcarb.w = stb.w

        NST = 6
        sst = sb("sst", [128, NST, 4]); sstb = [Buf("sst%d" % i) for i in range(NST)]
        stc = [0]

        def STAT():
            k = stc[0] % NST
            stc[0] += 1
            return sst[:, k, :], sstb[k]

        junkt = [sb("junk%d" % i, [128, 1024], BF16) for i in range(2)]; junkbs = [Buf("junk%d" % i) for i in range(2)]
        jc = [0]

        def JUNK():
            k = jc[0] % 2
            jc[0] += 1
            return junkt[k], junkbs[k]

        def rstd_from_ss(st, stbuf, n, eps):
            ts(st[:, 1:2], st[:, 0:1], 1.0 / n, eps, ALU.mult, ALU.add, [stbuf], [stbuf], eng="pool")
            tt(st[:, 1:2], st[:, 1:2], mhalf[:, 0:1], ALU.pow, [stbuf, cb], [stbuf], eng="pool")

        NSLAB = 2
        slab_t = [sb("slab%d" % i, [128, 8192], BF16) for i in range(NSLAB)]
        slab_b = [Buf("slab%d" % i) for i in range(NSLAB)]

        def piece(handle, r0, k, c0, n):
            return (handle, r0, k, c0, n)

        sched = []
        for blk in range(NB):
            def s_xbc(g):
                return [piece(w_in, 0, 8, OFF_X + g * 512, 512), piece(w_in, 0, 8, OFF_B + g * 128, 128),
                        piece(w_in, 0, 8, OFF_C + g * 128, 128)]

            def s_z(g):
                return [piece(w_in, 0, 8, OFF_Z + g * 512, 512)]
            sched.extend([s_xbc(0), s_xbc(1), s_z(0), s_xbc(2), s_z(1), s_xbc(3), s_z(2), s_z(3)])
            sched.append([piece(w_in, 0, 8, OFF_G, 1024)])
            sched.append([piece(w_brs, 0, 16, 0, 512)])
            sched.append([piece(w_brs, 0, 16, 512, 512)])
            for h in range(4):
                sched.append([piece(w_in, 0, 8, OFF_Q + h * 256, 256), piece(w_in, 0, 8, OFF_K + h * 256, 256),
                              piece(w_in, 0, 8, OFF_V + h * 256, 256), piece(w_in, 0, 8, OFF_O + h * 256, 256)])
            sched.append([piece(w_in, 0, 8, OFF_G + 1024, 1024)])
            sched.append([piece(w_brm, 0, 8, 0, 1024)])
            sched.append([piece(w_out, 0, 8, 0, 1024)])
            for qf in range(4):
                sched.append([piece(w_up, 0, 8, qf * 1024, 1024)])
                sched.append([piece(w_dn, qf * 1024, 8, 0, 1024)])
        slab_issued = [0]
        slab_cur = [0]

        def issue_slab(i):
            t = slab_t[i % NSLAB]; b = slab_b[i % NSLAB]
            off = 0
            for (handle, r0, k, c0, n) in sched[i]:
                dst = t[:, off:off + k * n].rearrange("p (k n) -> p k n", k=k)
                src = handle.ap()[r0:r0 + k * 128, c0:c0 + n].rearrange("(k p) n -> p k n", p=128)
                P.dma("pool", lambda e, dst=dst, src=src: e.dma_start(out=dst, in_=src), writes=[b], cost=2.5 + k * n * 512.0 / 300e3)
                off += k * n

        def next_slab():
            i = slab_cur[0]
            slab_cur[0] += 1
            while slab_issued[0] < min(len(sched), i + NSLAB):
                issue_slab(slab_issued[0])
                slab_issued[0] += 1
            t = slab_t[i % NSLAB]; b = slab_b[i % NSLAB]
            views = []
            off = 0
            for (handle, r0, k, c0, n) in sched[i]:
                views.append(t[:, off:off + k * n].rearrange("p (k n) -> p k n", k=k))
                off += k * n
            return views, b

        wsm = sb("wsm", [128, 8, 40], BF16); wsmb = Buf("wsm")
        P.dma("pool", lambda e: e.dma_start(out=wsm[:, :, 0:32],
                                            in_=w_in.ap()[:, OFF_DT:OFF_DT + 32].rearrange("(k p) n -> p k n", p=128)),
              writes=[wsmb])
        P.dma("pool", lambda e: e.dma_start(out=wsm[:, :, 32:40],
                                            in_=w_in.ap()[:, OFF_I:OFF_I + 8].rearrange("(k p) n -> p k n", p=128)),
              writes=[wsmb])

        uT = sb("uT", [128, 8, TB], BF16); uTb = [Buf("uT%d" % i) for i in range(NT)]
        big = sb("big", [128, 16, TB], BF16)
        bigb = [Buf("big%d" % i) for i in range(NT)]
        mixb = [Buf("mix%d" % i) for i in range(NT)]
        h1b = [Buf("h1_%d" % i) for i in range(NT)]
        xinb = [Buf("xin%d" % i) for i in range(4)]
        ubfb = [Buf("ubf%d" % i) for i in range(4)]
        nwb = Buf("nw")
        rawb = [Buf("raw%d" % i) for i in range(2)]
        accb = [Buf("acc%d" % i) for i in range(2)]
        fmTb = [Buf("fmT%d" % i) for i in range(2)]
        M = {}

        def alloc_norm_bufs(nb=2):
            M["nb"] = nb
            M["xin"] = [sb("xin%d" % i, [128, 1024]) for i in range(nb)]
            M["ubf"] = [sb("ubf%d" % i, [128, 1024], BF16) for i in range(nb)]
            M["nw"] = sb("nw", [128, 1024])

        def alloc_conv_bufs():
            M["raw"] = [sb("raw%d" % i, [128, 4 + TB], BF16) for i in range(2)]
            M["dg"] = [sb("dg%d" % i, [128, 4, 128], BF16) for i in range(2)]
            M["fmT"] = [sb("fmT%d" % i, [128, TB], BF16) for i in range(2)]
        cvc = [0]

        def load_nw(name, off, n):
            nw = M["nw"]
            P.dma("sp", lambda e: e.dma_start(out=nw[:, 0:n], in_=bc_row(vecs[name], off, n)), writes=[nwb])

        def norm_to_uT(src_fn, srcb_fn, t0):
            for i in range(NT):
                src = src_fn(i); sbuf_ = srcb_fn(i)
                st, stbuf = STAT()
                jk, jkb = JUNK()
                act(jk[:], src, AF.Square, [sbuf_], [stbuf, jkb], accum=st[:, 0:1])
                rstd_from_ss(st, stbuf, 1024.0, EPS)
                u = M["ubf"][i % M["nb"]]; ub_ = ubfb[i % M["nb"]]
                stt(u[:], src, st[:, 1:2], M["nw"][:], ALU.mult, ALU.mult, [sbuf_, stbuf, nwb], [ub_])
                ps, pb = PS()
                pbf = ps.bitcast(BF16)
                for k in range(8):
                    tr(pbf[:, k * 128:(k + 1) * 128], u[:, k * 128:(k + 1) * 128], identb[:], [ub_, cb], [pb], inc=(k == 7))
                cp(uT[:, :, i * 128:(i + 1) * 128], pbf[:, 0:1024].rearrange("p (k t) -> p k t", k=8), [pb], [uTb[i]], eng="act")

        def fm_conv(wp, wbuf, col0, cwg, dst, dstb):
            ri = cvc[0] % 2
            cvc[0] += 1
            raw = M["raw"][ri]; rb = rawb[ri]; dg = M["dg"][ri]; db = accb[ri]
            tt(dg[:], identf[:, :].unsqueeze(1).broadcast_to([128, 4, 128]),
               cw[:, cwg, 0:4].unsqueeze(2).broadcast_to([128, 4, 128]), ALU.mult, [cb, cwb], [db])
            cp(raw[:, 0:3], rawc[:, cwg, :], [rawcb[cwg]], [rb])
            for tb in range(TB // 512):
                ps, pb = PS("conv")
                for k in range(8):
                    mm(ps[:], wp[:, k, col0:col0 + 128], uT[:, k, tb * 512:(tb + 1) * 512], k == 0, k == 7,
                       [wbuf] + uTb[tb * 4:(tb + 1) * 4], [pb], inc=(k in (3, 7)))
                cp(raw[:, 3 + tb * 512:3 + (tb + 1) * 512], ps[:], [pb], [rb], eng="act")
            cp(rawc[:, cwg, :], raw[:, TB:TB + 3], [rb], [rawcb[cwg]])
            for tb in range(TB // 512):
                ps, pb = PS("conv")
                for tap in range(4):
                    mm(ps[:], dg[:, tap, :], raw[:, tap + tb * 512:tap + tb * 512 + 512], tap == 0, tap == 3, [db, rb], [pb], inc=(tap in (1, 3)))
                act(dst[:, tb * 512:(tb + 1) * 512], ps[:], AF.Silu, [pb, cwb], [dstb], bias=cw[:, cwg, 4:5])

        def transpose_blocks(src, srcb, n128, dst3, dstbs, eng="act"):
            ps, pb = PS("conv")
            pbf = ps.bitcast(BF16)
            for j in range(NT):
                tr(pbf[:, j * 128:(j + 1) * 128], src[:, j * 128:(j + 1) * 128], identb[:], [srcb, cb], [pb], inc=(j == NT - 1))
            cp(dst3, pbf[:, 0:NT * 128].rearrange("p (a b) -> p a b", a=NT), [pb], dstbs, eng=eng)

        for blk in range(NB):
            t0 = blk * TB
            P.tag = "S0"
            ph0 = Phase(); ph0.__enter__()
            alloc_norm_bufs(4)
            load_nw("norm_mix_w", 0, 1024)

            def xsrc(i, t0=t0):
                xt = M["xin"][i % 4]
                src = x_d.ap()[t0 + i * 128:t0 + (i + 1) * 128, :]
                P.dma("sp", lambda e: e.dma_start(out=xt[:], in_=src), writes=[xinb[i % 4]])
                return xt[:]
            norm_to_uT(xsrc, lambda i: xinb[i % 4], t0)
            if blk == 0:
                dump("uT", uT[:], uTb, [128, 8, TB], BF16)
            ph0.__exit__()
            if stop_after == "S0":
                break

            P.tag = "ssd_pre"
            with Phase():
                psb_ = sb
                set_rings(LAY_SSD)
                alloc_conv_bufs()
                dt_tok = psb_("dt_tok", [128, NT, 32]); a_tok = psb_("a_tok", [128, NT, 32])
                sm = psb_("sm", [128, NT, 4, 32]); dte = psb_("dte", [128, NT, 32])
                smb = [Buf("sm%d" % i) for i in range(NT)]
                for i in range(NT):
                    ps, pb = PS()
                    for k in range(8):
                        mm(ps[:, 0:32], uT[:, k, i * 128:(i + 1) * 128], wsm[:, k, 0:32], k == 0, k == 7,
                           [uTb[i], wsmb], [pb], inc=(k == 7))
                    tt(dt_tok[:, i, :], ps[:, 0:32], dtb_bc[:], ALU.add, [pb, cb], [smb[i]])
                    act(dt_tok[:, i, :], dt_tok[:, i, :], AF.Exp, [smb[i]], [smb[i]])
                for i in range(NT):
                    act(dt_tok[:, i, :], dt_tok[:, i, :], AF.Ln, [smb[i]], [smb[i]], bias=1.0)
                    tt(a_tok[:, i, :], dt_tok[:, i, :], A_bc[:], ALU.mult, [smb[i], cb], [smb[i]])
                for i in range(NT):
                    ps, pb = PS()
                    for j, msk in enumerate([Lmask, Umask, sel0, sel1]):
                        mm(ps[:, j * 32:(j + 1) * 32], msk[:], a_tok[:, i, :], True, True, [smb[i], cb], [pb], inc=(j == 3))
                    act(sm[:, i, :, :].rearrange("p a b -> p (a b)"), ps[:, 0:128], AF.Exp, [pb], [smb[i]])
                    tt(dte[:, i, :], dt_tok[:, i, :], sm[:, i, 1, :], ALU.mult, [smb[i]], [smb[i]])
                if blk == 0:
                    dump("dt", dt_tok[:], smb, [128, NT, 32])

                x_tok2 = [psb_("x_tok", [128, NT, 512], BF16) for _ in range(2)]
                x_tokb2 = [[Buf("xtok%d" % i) for i in range(NT)] for _ in range(2)]
                BT2 = [psb_("BT", [128, TB], BF16) for _ in range(2)]; BTb2 = [Buf("BT") for _ in range(2)]
                CT2 = [psb_("CT", [128, TB], BF16) for _ in range(2)]; CTb2 = [Buf("CT") for _ in range(2)]
                B_tok2 = [psb_("B_tok", [128, NT, 128], BF16) for _ in range(2)]
                B_tokb2 = [[Buf("Btok%d" % i) for i in range(NT)] for _ in range(2)]
                rseg = [psb_("rseg%d" % i, [128, 8, 64]) for i in range(3)]
                esg = [psb_("esg%d" % i, [128, 8, 64]) for i in range(3)]
                cbm = [psb_("cbm%d" % i, [128, 64]) for i in range(3)]
                wpr = [psb_("wpr%d" % i, [128, 8, 64], BF16) for i in range(3)]
                xdt = [psb_("xdt%d" % i, [128, 512], BF16) for i in range(3)]
                xw = [psb_("xw%d" % i, [128, 512], BF16) for i in range(3)]
                indb = [Buf("ind%d" % i) for i in range(3)]
                yis2 = [psb_("yis", [128, 512]) for _ in range(2)]; th2 = [psb_("th", [128, 512]) for _ in range(2)]
                ysn2 = [psb_("ysn", [128, 512], BF16) for _ in range(2)]
                seqb2 = [Buf("seq%d" % i) for i in range(2)]
                htmp = psb_("htmp", [128, 512]); htb = Buf("htmp")
                nws = psb_("nws", [128, 512]); nwsb = Buf("nws")

                def ssd_pre(g):
                    par = g % 2
                    x_tok = x_tok2[par]; x_tokb = x_tokb2[par]; BT = BT2[par]; BTb = BTb2[par]
                    CT = CT2[par]; CTb = CTb2[par]; B_tok = B_tok2[par]; B_tokb = B_tokb2[par]
                    (wx, wB, wC), wb1 = next_slab()
                    P.tag = "ssd_conv"
                    for c4 in range(4):
                        f = M["fmT"][c4 % 2]; fb_ = fmTb[c4 % 2]
                        fm_conv(wx, wb1, c4 * 128, g * 4 + c4, f[:], fb_)
                        transpose_blocks(f, fb_, 1, x_tok[:, :, c4 * 128:(c4 + 1) * 128], x_tokb, eng="dve")
                    fm_conv(wB, wb1, 0, 16 + g, BT[:], BTb)
                    transpose_blocks(BT, BTb, 1, B_tok[:, :, :], B_tokb, eng="dve")
                    fm_conv(wC, wb1, 0, 20 + g, CT[:], CTb)

                def ssd_core(g):
                    par = g % 2
                    x_tok = x_tok2[par]; x_tokb = x_tokb2[par]; BT = BT2[par]; BTb = BTb2[par]
                    CT = CT2[par]; CTb = CTb2[par]; B_tok = B_tok2[par]; B_tokb = B_tokb2[par]
                    (wz,), wb2 = next_slab()
                    P.dma("sp", lambda e, g=g: e.dma_start(out=nws[:], in_=bc_row(vecs["ssd_norm_w"], g * 512, 512)), writes=[nwsb])
                    if blk == 0 and g == 0:
                        dump("x_tok0", x_tok[:], x_tokb, [128, NT, 512], BF16)
                        dump("BT0", BT[:], [BTb], [128, TB], BF16)

                    def indep(i, g=g):
                        P.tag = "ssd_indep"
                        q = i % 3
                        ib = indb[q]
                        tt(rseg[q][:], a_tok[:, i, g * 8:(g + 1) * 8].unsqueeze(2).broadcast_to([128, 8, 64]),
                           tri[:, :].unsqueeze(1).broadcast_to([128, 8, 64]), ALU.mult, [smb[i], cb], [ib])
                        ps, pb = PS("s")
                        mm(ps[:], Umask[:], rseg[q][:].rearrange("p a b -> p (a b)"), True, True, [ib, cb], [pb])
                        act(esg[q][:].rearrange("p a b -> p (a b)"), ps[:], AF.Exp, [pb], [ib])
                        ps2, pb2 = PS("q")
                        mm(ps2[:, 0:128], BT[:, i * 128:(i + 1) * 128], CT[:, i * 128:(i + 1) * 128], True, True,
                           [BTb, CTb], [pb2])
                        tt(cbm[q][0:64, :], ps2[0:64, 0:64], tri[0:64, :], ALU.mult, [pb2, cb], [ib])
                        tt(cbm[q][64:128, :], ps2[64:128, 64:128], tri[64:128, :], ALU.mult, [pb2, cb], [ib])
                        tt(wpr[q][:], esg[q][:], cbm[q][:, :].unsqueeze(1).broadcast_to([128, 8, 64]), ALU.mult, [ib], [ib])
                        tt(xdt[q][:].rearrange("p (a b) -> p a b", a=8), x_tok[:, i, :].rearrange("p (a b) -> p a b", a=8),
                           dt_tok[:, i, g * 8:(g + 1) * 8].unsqueeze(2).broadcast_to([128, 8, 64]), ALU.mult,
                           [x_tokb[i], smb[i]], [ib], eng="pool")
                        tt(xw[q][:].rearrange("p (a b) -> p a b", a=8), x_tok[:, i, :].rearrange("p (a b) -> p a b", a=8),
                           dte[:, i, g * 8:(g + 1) * 8].unsqueeze(2).broadcast_to([128, 8, 64]), ALU.mult,
                           [x_tokb[i], smb[i]], [ib], eng="pool")

                    def seq(i, g=g):
                        P.tag = "ssd_seq"
                        q = i % 3
                        ib = indb[q]
                        q3 = i % 2
                        yis = yis2[q3]; yv = yis2[q3]; th = th2[q3]; t1 = th2[q3]; ysn = ysn2[q3]; seqb = seqb2[q3]
                        psy, pby = PS()
                        psi, pbi = PS()
                        psz, pbz = PS()
                        for k in range(8):
                            mm(psz[:], uT[:, k, i * 128:(i + 1) * 128], wz[:, k, :], k == 0, k == 7, [uTb[i], wb2], [pbz], inc=(k in (3, 7)))
                        for r in range(8):
                            for j in range(2):
                                sl = slice(64 * j, 64 * j + 64)
                                mm(psy[sl, r * 64:(r + 1) * 64], wpr[q][sl, r, :], xdt[q][sl, r * 64:(r + 1) * 64], True, False,
                                   [ib], [pby], inc=False)
                                mm(psy[sl, r * 64:(r + 1) * 64], dI[sl, g * 8 + r, :], x_tok[sl, i, r * 64:(r + 1) * 64], False, True,
                                   [cb, x_tokb[i]], [pby], inc=(r == 7 and j == 1))
                        for j in range(2):
                            sl = slice(64 * j, 64 * j + 64)
                            mm(psi[sl, :], CT[:, i * 128 + 64 * j:i * 128 + 64 * j + 64], hbf[:, g, :], True, True, [CTb, hbfb[g]], [pbi])
                            pss, pbs = PS("s2")
                            mm(pss[:], B_tok[sl, i, :], xw[q][sl, :], True, True, [B_tokb[i], ib], [pbs])
                            tt(htmp[:].rearrange("p (a b) -> p a b", a=8), hst[:, g, :].rearrange("p (a b) -> p a b", a=8),
                               sm[:, i, 2 + j, g * 8:(g + 1) * 8].unsqueeze(2).broadcast_to([128, 8, 64]), ALU.mult,
                               [hgb[g], smb[i]], [htb])
                            tt(hst[:, g, :], htmp[:], pss[:], ALU.add, [htb, pbs], [hgb[g]])
                            cp(hbf[:, g, :], hst[:, g, :], [hgb[g]], [hbfb[g]], eng="act")
                        tt(yis[:].rearrange("p (a b) -> p a b", a=8), psi[:].rearrange("p (a b) -> p a b", a=8),
                           sm[:, i, 0, g * 8:(g + 1) * 8].unsqueeze(2).broadcast_to([128, 8, 64]), ALU.mult, [pbi, smb[i]], [seqb])
                        tt(yv[:], psy[:], yis[:], ALU.add, [pby, seqb], [seqb])
                        if blk == 0 and g == 0 and i == 0:
                            dump("y_pre00", yv[:], [seqb], [128, 512])
                        act(th[:], psz[:], AF.Tanh, [pbz], [seqb], scale=0.5)
                        stt(t1[:], th[:], 1.0, psz[:], ALU.add, ALU.mult, [seqb, pbz], [seqb])
                        tt(t1[:], t1[:], yv[:], ALU.mult, [seqb], [seqb])
                        st, stbuf = STAT()
                        jk, jkb = JUNK()
                        act(jk[:, 0:512], t1[:], AF.Square, [seqb], [stbuf, jkb], accum=st[:, 0:1])
                        rstd_from_ss(st, stbuf, 512.0, 4.0 * EPS)
                        stt(ysn[:], t1[:], st[:, 1:2], nws[:], ALU.mult, ALU.mult, [seqb, stbuf, nwsb], [seqb])
                        pst, pbt = PS("t")
                        pbf = pst.bitcast(BF16)
                        for c in range(4):
                            tr(pbf[:, c * 128:(c + 1) * 128], ysn[:, c * 128:(c + 1) * 128], identb[:], [seqb, cb], [pbt], inc=(c == 3))
                        cp(big[:, g * 4:(g + 1) * 4, i * 128:(i + 1) * 128], pbf[:, 0:512].rearrange("p (c t) -> p c t", c=4),
                           [pbt], [bigb[i]], eng="act")

                    for i in range(NT + 1):
                        if i < NT:
                            indep(i)
                        if i > 0:
                            seq(i - 1)

                ssd_pre(0)
                for g in range(4):
                    if g < 3:
                        ssd_pre(g + 1)
                    ssd_core(g)
                if blk == 0:
                    dump("yT", big[:], bigb, [128, 16, TB], BF16)
            if stop_after == "SSD":
                break

            set_rings(LAY_DEFAULT)
            phA = Phase(); phA.__enter__()
            mixed = sb("mixed", [128, NT, 1024], BF16)
            gtok = sb("gtok", [128, NT, 5, 4]); gtokb = Buf("gtok")
            aold_bc = sb("aold_bc", [128, 4, 16]); aoldb = Buf("aold")
            P.tag = "post_ssd"
            with Phase():
                P.tag = "ml_gates"
                rmask = sb("rmask", [4, TB]); negbig = sb("negbig", [4, TB])
                G_l1 = sb("G_l1", [4, TB]); G_cs = sb("G_cs", [4, TB]); G_e = sb("G_e", [4, TB])
                G_Ml = sb("G_Ml", [4, TB]); G_M = sb("G_M", [4, TB]); G_t = sb("G_t", [4, TB])
                gs = sb("gs", [4, 8, 16])
                gb = Buf("gates")
                memset(rmask[:], 1.0, [gb])
                memset(rmask[:].rearrange("p (c t) -> p c t", t=64)[:, :, 0:1], 0.0, [gb])
                memset(negbig[:], 0.0, [gb])
                memset(negbig[:].rearrange("p (c t) -> p c t", t=64)[:, :, 0:1], -1e30, [gb])
                for tb in range(TB // 512):
                    ps, pb = PS(); ps2, pb2 = PS()
                    for k in range(8):
                        mm(ps[0:4, :], wsm[:, k, 32:36], uT[:, k, tb * 512:(tb + 1) * 512], k == 0, k == 7,
                           [wsmb] + uTb[tb * 4:(tb + 1) * 4], [pb], inc=(k == 7))
                    for k in range(8):
                        mm(ps2[0:4, :], wsm[:, k, 36:40], uT[:, k, tb * 512:(tb + 1) * 512], k == 0, k == 7,
                           [wsmb] + uTb[tb * 4:(tb + 1) * 4], [pb2], inc=(k == 7))
                    act(G_e[:, tb * 512:(tb + 1) * 512], ps[0:4, :], AF.Identity, [pb, cb], [gb], bias=ib4[:, 0:1])
                    act(G_l1[:, tb * 512:(tb + 1) * 512], ps2[0:4, :], AF.Exp, [pb2, cb], [gb], bias=negfb[:, 0:1], scale=-1.0)
                act(G_l1[:], G_l1[:], AF.Ln, [gb], [gb], bias=1.0)
                P.op("dve", lambda e: e.tensor_tensor_scan(G_cs[:], rmask[:], G_l1[:], 0.0, ALU.mult, ALU.add), [gb], [gb], cost=2.3)
                tt(G_e[:], G_e[:], G_cs[:], ALU.add, [gb], [gb])
                P.op("dve", lambda e: e.tensor_tensor_scan(G_Ml[:], negbig[:], G_e[:], 0.0, ALU.add, ALU.max), [gb], [gb], cost=2.3)

                def v3(t):
                    return t[:].rearrange("p (c t) -> p c t", t=64)
                csend = v3(G_cs)[:, :, 63]; emax = v3(G_Ml)[:, :, 63]
                mloc = gs[:, 0, :]; bend = gs[:, 1, :]; maft = gs[:, 2, :]; mprev = gs[:, 3, :]; dd = gs[:, 4, :]; ao = gs[:, 5, :]
                tt(mloc, emax, csend, ALU.subtract, [gb], [gb])
                ts(bend, csend, -1.0, None, ALU.mult, reads=[gb], writes=[gb])
                P.op("dve", lambda e: e.tensor_tensor_scan(maft, bend, mloc, mcar[:, 0:1], ALU.add, ALU.max), [gb, mcarb], [gb])
                cp(mprev[:, 0:1], mcar[:, 0:1], [mcarb, gb], [gb])
                cp(mprev[:, 1:16], maft[:, 0:15], [gb], [gb])
                cp(mcar[:, 0:1], maft[:, 15:16], [gb], [mcarb])
                tt(v3(G_M), v3(G_Ml), mprev.unsqueeze(2).broadcast_to([4, 16, 64]), ALU.max, [gb], [gb])
                psT, pbT = PS()

                def to_tok(src, qi):
                    for i in range(NT):
                        c0 = (i * 5 + qi) * 4
                        mm(psT[:, c0:c0 + 4], src[0:4, i * 128:(i + 1) * 128], identf[0:4, 0:4], True, True, [gb, cb], [pbT],
                           inc=(i == NT - 1))
                ts(G_t[:], G_e[:], LNSCALE, None, ALU.add, reads=[gb], writes=[gb])
                to_tok(G_t, 0)
                to_tok(G_M, 1)
                tt(v3(G_t), mprev.unsqueeze(2).broadcast_to([4, 16, 64]), v3(G_M), ALU.subtract, [gb, pbT], [gb])
                act(G_t[:], G_t[:], AF.Exp, [gb], [gb], bias=LNSCALE)
                to_tok(G_t, 2)
                tt(G_t[:], G_cs[:], G_M[:], ALU.subtract, [gb, pbT], [gb])
                act(G_t[:], G_t[:], AF.Exp, [gb], [gb])
                to_tok(G_t, 3)
                tt(dd, bend, maft, ALU.subtract, [gb], [gb])
                tt(v3(G_t), v3(G_e), dd.unsqueeze(2).broadcast_to([4, 16, 64]), ALU.add, [gb, pbT], [gb])
                act(G_t[:], G_t[:], AF.Exp, [gb], [gb])
                to_tok(G_t, 4)
                cp(gtok[:].rearrange("p a b c -> p (a b c)"), psT[:, 0:NT * 20], [pbT], [gtokb])
                tt(ao, dd, mprev, ALU.add, [gb], [gb])
                act(ao, ao, AF.Exp, [gb], [gb])
                Rx = sb("Rx", [4, 4, 16])
                tt(Rx[:], ao.unsqueeze(1).broadcast_to([4, 4, 16]), I4x[:, :, 0:16], ALU.mult, [gb, cb], [gb])
                psA, pbA = PS()
                mm(psA[:, 0:64], ones4[:], Rx[:].rearrange("p a b -> p (a b)"), True, True, [gb, cb], [pbA])
                cp(aold_bc[:].rearrange("p a b -> p (a b)"), psA[:, 0:64], [pbA], [aoldb])


                P.tag = "post_ssd"
                sgt = sb("sgt", [128, NT, 1024], BF16); sgtb = [Buf("sgt%d" % i) for i in range(NT)]
                (wg,), wbg = next_slab()
                for i in range(NT):
                    for cbk in range(2):
                        psg, pbg = PS()
                        for k in range(8):
                            mm(psg[:], uT[:, k, i * 128:(i + 1) * 128], wg[:, k, cbk * 512:(cbk + 1) * 512], k == 0, k == 7,
                               [uTb[i], wbg], [pbg], inc=(k == 7))
                        act(sgt[:, i, cbk * 512:(cbk + 1) * 512], psg[:], AF.Sigmoid, [pbg], [sgtb[i]])
                for cbk in range(2):
                    (wbr,), wbb = next_slab()
                    for i in range(NT):
                        psr, pbr = PS()
                        for k in range(16):
                            mm(psr[:], big[:, k, i * 128:(i + 1) * 128], wbr[:, k, :], k == 0, k == 15, [bigb[i], wbb], [pbr], inc=(k == 15))
                        tt(mixed[:, i, cbk * 512:(cbk + 1) * 512], sgt[:, i, cbk * 512:(cbk + 1) * 512], psr[:], ALU.mult,
                           [sgtb[i], pbr], [mixb[i]])

            P.tag = "ml_gates"
            with Phase():
                with Phase():
                    set_rings(LAY_ML)
                    alloc_conv_bufs()
                    qT2 = [sb("qT", [128, 2, TB], BF16) for _ in range(2)]; qTb2 = [Buf("qT") for _ in range(2)]
                    kT2 = [sb("kT", [128, 2, TB], BF16) for _ in range(2)]; kTb2 = [Buf("kT") for _ in range(2)]
                    k_tok2 = [sb("k_tok", [128, NT, 256], BF16) for _ in range(2)]
                    k_tokb2 = [[Buf("ktok%d" % i) for i in range(NT)] for _ in range(2)]
                    vext2 = [sb("vext", [128, NT, 258], BF16) for _ in range(2)]
                    vextb2 = [[Buf("vext%d" % i) for i in range(NT)] for _ in range(2)]
                    osig2 = [sb("osig", [128, NT, 256], BF16) for _ in range(2)]
                    osigb2 = [[Buf("osig%d" % i) for i in range(NT)] for _ in range(2)]
                    nwm2 = [sb("nwm", [128, 256]) for _ in range(2)]; nwmb2 = [Buf("nwm") for _ in range(2)]
                    Md = [sb("Md%d" % i, [128, 64]) for i in range(2)]
                    Dm = [sb("Dm%d" % i, [128, 64]) for i in range(2)]
                    Sx = [sb("Sx%d" % i, [128, 64], BF16) for i in range(2)]
                    vw = [sb("vw%d" % i, [128, 258], BF16) for i in range(2)]
                    mib = [Buf("mind%d" % i) for i in range(2)]
                    num2 = [sb("num", [128, 258]) for _ in range(2)]
                    hb2 = [sb("hb", [128, 256], BF16) for _ in range(2)]
                    msb2 = [Buf("mseq%d" % i) for i in range(2)]
                    for par_ in range(2):
                        vinit = Buf("vinit")
                        memset(vext2[par_][:, :, 256:257], 1.0, [vinit]); memset(vext2[par_][:, :, 257:258], 0.0, [vinit])
                        for b_ in vextb2[par_]:
                            b_.w = vinit.w

                    def sel(h):
                        p_ = h % 2
                        return (qT2[p_], qTb2[p_], kT2[p_], kTb2[p_], k_tok2[p_], k_tokb2[p_], vext2[p_], vextb2[p_],
                                osig2[p_], osigb2[p_], nwm2[p_], nwmb2[p_])

                    def ml_pre(h):
                        qT, qTb, kT, kTb, k_tok, k_tokb, vext, vextb, osig, osigb, nwm, nwmb = sel(h)
                        (wq, wk, wv, wo), wbh = next_slab()
                        wvo = slab_t[(slab_cur[0] - 1) % NSLAB][:, 4096:8192].rearrange("p (s k n) -> p s k n", s=2, k=8)
                        P.tag = "ml_conv"
                        P.dma("sp", lambda e, h=h: e.dma_start(out=nwm[:], in_=bc_row(vecs["mlstm_norm_w"], h * 256, 256)), writes=[nwmb])
                        for half in range(2):
                            fm_conv(wq, wbh, half * 128, 24 + h * 2 + half, qT[:, half, :], qTb)
                        for half in range(2):
                            fm_conv(wk, wbh, half * 128, 32 + h * 2 + half, kT[:, half, :], kTb)
                        for half in range(2):
                            transpose_blocks(kT[:, half, :], kTb, 1, k_tok[:, :, half * 128:(half + 1) * 128], k_tokb)
                        for i in range(NT):
                            ps, pb = PS("conv")
                            for k in range(8):
                                mm(ps[:, 0:512].rearrange("p (s n) -> p s n", s=2), uT[:, k, i * 128:(i + 1) * 128], wvo[:, :, k, :], k == 0, k == 7,
                                   [uTb[i], wbh], [pb], inc=(k in (3, 7)))
                            cp(vext[:, i, 0:256], ps[:, 0:256], [pb], [vextb[i]], eng="act")
                            act(osig[:, i, :], ps[:, 256:512], AF.Sigmoid, [pb], [osigb[i]])

                    def ml_core(h):
                        qT, qTb, kT, kTb, k_tok, k_tokb, vext, vextb, osig, osigb, nwm, nwmb = sel(h)

                        def mindep(i, h=h):
                            P.tag = "ml_indep"
                            q = i % 2
                            ib = mib[q]
                            ts(Md[q][:], Idm[:], gtok[:, i, 1, h:h + 1], None, ALU.mult, reads=[gtokb, cb], writes=[ib])
                            psM, pbM = PS("m64")
                            mm(psM[:, 0:64], Bd[:], Md[q][:], True, True, [ib, cb], [pbM])
                            act(Dm[q][:], psM[:, 0:64], AF.Exp, [pbM, gtokb], [ib], bias=gtok[:, i, 0, h:h + 1], scale=-1.0)
                            tt(Dm[q][:], Dm[q][:], tri[:], ALU.mult, [ib, cb], [ib])
                            psq, pbq = PS("q")
                            for half in range(2):
                                mm(psq[:, 0:128], kT[:, half, i * 128:(i + 1) * 128], qT[:, half, i * 128:(i + 1) * 128], half == 0, half == 1,
                                   [kTb, qTb], [pbq], inc=(half == 1))
                            tt(Sx[q][0:64, :], psq[0:64, 0:64], Dm[q][0:64, :], ALU.mult, [pbq, ib], [ib])
                            tt(Sx[q][64:128, :], psq[64:128, 64:128], Dm[q][64:128, :], ALU.mult, [pbq, ib], [ib])
                            ts(vw[q][:], vext[:, i, :], gtok[:, i, 4, h:h + 1], None, ALU.mult, reads=[vextb[i], gtokb], writes=[ib])

                        def mseq(i, h=h):
                            P.tag = "ml_seq"
                            q = i % 2
                            ib = mib[q]
                            tmpi = num2[q]; num = num2[q]; hn = num2[q]; hb = hb2[q]; msb = msb2[q]
                            psn, pbn = PS(); psi, pbi = PS()
                            for j in range(2):
                                sl = slice(64 * j, 64 * j + 64)
                                c = 2 * i + j
                                mm(psn[sl, 0:258], Sx[q][sl, :], vext[sl, i, :], True, True, [ib, vextb[i]], [pbn])
                                for half in range(2):
                                    mm(psi[sl, 0:258], qT[:, half, i * 128 + 64 * j:i * 128 + 64 * j + 64], Cbf[:, h, half, :], half == 0, half == 1,
                                       [qTb, Cbfb[h]], [pbi], inc=(half == 1))
                                psc, pbc = PS("c")
                                pcn, pbcn = PS("n")
                                for half in range(2):
                                    mm(psc[:, half * 256:(half + 1) * 256], k_tok[sl, i, half * 128:(half + 1) * 128], vw[q][sl, 0:256], True, True,
                                       [k_tokb[i], ib], [pbc], inc=(half == 1))
                                for half in range(2):
                                    mm(pcn[:, half * 2:half * 2 + 2], k_tok[sl, i, half * 128:(half + 1) * 128], vw[q][sl, 256:258], True, True,
                                       [k_tokb[i], ib], [pbcn], inc=(half == 1))
                                stt(Cst[:, h, :, 0:256], Cst[:, h, :, 0:256], aold_bc[:, h, c:c + 1], psc[:, :].rearrange("p (a b) -> p a b", a=2),
                                    ALU.mult, ALU.add, [Chb[h], aoldb, pbc], [Chb[h]])
                                stt(Cst[:, h, :, 256:258], Cst[:, h, :, 256:258], aold_bc[:, h, c:c + 1],
                                    pcn[:, 0:4].rearrange("p (a b) -> p a b", a=2), ALU.mult, ALU.add, [Chb[h], aoldb, pbcn], [Chb[h]])
                                cp(Cbf[:, h, :, :], Cst[:, h, :, :], [Chb[h]], [Cbfb[h]], eng="act")
                            act(tmpi[:], psi[:, 0:258], AF.Identity, [pbi, gtokb], [msb], scale=gtok[:, i, 2, h:h + 1])
                            tt(num[:], tmpi[:], psn[:, 0:258], ALU.add, [msb, pbn], [msb])
                            st, stbuf = STAT()
                            ts(st[:, 2:3], num[:, 256:257], -1.0, None, ALU.mult, reads=[msb], writes=[stbuf])
                            tt(st[:, 2:3], st[:, 2:3], num[:, 256:257], ALU.max, [msb, stbuf], [stbuf])
                            tt(st[:, 2:3], st[:, 2:3], gtok[:, i, 3, h:h + 1], ALU.max, [stbuf, gtokb], [stbuf])
                            P.op("dve", lambda e: e.reciprocal(st[:, 3:4], st[:, 2:3]), [stbuf], [stbuf])
                            jk, jkb = JUNK()
                            act(jk[:, 0:256], num[:, 0:256], AF.Square, [msb, stbuf], [stbuf, jkb], scale=st[:, 3:4], accum=st[:, 0:1])
                            rstd_from_ss(st, stbuf, 256.0, EPS)
                            tt(st[:, 2:3], st[:, 1:2], st[:, 3:4], ALU.mult, [stbuf], [stbuf])
                            stt(hn[:, 0:256], num[:, 0:256], st[:, 2:3], nwm[:], ALU.mult, ALU.mult, [msb, stbuf, nwmb], [msb])
                            tt(hb[:], hn[:, 0:256], osig[:, i, :], ALU.mult, [msb, osigb[i]], [msb])
                            pst, pbt = PS("t")
                            pbf = pst.bitcast(BF16)
                            for c2 in range(2):
                                tr(pbf[:, c2 * 128:(c2 + 1) * 128], hb[:, c2 * 128:(c2 + 1) * 128], identb[:], [msb, cb], [pbt], inc=(c2 == 1))
                            cp(big[:, h * 2:(h + 1) * 2, i * 128:(i + 1) * 128], pbf[:, 0:256].rearrange("p (c t) -> p c t", c=2),
                               [pbt], [bigb[i]], eng="act")

                        for i in range(NT + 1):
                            if i < NT:
                                mindep(i)
                            if i > 0:
                                mseq(i - 1)

                    ml_pre(0)
                    for h in range(4):
                        if h < 3:
                            ml_pre(h + 1)
                        ml_core(h)
                if blk == 0:
                    dump("hT", big[:, 0:8, :], bigb, [128, 8, TB], BF16)
            if stop_after == "ML":
                break

            set_rings(LAY_DEFAULT)
            P.tag = "post_ml"
            with Phase():
                h1 = sb("h1", [128, NT, 1024])
                alloc_norm_bufs()
                sgt = sb("sgt", [128, NT, 1024], BF16); sgtb = [Buf("sgt%d" % i) for i in range(NT)]
                tmpm = [sb("tmpm%d" % i, [128, 512]) for i in range(2)]; sgb = [Buf("tmpm%d" % i) for i in range(2)]
                (wg2,), wbg = next_slab()
                for i in range(NT):
                    for cbk in range(2):
                        psg, pbg = PS()
                        for k in range(8):
                            mm(psg[:], uT[:, k, i * 128:(i + 1) * 128], wg2[:, k, cbk * 512:(cbk + 1) * 512], k == 0, k == 7,
                               [uTb[i], wbg], [pbg], inc=(k == 7))
                        act(sgt[:, i, cbk * 512:(cbk + 1) * 512], psg[:], AF.Sigmoid, [pbg], [sgtb[i]])
                (wbm,), wbb = next_slab()
                cnt = 0
                for i in range(NT):
                    for cbk in range(2):
                        psr, pbr = PS()
                        for k in range(8):
                            mm(psr[:], big[:, k, i * 128:(i + 1) * 128], wbm[:, k, cbk * 512:(cbk + 1) * 512], k == 0, k == 7,
                               [bigb[i], wbb], [pbr], inc=(k == 7))
                        q = cnt % 2; cnt += 1
                        tt(tmpm[q][:], sgt[:, i, cbk * 512:(cbk + 1) * 512], psr[:], ALU.mult, [sgtb[i], pbr], [sgb[q]])
                        tt(mixed[:, i, cbk * 512:(cbk + 1) * 512], mixed[:, i, cbk * 512:(cbk + 1) * 512], tmpm[q][:], ALU.add,
                           [sgb[q], mixb[i]], [mixb[i]])
                if blk == 0:
                    dump("mixed", mixed[:], mixb, [128, NT, 1024], BF16)
                for i in range(NT):
                    ps, pb = PS()
                    pbf = ps.bitcast(BF16)
                    for k in range(8):
                        tr(pbf[:, k * 128:(k + 1) * 128], mixed[:, i, k * 128:(k + 1) * 128], identb[:], [mixb[i], cb], [pb], inc=(k == 7))
                    cp(big[:, 8:16, i * 128:(i + 1) * 128], pbf[:, 0:1024].rearrange("p (k t) -> p k t", k=8), [pb], [bigb[i]], eng="act")
                (wo_,), wbo = next_slab()
                for i in range(NT):
                    xt = M["xin"][i % 2]
                    src = x_d.ap()[t0 + i * 128:t0 + (i + 1) * 128, :]
                    P.dma("sp", lambda e, xt=xt, src=src: e.dma_start(out=xt[:], in_=src), writes=[xinb[i % 2]])
                    for cbk in range(2):
                        ps, pb = PS()
                        for k in range(8):
                            mm(ps[:], big[:, 8 + k, i * 128:(i + 1) * 128], wo_[:, k, cbk * 512:(cbk + 1) * 512], k == 0, k == 7,
                               [bigb[i], wbo], [pb], inc=(k == 7))
                        tt(h1[:, i, cbk * 512:(cbk + 1) * 512], ps[:], xt[:, cbk * 512:(cbk + 1) * 512], ALU.add, [pb, xinb[i % 2]], [h1b[i]])
                if blk == 0:
                    dump("h1", h1[:], h1b, [128, NT, 1024])
                P.tag = "mlp"
                load_nw("norm_mlp_w", 0, 1024)
                norm_to_uT(lambda i: h1[:, i, :], lambda i: h1b[i], t0)
                hidb2 = [[Buf("hid%d" % i) for i in range(TB // 512)] for _ in range(2)]
                rl = tmpm; rlb = sgb
                cnt = 0
                for qf in range(4):
                    (wu,), wbu = next_slab()
                    for e2 in range(2):
                        hidb = hidb2[e2]
                        for f4 in range(4):
                            fg = e2 * 4 + f4
                            for tb in range(TB // 512):
                                ps, pb = PS()
                                for k in range(8):
                                    mm(ps[:], wu[:, k, fg * 128:(fg + 1) * 128], uT[:, k, tb * 512:(tb + 1) * 512], k == 0, k == 7,
                                       [wbu] + uTb[tb * 4:(tb + 1) * 4], [pb], inc=(k == 7))
                                q = cnt % 2; cnt += 1
                                act(rl[q][:], ps[:], AF.Relu, [pb], [rlb[q]])
                                tt(sgt[:, fg, tb * 512:(tb + 1) * 512], rl[q][:], rl[q][:], ALU.mult, [rlb[q]], [hidb[tb]])
                    (wd,), wbd = next_slab()
                    for e2 in range(2):
                        hidb = hidb2[e2]
                        for i in range(NT):
                            for cbk in range(2):
                                ps, pb = PS()
                                for c4 in range(4):
                                    c = e2 * 4 + c4
                                    mm(ps[:], sgt[:, c, i * 128:(i + 1) * 128], wd[:, c, cbk * 512:(cbk + 1) * 512], c4 == 0, c4 == 3,
                                       [hidb[i // 4], wbd], [pb], inc=(c4 == 3))
                                tt(h1[:, i, cbk * 512:(cbk + 1) * 512], h1[:, i, cbk * 512:(cbk + 1) * 512], ps[:], ALU.add, [pb, h1b[i]], [h1b[i]])
                if blk == 0:
                    dump("h2", h1[:], h1b, [128, NT, 1024])
                load_nw("norm_final_w", 0, 1024)
                for i in range(NT):
                    st, stbuf = STAT()
                    jk, jkb = JUNK()
                    act(jk[:], h1[:, i, :], AF.Square, [h1b[i]], [stbuf, jkb], accum=st[:, 0:1])
                    rstd_from_ss(st, stbuf, 1024.0, EPS)
                    ot = M["xin"][i % 2]
                    stt(ot[:], h1[:, i, :], st[:, 1:2], M["nw"][:], ALU.mult, ALU.mult, [h1b[i], stbuf, nwb], [xinb[i % 2]])
                    dst = y_d.ap()[t0 + i * 128:t0 + (i + 1) * 128, :]
                    tok = P.dma("sp", lambda e, ot=ot, dst=dst: e.dma_start(out=dst, in_=ot[:]), reads=[xinb[i % 2]])
                    outtoks.append(tok)
            phA.__exit__()

        P.wait_all("sp", dumptoks + outtoks)
        P.emit(sems)
    return nc, dump_t


def make_in_map(inputs, b):
    m = {"x": np.ascontiguousarray(np.asarray(inputs["x"])[b], dtype=np.float32)}
    for n in ("w_in", "w_br_ssd", "w_br_mlstm", "w_out", "w_up", "w_down", "conv_ssd_w", "conv_qk_w"):
        m[n] = np.ascontiguousarray(np.asarray(inputs[n])[0], dtype=np.float32)
    for n in ("conv_ssd_b", "conv_qk_b", "norm_mix_w", "norm_mlp_w", "ssd_norm_w", "mlstm_norm_w", "dt_bias",
              "a_log", "d_skip", "i_bias", "f_bias"):
        m[n] = np.ascontiguousarray(np.asarray(inputs[n]).reshape(1, -1), dtype=np.float32)
    m["norm_final_w"] = np.ascontiguousarray(np.asarray(inputs["norm_final_w"]).reshape(1, -1), dtype=np.float32)
    return m


def kernel(**inputs):
    nc, _ = build()
    in_maps = [make_in_map(inputs, b) for b in range(8)]
    res = run_bass_kernel_spmd(nc, in_maps, core_ids=list(range(8)))
    return np.stack([np.asarray(r["y"], dtype=np.float32) for r in res.results], axis=0)
```

```python
import contextlib
import math
import numpy as np
import concourse.bass as bass
import concourse.mybir as mybir
from concourse.bass_utils import run_bass_kernel_spmd

F32 = mybir.dt.float32
BF16 = mybir.dt.bfloat16
AF = mybir.ActivationFunctionType
ALU = mybir.AluOpType

ENGS = ["pe", "act", "dve", "pool", "sp"]
NDMASEM = 6
EPS = 1e-5

S = 2048
TB = 1024
NB = S // TB
NT = TB // 128
OFF_Z, OFF_X, OFF_B, OFF_C, OFF_DT = 0, 2048, 4096, 4608, 5120
OFF_Q, OFF_K, OFF_V, OFF_O, OFF_I, OFF_F, OFF_G = 5152, 6176, 7200, 8224, 9248, 9252, 9256
LNSCALE = math.log(256 ** -0.5)


class Buf:
    __slots__ = ("name", "w", "r")

    def __init__(self, name):
        self.name = name
        self.w = None
        self.r = []


class Unit:
    __slots__ = ("eng", "fns", "deps", "cost", "kind", "tag", "idx", "start", "end", "count", "tok")

    def __init__(self, eng, kind, tag, idx):
        self.eng = eng
        self.fns = []
        self.deps = set()
        self.cost = 0.0
        self.kind = kind
        self.tag = tag
        self.idx = idx
        self.start = None
        self.end = None
        self.count = None
        self.tok = None


XLAT = 0.45
WINDOW = 600
LOOKAHEAD = 0.3


class Prog:
    def __init__(self, nc):
        self.nc = nc
        self.units = []
        self.open = {e: None for e in ENGS}
        self.last_barrier = {e: None for e in ENGS}
        self.since_barrier = []
        self.semnames = list(ENGS) + ["d%s%d" % (e, i) for e in ("sp", "pool") for i in range(NDMASEM)]
        self.tag = ""
        self.annot = False
        self.schedule = True

    def _unit(self, eng, kind):
        u = self.open[eng]
        if u is None or kind != "op":
            u = Unit(eng, kind, self.tag, len(self.units))
            self.units.append(u)
            if self.last_barrier[eng] is not None:
                u.deps.add(self.last_barrier[eng])
            self.since_barrier.append(u.idx)
        return u

    def _deps(self, u, reads, writes):
        for b in reads:
            if b.w is not None:
                u.deps.add(b.w)
        for b in writes:
            if b.w is not None:
                u.deps.add(b.w)
            u.deps.update(b.r)
        u.deps.discard(u.idx)
        for b in reads:
            if not b.r or b.r[-1] != u.idx:
                b.r.append(u.idx)
        for b in writes:
            b.w = u.idx
            b.r = []

    def op(self, eng, fn, reads=(), writes=(), inc=True, cost=0.3):
        u = self._unit(eng, "op")
        u.fns.append(fn)
        u.cost += cost
        self._deps(u, reads, writes)
        self.open[eng] = None if inc else u
        return u.idx

    def dma(self, eng, fn, reads=(), writes=(), cost=3.0):
        assert self.open[eng] is None
        u = self._unit(eng, "dma")
        u.fns.append(fn)
        u.cost = cost
        self._deps(u, reads, writes)
        return u.idx

    def barrier(self):
        prev = list(self.since_barrier)
        self.since_barrier = []
        news = {}
        for e in ENGS:
            assert self.open[e] is None
            if e == "pe":
                continue
            u = Unit(e, "bar", "", len(self.units))
            self.units.append(u)
            u.deps.update(prev)
            if self.last_barrier[e] is not None:
                u.deps.add(self.last_barrier[e])
            news[e] = u.idx
        news["pe"] = None
        self.last_barrier = news
        self.since_barrier = [v for v in news.values() if v is not None]

    def wait_all(self, eng, toks):
        u = Unit(eng, "bar", "", len(self.units))
        self.units.append(u)
        u.deps.update(toks)
        if self.last_barrier[eng] is not None:
            u.deps.add(self.last_barrier[eng])

    def _schedule(self):
        import heapq
        units = self.units
        byeng = {e: [u for u in units if u.eng == e] for e in ENGS}
        if not self.schedule:
            return byeng
        nun = len(units)
        succ = [[] for _ in range(nun)]
        ndep = [0] * nun
        for u in units:
            ndep[u.idx] = len(u.deps)
            for d in u.deps:
                succ[d].append(u.idx)
        blev = [0.0] * nun
        for u in reversed(units):
            m = 0.0
            for sidx in succ[u.idx]:
                su = units[sidx]
                t = blev[sidx] + (XLAT if su.eng != u.eng or u.kind == "dma" else 0.0)
                if t > m:
                    m = t
            blev[u.idx] = m + u.cost
        inwin = [False] * nun
        nxt = {e: 0 for e in ENGS}
        nin = {e: 0 for e in ENGS}
        blocked = {e: False for e in ENGS}
        hA = {e: [] for e in ENGS}
        hB = {e: [] for e in ENGS}
        etime = {e: 0.0 for e in ENGS}
        order = {e: [] for e in ENGS}

        def ready_time(u):
            r = 0.0
            for d in u.deps:
                du = units[d]
                t = du.end + ((XLAT - (0.15 if u.eng == "dve" else 0.0)) if du.eng != u.eng or du.kind == "dma" else 0.0)
                if t > r:
                    r = t
            return r

        def admit(e):
            lst = byeng[e]
            while nxt[e] < len(lst) and nin[e] < WINDOW and not blocked[e]:
                u = lst[nxt[e]]
                nxt[e] += 1
                nin[e] += 1
                inwin[u.idx] = True
                if u.kind == "bar":
                    blocked[e] = True
                if ndep[u.idx] == 0:
                    heapq.heappush(hA[e], (ready_time(u), u.idx))

        for e in ENGS:
            admit(e)
        remaining = nun
        while remaining:
            best = None
            for e in ENGS:
                A = hA[e]; B = hB[e]
                while A and A[0][0] <= etime[e]:
                    ii = heapq.heappop(A)[1]
                    heapq.heappush(B, (-blev[ii], ii))
                if B:
                    key = (etime[e], B[0][1])
                    if A and LOOKAHEAD > 0:
                        rA, iA = A[0]
                        ub = units[B[0][1]]
                        if blev[iA] > blev[ub.idx] + 1.0 and rA - etime[e] < min(ub.cost * 0.6, LOOKAHEAD):
                            key = (rA, iA)
                elif A:
                    key = (A[0][0], A[0][1])
                else:
                    continue
                if best is None or key < best[0]:
                    best = (key, e)
            assert best is not None, "scheduler stuck"
            (st, idx), e = best
            if hB[e] and hB[e][0][1] == idx:
                heapq.heappop(hB[e])
            else:
                heapq.heappop(hA[e])
            u = units[idx]
            u.start = st
            if u.kind == "dma":
                etime[e] = st + 0.15
                u.end = st + u.cost
            else:
                etime[e] = st + u.cost
                u.end = etime[e]
            order[e].append(u)
            nin[e] -= 1
            if u.kind == "bar":
                blocked[e] = False
            remaining -= 1
            for sidx in succ[idx]:
                ndep[sidx] -= 1
                if ndep[sidx] == 0 and inwin[sidx]:
                    su = units[sidx]
                    heapq.heappush(hA[su.eng], (ready_time(su), sidx))
            admit(e)
        self.makespan = max(u.end for u in units)
        return order

    def emit(self, sems):
        nc = self.nc
        units = self.units
        order = self._schedule()
        for e in ENGS:
            c = 0
            k = 0
            dval = {}
            for u in order[e]:
                if u.kind == "op":
                    c += 1
                    u.tok = (e, c)
                elif u.kind == "dma":
                    sem = "d%s%d" % (e, k)
                    k = (k + 1) % NDMASEM
                    prev = dval.get(sem, 0)
                    dval[sem] = prev + 16
                    u.tok = (sem, prev + 16)
                    u.count = (sem, prev)
                else:
                    u.tok = None
        K = {e: {} for e in ENGS}
        snap = {}
        queues = {e: [] for e in ENGS}
        ptr = {e: 0 for e in ENGS}
        processed = set()
        total = sum(len(order[e]) for e in ENGS)
        ndone = 0
        while ndone < total:
            progressed = False
            for e in ENGS:
                lst = order[e]
                while ptr[e] < len(lst):
                    u = lst[ptr[e]]
                    if any(d not in processed for d in u.deps):
                        break
                    known = K[e]
                    need = {}
                    srcs = []
                    for d in u.deps:
                        du = units[d]
                        if du.tok is None:
                            continue
                        sname, v = du.tok
                        if du.eng == e and du.kind == "op":
                            if e in ("pe", "sp") and u.kind != "dma":
                                continue
                        if known.get(sname, 0) < v:
                            if need.get(sname, 0) < v:
                                need[sname] = v
                            srcs.append(d)
                    if u.kind == "dma" and u.count[1] > 0:
                        sname, v = u.count
                        if known.get(sname, 0) < v and need.get(sname, 0) < v:
                            need[sname] = v
                    for d in srcs:
                        for sname, v in snap[d].items():
                            if known.get(sname, 0) < v:
                                known[sname] = v
                    final = []
                    for sname, v in need.items():
                        implied = False
                        for d in srcs:
                            du = units[d]
                            if du.tok[0] != sname and snap[d].get(sname, 0) >= v:
                                implied = True
                                break
                        if not implied:
                            final.append((sname, v))
                        if known.get(sname, 0) < v:
                            known[sname] = v
                    queues[e].append((final, u))
                    sn = dict(known)
                    if u.tok is not None:
                        if sn.get(u.tok[0], 0) < u.tok[1]:
                            sn[u.tok[0]] = u.tok[1]
                    snap[u.idx] = sn
                    processed.add(u.idx)
                    ptr[e] += 1
                    ndone += 1
                    progressed = True
            assert progressed, "wait derivation stuck"
        self._check(queues)
        self.queues = queues

        def replay(eobj, name):
            own = sems[name]
            for waits, u in queues[name]:
                for s, v in waits:
                    eobj.wait_ge(sems[s], v)
                n = len(u.fns)
                for j, fn in enumerate(u.fns):
                    ins = fn(eobj)
                    if self.annot:
                        ins.annotate(u.tag)
                    if j == n - 1:
                        if u.kind == "op":
                            ins.then_inc(own, 1)
                        else:
                            ins.then_inc(sems[u.tok[0]], 16)

        with nc.Block() as block:
            @block.tensor
            def _(e):
                replay(e, "pe")

            @block.scalar
            def _(e):
                replay(e, "act")

            @block.vector
            def _(e):
                replay(e, "dve")

            @block.gpsimd
            def _(e):
                replay(e, "pool")

            @block.sync
            def _(e):
                replay(e, "sp")

    def _check(self, queues):
        sem = {}
        ptr = {e: 0 for e in ENGS}
        total = sum(len(q) for q in queues.values())
        donecount = 0
        progress = True
        while progress:
            progress = False
            for e in ENGS:
                q = queues[e]
                while ptr[e] < len(q):
                    waits, u = q[ptr[e]]
                    if all(sem.get(s, 0) >= v for s, v in waits):
                        if u.tok is not None:
                            s, v = u.tok
                            sem[s] = sem.get(s, 0) + (1 if u.kind == "op" else 16)
                            assert sem[s] == v, (s, v, sem[s])
                        ptr[e] += 1
                        donecount += 1
                        progress = True
                    else:
                        break
        assert donecount == total, ("deadlock in emitted program", {e: (ptr[e], len(queues[e])) for e in ENGS})


def build(stop_after=None, dumps=(), annot=False):
    nc = bass.Bass("TRN2", target_bir_lowering=False)
    P = Prog(nc)
    P.annot = annot
    dumps = set(dumps)

    def din(name, shape):
        return nc.dram_tensor(name, shape, F32, kind="ExternalInput")

    x_d = din("x", [S, 1024])
    w_in = din("w_in", [1024, 11304])
    w_brs = din("w_br_ssd", [2048, 1024])
    w_brm = din("w_br_mlstm", [1024, 1024])
    w_out = din("w_out", [1024, 1024])
    w_up = din("w_up", [1024, 4096])
    w_dn = din("w_down", [4096, 1024])
    conv_ssd_w = din("conv_ssd_w", [4, 3072])
    conv_ssd_b = din("conv_ssd_b", [1, 3072])
    conv_qk_w = din("conv_qk_w", [4, 2048])
    conv_qk_b = din("conv_qk_b", [1, 2048])
    vecs = {n: din(n, [1, m]) for n, m in [("norm_mix_w", 1024), ("norm_mlp_w", 1024), ("norm_final_w", 1024),
                                           ("ssd_norm_w", 2048), ("mlstm_norm_w", 1024), ("dt_bias", 32),
                                           ("a_log", 32), ("d_skip", 32), ("i_bias", 4), ("f_bias", 4)]}
    y_d = nc.dram_tensor("y", [S, 1024], F32, kind="ExternalOutput")
    dump_t = {}

    es = contextlib.ExitStack()
    ARENA_BASE, ARENA_TOP = 16512, 229376
    arena = {"p": ARENA_BASE, "n": 0, "hi": 0}
    arena_cache = {}

    def sb(name, shape, dt=F32, stack=None):
        nbytes = (2 if dt == BF16 else 4)
        for d in shape[1:]:
            nbytes *= d
        nbytes = (nbytes + 63) // 64 * 64
        off = arena["p"]
        arena["p"] = off + nbytes
        arena["hi"] = max(arena["hi"], arena["p"])
        assert arena["p"] <= ARENA_TOP, ("SBUF overflow", name, arena["p"])
        key = (name, tuple(shape), str(dt), off)
        if key not in arena_cache:
            arena["n"] += 1
            arena_cache[key] = nc.alloc_sbuf_tensor_at("%s_%d" % (name, arena["n"]), list(shape), dt, offset=off)
        return arena_cache[key]

    class Phase:
        def __enter__(self):
            self.mark = arena["p"]
            return self

        def __exit__(self, *a):
            P.barrier()
            arena["p"] = self.mark
            return False

    with es:

        sems = {n: es.enter_context(nc.semaphore(n)) for n in P.semnames}
        psb = [es.enter_context(nc.psum_tensor("ps%d" % i, [128, 512], F32)) for i in range(8)]
        psbuf = [Buf("ps%d" % i) for i in range(8)]
        pscnt = [0]

        pscnt2 = [0]

        regbuf = {}
        rings = {}
        ringcnt = {}

        def set_rings(layout):
            rings.clear()
            for cls, regs in layout.items():
                rings[cls] = regs
                ringcnt.setdefault(cls, 0)

        def full(banks):
            return [(k, 0, 512) for k in banks]

        LAY_DEFAULT = {"conv": full([0, 1]), "core": full([2, 3, 4, 5, 6, 7])}
        LAY_SSD = {"conv": full([0, 1]), "core": full([2, 3, 4]), "s": full([5]), "s2": full([6]),
                   "q": full([7]), "t": full([7])}
        LAY_ML = {"conv": full([0, 1]), "core": full([2, 3, 4]), "c": full([5]), "n": full([6]),
                  "m64": full([7]), "q": full([7]), "t": full([7])}
        set_rings(LAY_DEFAULT)

        def PS(cls="core"):
            regs = rings[cls]
            r = regs[ringcnt[cls] % len(regs)]
            ringcnt[cls] += 1
            if r not in regbuf:
                regbuf[r] = Buf("ps%d_%d" % (r[0], r[1]))
            k, c0, w = r
            return psb[k][:, c0:c0 + w], regbuf[r]

        def fsz(ap):
            n = 1
            for d in ap.shape[1:]:
                n *= d
            return n

        def mm(out, lhsT, rhs, start=True, stop=True, reads=(), writes=(), inc=True):
            c = 0.03 + max(fsz(rhs), 64) / 2400.0 * (4.0 if rhs.dtype == F32 else 1.0)
            P.op("pe", lambda e: e.matmul(out, lhsT=lhsT, rhs=rhs, start=start, stop=stop), reads, writes, inc, cost=c)

        def tr(out, in_, ident, reads=(), writes=(), inc=True):
            P.op("pe", lambda e: e.transpose(out, in_, ident), reads, writes, inc, cost=0.1)

        def act(out, in_, func, reads=(), writes=(), bias=None, scale=None, accum=None):
            kw = {}
            if bias is not None:
                kw["bias"] = bias
            if scale is not None:
                kw["scale"] = scale
            if accum is not None:
                kw["accum_out"] = accum
            c = 0.2 + max(fsz(out), 64) / 1400.0 + (0.1 if accum is not None else 0.0)
            P.op("act", lambda e: e.activation(out, in_, func, **kw), reads, writes, cost=c)

        def tt(out, in0, in1, op, reads=(), writes=(), eng="dve"):
            c = (0.08 if eng == "dve" else 0.35) + max(fsz(out), 64) / 960.0
            if op == ALU.pow:
                c = 1.0
            P.op(eng, lambda e: e.tensor_tensor(out, in0, in1, op), reads, writes, cost=c)

        def ts(out, in0, s1, s2, op0, op1=None, reads=(), writes=(), eng="dve"):
            c = (0.08 if eng == "dve" else 0.35) + max(fsz(out), 64) / 960.0
            if op1 is None:
                P.op(eng, lambda e: e.tensor_scalar(out, in0, s1, None, op0), reads, writes, cost=c)
            else:
                P.op(eng, lambda e: e.tensor_scalar(out, in0, s1, s2, op0, op1), reads, writes, cost=c)

        def stt(out, in0, scalar, in1, op0, op1, reads=(), writes=()):
            c = 0.08 + max(fsz(out), 64) / 960.0
            P.op("dve", lambda e: e.scalar_tensor_tensor(out, in0, scalar, in1, op0, op1), reads, writes, cost=c)

        def cp(out, in_, reads=(), writes=(), eng="dve"):
            if eng == "act":
                act(out, in_, AF.Copy, reads, writes)
            else:
                c = (0.08 if eng == "dve" else 0.35) + max(fsz(out), 64) / 960.0
                P.op(eng, lambda e: e.tensor_copy(out, in_), reads, writes, cost=c)

        def memset(ap, val, writes=(), eng="pool"):
            P.op(eng, lambda e: e.memset(ap, val), (), writes, cost=0.35 + fsz(ap) / 960.0)

        def bc_row(handle, off, n):
            return bass.AP(handle, off, [[0, 128], [1, n]])

        def dump(name, ap, bufs, shape, dt=F32):
            if name not in dumps:
                return
            t = nc.dram_tensor("dbg_" + name, list(shape), dt, kind="ExternalOutput")
            dump_t[name] = t
            tok = P.dma("sp", lambda e: e.dma_start(out=t.ap(), in_=ap), reads=bufs)
            dumptoks.append(tok)

        dumptoks = []
        outtoks = []

        cb = Buf("const")
        identf = sb("identf", [128, 128]); identb = sb("identb", [128, 128], BF16)
        Lmask = sb("Lmask", [128, 128]); Umask = sb("Umask", [128, 128]); Bd = sb("Bd", [128, 128])
        sel0 = sb("sel0", [128, 128]); sel1 = sb("sel1", [128, 128])
        tri = sb("tri", [128, 64]); Idm = sb("Idm", [128, 64])
        mhalf = sb("mhalf", [128, 1])
        memset(identf[:], 0.0, [cb])
        P.op("pool", lambda e: e.affine_select(identf[:], identf[:], pattern=[[-1, 128]], compare_op=ALU.not_equal,
                                               fill=1.0, base=0, channel_multiplier=1), [cb], [cb])
        cp(identb[:], identf[:], [cb], [cb], eng="pool")
        memset(Lmask[:], 1.0, [cb])
        P.op("pool", lambda e: e.affine_select(Lmask[:], Lmask[:], pattern=[[1, 128]], compare_op=ALU.is_ge,
                                               fill=0.0, base=0, channel_multiplier=-1), [cb], [cb])
        memset(Lmask[0:64, 64:128], 0.0, [cb])
        memset(Umask[:], 1.0, [cb])
        P.op("pool", lambda e: e.affine_select(Umask[:], Umask[:], pattern=[[-1, 128]], compare_op=ALU.is_gt,
                                               fill=0.0, base=0, channel_multiplier=1), [cb], [cb])
        memset(Umask[64:128, 0:64], 0.0, [cb])
        memset(Bd[:], 0.0, [cb]); memset(Bd[0:64, 0:64], 1.0, [cb]); memset(Bd[64:128, 64:128], 1.0, [cb])
        memset(sel0[:], 0.0, [cb]); memset(sel0[0:64, :], 1.0, [cb])
        memset(sel1[:], 0.0, [cb]); memset(sel1[64:128, :], 1.0, [cb])
        cp(tri[0:64, :], Lmask[0:64, 0:64], [cb], [cb], eng="pool")
        cp(tri[64:128, :], Lmask[64:128, 64:128], [cb], [cb], eng="pool")
        cp(Idm[0:64, :], identf[0:64, 0:64], [cb], [cb], eng="pool")
        cp(Idm[64:128, :], identf[64:128, 64:128], [cb], [cb], eng="pool")
        memset(mhalf[:], -0.5, [cb])

        cw = sb("cw", [128, 40, 5])
        cwb = Buf("cw")
        with Phase():
            cwT = sb("cwT", [5, 5120])
            P.dma("sp", lambda e: e.dma_start(out=cwT[0:4, 0:3072], in_=conv_ssd_w.ap()), writes=[cwb])
            P.dma("sp", lambda e: e.dma_start(out=cwT[4:5, 0:3072], in_=conv_ssd_b.ap()), writes=[cwb])
            P.dma("sp", lambda e: e.dma_start(out=cwT[0:4, 3072:5120], in_=conv_qk_w.ap()), writes=[cwb])
            P.dma("sp", lambda e: e.dma_start(out=cwT[4:5, 3072:5120], in_=conv_qk_b.ap()), writes=[cwb])
            ps, pb = PS()
            for g in range(40):
                mm(ps[:, g * 5:(g + 1) * 5], cwT[0:5, g * 128:(g + 1) * 128], identf[0:5, 0:5],
                   reads=[cwb, cb], writes=[pb], inc=(g == 39))
            cp(cw[:].rearrange("p g f -> p (g f)"), ps[:, 0:200], [pb], [cwb])

        dtb_bc = sb("dtb_bc", [128, 32]); A_bc = sb("A_bc", [128, 32]); dsk_bc = sb("dsk_bc", [128, 32])
        fb4 = sb("fb4", [4, 1]); ib4 = sb("ib4", [4, 1]); negfb = sb("negfb", [4, 1])
        P.dma("sp", lambda e: e.dma_start(out=dtb_bc[:], in_=bc_row(vecs["dt_bias"], 0, 32)), writes=[cb])
        P.dma("sp", lambda e: e.dma_start(out=A_bc[:], in_=bc_row(vecs["a_log"], 0, 32)), writes=[cb])
        P.dma("sp", lambda e: e.dma_start(out=dsk_bc[:], in_=bc_row(vecs["d_skip"], 0, 32)), writes=[cb])
        P.dma("sp", lambda e: e.dma_start(out=fb4[:], in_=bass.AP(vecs["f_bias"], 0, [[1, 4], [1, 1]])), writes=[cb])
        P.dma("sp", lambda e: e.dma_start(out=ib4[:], in_=bass.AP(vecs["i_bias"], 0, [[1, 4], [1, 1]])), writes=[cb])
        act(A_bc[:], A_bc[:], AF.Exp, [cb], [cb])
        ts(A_bc[:], A_bc[:], -1.0, None, ALU.mult, reads=[cb], writes=[cb])
        ts(negfb[:], fb4[:], -1.0, None, ALU.mult, reads=[cb], writes=[cb])
        dI = sb("dI", [128, 32, 64], BF16)
        tt(dI[:], dsk_bc[:, :].unsqueeze(2).broadcast_to([128, 32, 64]),
           Idm[:, :].unsqueeze(1).broadcast_to([128, 32, 64]), ALU.mult, [cb], [cb])
        I4x = sb("I4x", [4, 4, 32])
        tt(I4x[:], identf[0:4, 0:4].unsqueeze(2).broadcast_to([4, 4, 32]),
           identf[0:4, 0:4].unsqueeze(2).broadcast_to([4, 4, 32]), ALU.mult, [cb], [cb])
        ones4 = sb("ones4", [4, 128])
        memset(ones4[:], 1.0, [cb])
        stb = Buf("state")
        hst = sb("hst", [128, 4, 512]); hbf = sb("hbf", [128, 4, 512], BF16)
        Cst = sb("Cst", [128, 4, 2, 258]); Cbf = sb("Cbf", [128, 4, 2, 258], BF16)
        rawc = sb("rawc", [128, 40, 3])
        mcar = sb("mcar", [4, 1])
        memset(hst[:], 0.0, [stb]); memset(hbf[:], 0.0, [stb])
        memset(Cst[:], 0.0, [stb]); memset(Cbf[:], 0.0, [stb])
        memset(rawc[:], 0.0, [stb]); memset(mcar[:], 0.0, [stb])
        hgb = [Buf("h%d" % g) for g in range(4)]
        hbfb = [Buf("hbf%d" % g) for g in range(4)]
        Chb = [Buf("C%d" % h) for h in range(4)]
        Cbfb = [Buf("Cbf%d" % h) for h in range(4)]
        rawcb = [Buf("rawc%d" % g) for g in range(40)]
        for b in hgb + hbfb + Chb + Cbfb + rawcb:
            b.w = stb.w
        mcarb = Buf("mcar"); mcarb.w = stb.w

        NST = 6
        sst = sb("sst", [128, NST, 4]); sstb = [Buf("sst%d" % i) for i in range(NST)]
        stc = [0]

        def STAT():
            k = stc[0] % NST
            stc[0] += 1
            return sst[:, k, :], sstb[k]

        junkt = [sb("junk%d" % i, [128, 1024], BF16) for i in range(2)]; junkbs = [Buf("junk%d" % i) for i in range(2)]
        jc = [0]

        def JUNK():
            k = jc[0] % 2
            jc[0] += 1
            return junkt[k], junkbs[k]

        def rstd_from_ss(st, stbuf, n, eps):
            ts(st[:, 1:2], st[:, 0:1], 1.0 / n, eps, ALU.mult, ALU.add, [stbuf], [stbuf], eng="pool")
            tt(st[:, 1:2], st[:, 1:2], mhalf[:, 0:1], ALU.pow, [stbuf, cb], [stbuf], eng="pool")

        NSLAB = 2
        slab_t = [sb("slab%d" % i, [128, 8192], BF16) for i in range(NSLAB)]
        slab_b = [Buf("slab%d" % i) for i in range(NSLAB)]

        def piece(handle, r0, k, c0, n):
            return (handle, r0, k, c0, n)

        sched = []
        for blk in range(NB):
            def s_xbc(g):
                return [piece(w_in, 0, 8, OFF_X + g * 512, 512), piece(w_in, 0, 8, OFF_B + g * 128, 128),
                        piece(w_in, 0, 8, OFF_C + g * 128, 128)]

            def s_z(g):
                return [piece(w_in, 0, 8, OFF_Z + g * 512, 512)]
            sched.extend([s_xbc(0), s_xbc(1), s_z(0), s_xbc(2), s_z(1), s_xbc(3), s_z(2), s_z(3)])
            sched.append([piece(w_in, 0, 8, OFF_G, 1024)])
            sched.append([piece(w_brs, 0, 16, 0, 512)])
            sched.append([piece(w_brs, 0, 16, 512, 512)])
            for h in range(4):
                sched.append([piece(w_in, 0, 8, OFF_Q + h * 256, 256), piece(w_in, 0, 8, OFF_K + h * 256, 256),
                              piece(w_in, 0, 8, OFF_V + h * 256, 256), piece(w_in, 0, 8, OFF_O + h * 256, 256)])
            sched.append([piece(w_in, 0, 8, OFF_G + 1024, 1024)])
            sched.append([piece(w_brm, 0, 8, 0, 1024)])
            sched.append([piece(w_out, 0, 8, 0, 1024)])
            for qf in range(4):
                sched.append([piece(w_up, 0, 8, qf * 1024, 1024)])
                sched.append([piece(w_dn, qf * 1024, 8, 0, 1024)])
        slab_issued = [0]
        slab_cur = [0]

        def issue_slab(i):
            t = slab_t[i % NSLAB]; b = slab_b[i % NSLAB]
            off = 0
            for (handle, r0, k, c0, n) in sched[i]:
                dst = t[:, off:off + k * n].rearrange("p (k n) -> p k n", k=k)
                src = handle.ap()[r0:r0 + k * 128, c0:c0 + n].rearrange("(k p) n -> p k n", p=128)
                P.dma("pool", lambda e, dst=dst, src=src: e.dma_start(out=dst, in_=src), writes=[b], cost=2.5 + k * n * 512.0 / 300e3)
                off += k * n

        def next_slab():
            i = slab_cur[0]
            slab_cur[0] += 1
            while slab_issued[0] < min(len(sched), i + NSLAB):
                issue_slab(slab_issued[0])
                slab_issued[0] += 1
            t = slab_t[i % NSLAB]; b = slab_b[i % NSLAB]
            views = []
            off = 0
            for (handle, r0, k, c0, n) in sched[i]:
                views.append(t[:, off:off + k * n].rearrange("p (k n) -> p k n", k=k))
                off += k * n
            return views, b

        wsm = sb("wsm", [128, 8, 40], BF16); wsmb = Buf("wsm")
        P.dma("pool", lambda e: e.dma_start(out=wsm[:, :, 0:32],
                                            in_=w_in.ap()[:, OFF_DT:OFF_DT + 32].rearrange("(k p) n -> p k n", p=128)),
              writes=[wsmb])
        P.dma("pool", lambda e: e.dma_start(out=wsm[:, :, 32:40],
                                            in_=w_in.ap()[:, OFF_I:OFF_I + 8].rearrange("(k p) n -> p k n", p=128)),
              writes=[wsmb])

        uT = sb("uT", [128, 8, TB], BF16); uTb = [Buf("uT%d" % i) for i in range(NT)]
        big = sb("big", [128, 16, TB], BF16)
        bigb = [Buf("big%d" % i) for i in range(NT)]
        mixb = [Buf("mix%d" % i) for i in range(NT)]
        h1b = [Buf("h1_%d" % i) for i in range(NT)]
        xinb = [Buf("xin%d" % i) for i in range(4)]
        ubfb = [Buf("ubf%d" % i) for i in range(4)]
        nwb = Buf("nw")
        rawb = [Buf("raw%d" % i) for i in range(2)]
        accb = [Buf("acc%d" % i) for i in range(2)]
        fmTb = [Buf("fmT%d" % i) for i in range(2)]
        M = {}

        def alloc_norm_bufs(nb=2):
            M["nb"] = nb
            M["xin"] = [sb("xin%d" % i, [128, 1024]) for i in range(nb)]
            M["ubf"] = [sb("ubf%d" % i, [128, 1024], BF16) for i in range(nb)]
            M["nw"] = sb("nw", [128, 1024])

        def alloc_conv_bufs():
            M["raw"] = [sb("raw%d" % i, [128, 4 + TB], BF16) for i in range(2)]
            M["dg"] = [sb("dg%d" % i, [128, 4, 128], BF16) for i in range(2)]
            M["fmT"] = [sb("fmT%d" % i, [128, TB], BF16) for i in range(2)]
        cvc = [0]

        def load_nw(name, off, n):
            nw = M["nw"]
            P.dma("sp", lambda e: e.dma_start(out=nw[:, 0:n], in_=bc_row(vecs[name], off, n)), writes=[nwb])

        def norm_to_uT(src_fn, srcb_fn, t0):
            for i in range(NT):
                src = src_fn(i); sbuf_ = srcb_fn(i)
                st, stbuf = STAT()
                jk, jkb = JUNK()
                act(jk[:], src, AF.Square, [sbuf_], [stbuf, jkb], accum=st[:, 0:1])
                rstd_from_ss(st, stbuf, 1024.0, EPS)
                u = M["ubf"][i % M["nb"]]; ub_ = ubfb[i % M["nb"]]
                stt(u[:], src, st[:, 1:2], M["nw"][:], ALU.mult, ALU.mult, [sbuf_, stbuf, nwb], [ub_])
                ps, pb = PS()
                pbf = ps.bitcast(BF16)
                for k in range(8):
                    tr(pbf[:, k * 128:(k + 1) * 128], u[:, k * 128:(k + 1) * 128], identb[:], [ub_, cb], [pb], inc=(k == 7))
                cp(uT[:, :, i * 128:(i + 1) * 128], pbf[:, 0:1024].rearrange("p (k t) -> p k t", k=8), [pb], [uTb[i]], eng="act")

        def fm_conv(wp, wbuf, col0, cwg, dst, dstb):
            ri = cvc[0] % 2
            cvc[0] += 1
            raw = M["raw"][ri]; rb = rawb[ri]; dg = M["dg"][ri]; db = accb[ri]
            tt(dg[:], identf[:, :].unsqueeze(1).broadcast_to([128, 4, 128]),
               cw[:, cwg, 0:4].unsqueeze(2).broadcast_to([128, 4, 128]), ALU.mult, [cb, cwb], [db])
            cp(raw[:, 0:3], rawc[:, cwg, :], [rawcb[cwg]], [rb])
            for tb in range(TB // 512):
                ps, pb = PS("conv")
                for k in range(8):
                    mm(ps[:], wp[:, k, col0:col0 + 128], uT[:, k, tb * 512:(tb + 1) * 512], k == 0, k == 7,
                       [wbuf] + uTb[tb * 4:(tb + 1) * 4], [pb], inc=(k in (3, 7)))
                cp(raw[:, 3 + tb * 512:3 + (tb + 1) * 512], ps[:], [pb], [rb], eng="act")
            cp(rawc[:, cwg, :], raw[:, TB:TB + 3], [rb], [rawcb[cwg]])
            for tb in range(TB // 512):
                ps, pb = PS("conv")
                for tap in range(4):
                    mm(ps[:], dg[:, tap, :], raw[:, tap + tb * 512:tap + tb * 512 + 512], tap == 0, tap == 3, [db, rb], [pb], inc=(tap in (1, 3)))
                act(dst[:, tb * 512:(tb + 1) * 512], ps[:], AF.Silu, [pb, cwb], [dstb], bias=cw[:, cwg, 4:5])

        def transpose_blocks(src, srcb, n128, dst3, dstbs, eng="act"):
            ps, pb = PS("conv")
            pbf = ps.bitcast(BF16)
            for j in range(NT):
                tr(pbf[:, j * 128:(j + 1) * 128], src[:, j * 128:(j + 1) * 128], identb[:], [srcb, cb], [pb], inc=(j == NT - 1))
            cp(dst3, pbf[:, 0:NT * 128].rearrange("p (a b) -> p a b", a=NT), [pb], dstbs, eng=eng)

        for blk in range(NB):
            t0 = blk * TB
            P.tag = "S0"
            ph0 = Phase(); ph0.__enter__()
            alloc_norm_bufs(4)
            load_nw("norm_mix_w", 0, 1024)

            def xsrc(i, t0=t0):
                xt = M["xin"][i % 4]
                src = x_d.ap()[t0 + i * 128:t0 + (i + 1) * 128, :]
                P.dma("sp", lambda e: e.dma_start(out=xt[:], in_=src), writes=[xinb[i % 4]])
                return xt[:]
            norm_to_uT(xsrc, lambda i: xinb[i % 4], t0)
            if blk == 0:
                dump("uT", uT[:], uTb, [128, 8, TB], BF16)
            ph0.__exit__()
            if stop_after == "S0":
                break

            P.tag = "ssd_pre"
            with Phase():
                psb_ = sb
                set_rings(LAY_SSD)
                alloc_conv_bufs()
                dt_tok = psb_("dt_tok", [128, NT, 32]); a_tok = psb_("a_tok", [128, NT, 32])
                sm = psb_("sm", [128, NT, 4, 32]); dte = psb_("dte", [128, NT, 32])
                smb = [Buf("sm%d" % i) for i in range(NT)]
                for i in range(NT):
                    ps, pb = PS()
                    for k in range(8):
                        mm(ps[:, 0:32], uT[:, k, i * 128:(i + 1) * 128], wsm[:, k, 0:32], k == 0, k == 7,
                           [uTb[i], wsmb], [pb], inc=(k == 7))
                    tt(dt_tok[:, i, :], ps[:, 0:32], dtb_bc[:], ALU.add, [pb, cb], [smb[i]])
                    act(dt_tok[:, i, :], dt_tok[:, i, :], AF.Exp, [smb[i]], [smb[i]])
                for i in range(NT):
                    act(dt_tok[:, i, :], dt_tok[:, i, :], AF.Ln, [smb[i]], [smb[i]], bias=1.0)
                    tt(a_tok[:, i, :], dt_tok[:, i, :], A_bc[:], ALU.mult, [smb[i], cb], [smb[i]])
                for i in range(NT):
                    ps, pb = PS()
                    for j, msk in enumerate([Lmask, Umask, sel0, sel1]):
                        mm(ps[:, j * 32:(j + 1) * 32], msk[:], a_tok[:, i, :], True, True, [smb[i], cb], [pb], inc=(j == 3))
                    act(sm[:, i, :, :].rearrange("p a b -> p (a b)"), ps[:, 0:128], AF.Exp, [pb], [smb[i]])
                    tt(dte[:, i, :], dt_tok[:, i, :], sm[:, i, 1, :], ALU.mult, [smb[i]], [smb[i]])
                if blk == 0:
                    dump("dt", dt_tok[:], smb, [128, NT, 32])

                x_tok2 = [psb_("x_tok", [128, NT, 512], BF16) for _ in range(2)]
                x_tokb2 = [[Buf("xtok%d" % i) for i in range(NT)] for _ in range(2)]
                BT2 = [psb_("BT", [128, TB], BF16) for _ in range(2)]; BTb2 = [Buf("BT") for _ in range(2)]
                CT2 = [psb_("CT", [128, TB], BF16) for _ in range(2)]; CTb2 = [Buf("CT") for _ in range(2)]
                B_tok2 = [psb_("B_tok", [128, NT, 128], BF16) for _ in range(2)]
                B_tokb2 = [[Buf("Btok%d" % i) for i in range(NT)] for _ in range(2)]
                rseg = [psb_("rseg%d" % i, [128, 8, 64]) for i in range(3)]
                esg = [psb_("esg%d" % i, [128, 8, 64]) for i in range(3)]
                cbm = [psb_("cbm%d" % i, [128, 64]) for i in range(3)]
                wpr = [psb_("wpr%d" % i, [128, 8, 64], BF16) for i in range(3)]
                xdt = [psb_("xdt%d" % i, [128, 512], BF16) for i in range(3)]
                xw = [psb_("xw%d" % i, [128, 512], BF16) for i in range(3)]
                indb = [Buf("ind%d" % i) for i in range(3)]
                yis2 = [psb_("yis", [128, 512]) for _ in range(2)]; th2 = [psb_("th", [128, 512]) for _ in range(2)]
                ysn2 = [psb_("ysn", [128, 512], BF16) for _ in range(2)]
                seqb2 = [Buf("seq%d" % i) for i in range(2)]
                htmp = psb_("htmp", [128, 512]); htb = Buf("htmp")
                nws = psb_("nws", [128, 512]); nwsb = Buf("nws")

                def ssd_pre(g):
                    par = g % 2
                    x_tok = x_tok2[par]; x_tokb = x_tokb2[par]; BT = BT2[par]; BTb = BTb2[par]
                    CT = CT2[par]; CTb = CTb2[par]; B_tok = B_tok2[par]; B_tokb = B_tokb2[par]
                    (wx, wB, wC), wb1 = next_slab()
                    P.tag = "ssd_conv"
                    for c4 in range(4):
                        f = M["fmT"][c4 % 2]; fb_ = fmTb[c4 % 2]
                        fm_conv(wx, wb1, c4 * 128, g * 4 + c4, f[:], fb_)
                        transpose_blocks(f, fb_, 1, x_tok[:, :, c4 * 128:(c4 + 1) * 128], x_tokb, eng="dve")
                    fm_conv(wB, wb1, 0, 16 + g, BT[:], BTb)
                    transpose_blocks(BT, BTb, 1, B_tok[:, :, :], B_tokb, eng="dve")
                    fm_conv(wC, wb1, 0, 20 + g, CT[:], CTb)

                def ssd_core(g):
                    par = g % 2
                    x_tok = x_tok2[par]; x_tokb = x_tokb2[par]; BT = BT2[par]; BTb = BTb2[par]
                    CT = CT2[par]; CTb = CTb2[par]; B_tok = B_tok2[par]; B_tokb = B_tokb2[par]
                    (wz,), wb2 = next_slab()
                    P.dma("sp", lambda e, g=g: e.dma_start(out=nws[:], in_=bc_row(vecs["ssd_norm_w"], g * 512, 512)), writes=[nwsb])
                    if blk == 0 and g == 0:
                        dump("x_tok0", x_tok[:], x_tokb, [128, NT, 512], BF16)
                        dump("BT0", BT[:], [BTb], [128, TB], BF16)

                    def indep(i, g=g):
                        P.tag = "ssd_indep"
                        q = i % 3
                        ib = indb[q]
                        tt(rseg[q][:], a_tok[:, i, g * 8:(g + 1) * 8].unsqueeze(2).broadcast_to([128, 8, 64]),
                           tri[:, :].unsqueeze(1).broadcast_to([128, 8, 64]), ALU.mult, [smb[i], cb], [ib])
                        ps, pb = PS("s")
                        mm(ps[:], Umask[:], rseg[q][:].rearrange("p a b -> p (a b)"), True, True, [ib, cb], [pb])
                        act(esg[q][:].rearrange("p a b -> p (a b)"), ps[:], AF.Exp, [pb], [ib])
                        ps2, pb2 = PS("q")
                        mm(ps2[:, 0:128], BT[:, i * 128:(i + 1) * 128], CT[:, i * 128:(i + 1) * 128], True, True,
                           [BTb, CTb], [pb2])
                        tt(cbm[q][0:64, :], ps2[0:64, 0:64], tri[0:64, :], ALU.mult, [pb2, cb], [ib])
                        tt(cbm[q][64:128, :], ps2[64:128, 64:128], tri[64:128, :], ALU.mult, [pb2, cb], [ib])
                        tt(wpr[q][:], esg[q][:], cbm[q][:, :].unsqueeze(1).broadcast_to([128, 8, 64]), ALU.mult, [ib], [ib])
                        tt(xdt[q][:].rearrange("p (a b) -> p a b", a=8), x_tok[:, i, :].rearrange("p (a b) -> p a b", a=8),
                           dt_tok[:, i, g * 8:(g + 1) * 8].unsqueeze(2).broadcast_to([128, 8, 64]), ALU.mult,
                           [x_tokb[i], smb[i]], [ib], eng="pool")
                        tt(xw[q][:].rearrange("p (a b) -> p a b", a=8), x_tok[:, i, :].rearrange("p (a b) -> p a b", a=8),
                           dte[:, i, g * 8:(g + 1) * 8].unsqueeze(2).broadcast_to([128, 8, 64]), ALU.mult,
                           [x_tokb[i], smb[i]], [ib], eng="pool")

                    def seq(i, g=g):
                        P.tag = "ssd_seq"
                        q = i % 3
                        ib = indb[q]
                        q3 = i % 2
                        yis = yis2[q3]; yv = yis2[q3]; th = th2[q3]; t1 = th2[q3]; ysn = ysn2[q3]; seqb = seqb2[q3]
                        psy, pby = PS()
                        psi, pbi = PS()
                        psz, pbz = PS()
                        for k in range(8):
                            mm(psz[:], uT[:, k, i * 128:(i + 1) * 128], wz[:, k, :], k == 0, k == 7, [uTb[i], wb2], [pbz], inc=(k in (3, 7)))
                        for r in range(8):
                            for j in range(2):
                                sl = slice(64 * j, 64 * j + 64)
                                mm(psy[sl, r * 64:(r + 1) * 64], wpr[q][sl, r, :], xdt[q][sl, r * 64:(r + 1) * 64], True, False,
                                   [ib], [pby], inc=False)
                                mm(psy[sl, r * 64:(r + 1) * 64], dI[sl, g * 8 + r, :], x_tok[sl, i, r * 64:(r + 1) * 64], False, True,
                                   [cb, x_tokb[i]], [pby], inc=(r == 7 and j == 1))
                        for j in range(2):
                            sl = slice(64 * j, 64 * j + 64)
                            mm(psi[sl, :], CT[:, i * 128 + 64 * j:i * 128 + 64 * j + 64], hbf[:, g, :], True, True, [CTb, hbfb[g]], [pbi])
                            pss, pbs = PS("s2")
                            mm(pss[:], B_tok[sl, i, :], xw[q][sl, :], True, True, [B_tokb[i], ib], [pbs])
                            tt(htmp[:].rearrange("p (a b) -> p a b", a=8), hst[:, g, :].rearrange("p (a b) -> p a b", a=8),
                               sm[:, i, 2 + j, g * 8:(g + 1) * 8].unsqueeze(2).broadcast_to([128, 8, 64]), ALU.mult,
                               [hgb[g], smb[i]], [htb])
                            tt(hst[:, g, :], htmp[:], pss[:], ALU.add, [htb, pbs], [hgb[g]])
                            cp(hbf[:, g, :], hst[:, g, :], [hgb[g]], [hbfb[g]], eng="act")
                        tt(yis[:].rearrange("p (a b) -> p a b", a=8), psi[:].rearrange("p (a b) -> p a b", a=8),
                           sm[:, i, 0, g * 8:(g + 1) * 8].unsqueeze(2).broadcast_to([128, 8, 64]), ALU.mult, [pbi, smb[i]], [seqb])
                        tt(yv[:], psy[:], yis[:], ALU.add, [pby, seqb], [seqb])
                        if blk == 0 and g == 0 and i == 0:
                            dump("y_pre00", yv[:], [seqb], [128, 512])
                        act(th[:], psz[:], AF.Tanh, [pbz], [seqb], scale=0.5)
                        stt(t1[:], th[:], 1.0, psz[:], ALU.add, ALU.mult, [seqb, pbz], [seqb])
                        tt(t1[:], t1[:], yv[:], ALU.mult, [seqb], [seqb])
                        st, stbuf = STAT()
                        jk, jkb = JUNK()
                        act(jk[:, 0:512], t1[:], AF.Square, [seqb], [stbuf, jkb], accum=st[:, 0:1])
                        rstd_from_ss(st, stbuf, 512.0, 4.0 * EPS)
                        stt(ysn[:], t1[:], st[:, 1:2], nws[:], ALU.mult, ALU.mult, [seqb, stbuf, nwsb], [seqb])
                        pst, pbt = PS("t")
                        pbf = pst.bitcast(BF16)
                        for c in range(4):
                            tr(pbf[:, c * 128:(c + 1) * 128], ysn[:, c * 128:(c + 1) * 128], identb[:], [seqb, cb], [pbt], inc=(c == 3))
                        cp(big[:, g * 4:(g + 1) * 4, i * 128:(i + 1) * 128], pbf[:, 0:512].rearrange("p (c t) -> p c t", c=4),
                           [pbt], [bigb[i]], eng="act")

                    for i in range(NT + 1):
                        if i < NT:
                            indep(i)
                        if i > 0:
                            seq(i - 1)

                ssd_pre(0)
                for g in range(4):
                    if g < 3:
                        ssd_pre(g + 1)
                    ssd_core(g)
                if blk == 0:
                    dump("yT", big[:], bigb, [128, 16, TB], BF16)
            if stop_after == "SSD":
                break

            set_rings(LAY_DEFAULT)
            phA = Phase(); phA.__enter__()
            mixed = sb("mixed", [128, NT, 1024], BF16)
            gtok = sb("gtok", [128, NT, 5, 4]); gtokb = Buf("gtok")
            aold_bc = sb("aold_bc", [128, 4, 16]); aoldb = Buf("aold")
            P.tag = "post_ssd"
            with Phase():
                P.tag = "ml_gates"
                rmask = sb("rmask", [4, TB]); negbig = sb("negbig", [4, TB])
                G_l1 = sb("G_l1", [4, TB]); G_cs = sb("G_cs", [4, TB]); G_e = sb("G_e", [4, TB])
                G_Ml = sb("G_Ml", [4, TB]); G_M = sb("G_M", [4, TB]); G_t = sb("G_t", [4, TB])
                gs = sb("gs", [4, 8, 16])
                gb = Buf("gates")
                memset(rmask[:], 1.0, [gb])
                memset(rmask[:].rearrange("p (c t) -> p c t", t=64)[:, :, 0:1], 0.0, [gb])
                memset(negbig[:], 0.0, [gb])
                memset(negbig[:].rearrange("p (c t) -> p c t", t=64)[:, :, 0:1], -1e30, [gb])
                for tb in range(TB // 512):
                    ps, pb = PS(); ps2, pb2 = PS()
                    for k in range(8):
                        mm(ps[0:4, :], wsm[:, k, 32:36], uT[:, k, tb * 512:(tb + 1) * 512], k == 0, k == 7,
                           [wsmb] + uTb[tb * 4:(tb + 1) * 4], [pb], inc=(k == 7))
                    for k in range(8):
                        mm(ps2[0:4, :], wsm[:, k, 36:40], uT[:, k, tb * 512:(tb + 1) * 512], k == 0, k == 7,
                           [wsmb] + uTb[tb * 4:(tb + 1) * 4], [pb2], inc=(k == 7))
                    act(G_e[:, tb * 512:(tb + 1) * 512], ps[0:4, :], AF.Identity, [pb, cb], [gb], bias=ib4[:, 0:1])
                    act(G_l1[:, tb * 512:(tb + 1) * 512], ps2[0:4, :], AF.Exp, [pb2, cb], [gb], bias=negfb[:, 0:1], scale=-1.0)
                act(G_l1[:], G_l1[:], AF.Ln, [gb], [gb], bias=1.0)
                P.op("dve", lambda e: e.tensor_tensor_scan(G_cs[:], rmask[:], G_l1[:], 0.0, ALU.mult, ALU.add), [gb], [gb], cost=2.3)
                tt(G_e[:], G_e[:], G_cs[:], ALU.add, [gb], [gb])
                P.op("dve", lambda e: e.tensor_tensor_scan(G_Ml[:], negbig[:], G_e[:], 0.0, ALU.add, ALU.max), [gb], [gb], cost=2.3)

                def v3(t):
                    return t[:].rearrange("p (c t) -> p c t", t=64)
                csend = v3(G_cs)[:, :, 63]; emax = v3(G_Ml)[:, :, 63]
                mloc = gs[:, 0, :]; bend = gs[:, 1, :]; maft = gs[:, 2, :]; mprev = gs[:, 3, :]; dd = gs[:, 4, :]; ao = gs[:, 5, :]
                tt(mloc, emax, csend, ALU.subtract, [gb], [gb])
                ts(bend, csend, -1.0, None, ALU.mult, reads=[gb], writes=[gb])
                P.op("dve", lambda e: e.tensor_tensor_scan(maft, bend, mloc, mcar[:, 0:1], ALU.add, ALU.max), [gb, mcarb], [gb])
                cp(mprev[:, 0:1], mcar[:, 0:1], [mcarb, gb], [gb])
                cp(mprev[:, 1:16], maft[:, 0:15], [gb], [gb])
                cp(mcar[:, 0:1], maft[:, 15:16], [gb], [mcarb])
                tt(v3(G_M), v3(G_Ml), mprev.unsqueeze(2).broadcast_to([4, 16, 64]), ALU.max, [gb], [gb])
                psT, pbT = PS()

                def to_tok(src, qi):
                    for i in range(NT):
                        c0 = (i * 5 + qi) * 4
                        mm(psT[:, c0:c0 + 4], src[0:4, i * 128:(i + 1) * 128], identf[0:4, 0:4], True, True, [gb, cb], [pbT],
                           inc=(i == NT - 1))
                ts(G_t[:], G_e[:], LNSCALE, None, ALU.add, reads=[gb], writes=[gb])
                to_tok(G_t, 0)
                to_tok(G_M, 1)
                tt(v3(G_t), mprev.unsqueeze(2).broadcast_to([4, 16, 64]), v3(G_M), ALU.subtract, [gb, pbT], [gb])
                act(G_t[:], G_t[:], AF.Exp, [gb], [gb], bias=LNSCALE)
                to_tok(G_t, 2)
                tt(G_t[:], G_cs[:], G_M[:], ALU.subtract, [gb, pbT], [gb])
                act(G_t[:], G_t[:], AF.Exp, [gb], [gb])
                to_tok(G_t, 3)
                tt(dd, bend, maft, ALU.subtract, [gb], [gb])
                tt(v3(G_t), v3(G_e), dd.unsqueeze(2).broadcast_to([4, 16, 64]), ALU.add, [gb, pbT], [gb])
                act(G_t[:], G_t[:], AF.Exp, [gb], [gb])
                to_tok(G_t, 4)
                cp(gtok[:].rearrange("p a b c -> p (a b c)"), psT[:, 0:NT * 20], [pbT], [gtokb])
                tt(ao, dd, mprev, ALU.add, [gb], [gb])
                act(ao, ao, AF.Exp, [gb], [gb])
                Rx = sb("Rx", [4, 4, 16])
                tt(Rx[:], ao.unsqueeze(1).broadcast_to([4, 4, 16]), I4x[:, :, 0:16], ALU.mult, [gb, cb], [gb])
                psA, pbA = PS()
                mm(psA[:, 0:64], ones4[:], Rx[:].rearrange("p a b -> p (a b)"), True, True, [gb, cb], [pbA])
                cp(aold_bc[:].rearrange("p a b -> p (a b)"), psA[:, 0:64], [pbA], [aoldb])


                P.tag = "post_ssd"
                sgt = sb("sgt", [128, NT, 1024], BF16); sgtb = [Buf("sgt%d" % i) for i in range(NT)]
                (wg,), wbg = next_slab()
                for i in range(NT):
                    for cbk in range(2):
                        psg, pbg = PS()
                        for k in range(8):
                            mm(psg[:], uT[:, k, i * 128:(i + 1) * 128], wg[:, k, cbk * 512:(cbk + 1) * 512], k == 0, k == 7,
                               [uTb[i], wbg], [pbg], inc=(k == 7))
                        act(sgt[:, i, cbk * 512:(cbk + 1) * 512], psg[:], AF.Sigmoid, [pbg], [sgtb[i]])
                for cbk in range(2):
                    (wbr,), wbb = next_slab()
                    for i in range(NT):
                        psr, pbr = PS()
                        for k in range(16):
                            mm(psr[:], big[:, k, i * 128:(i + 1) * 128], wbr[:, k, :], k == 0, k == 15, [bigb[i], wbb], [pbr], inc=(k == 15))
                        tt(mixed[:, i, cbk * 512:(cbk + 1) * 512], sgt[:, i, cbk * 512:(cbk + 1) * 512], psr[:], ALU.mult,
                           [sgtb[i], pbr], [mixb[i]])

            P.tag = "ml_gates"
            with Phase():
                with Phase():
                    set_rings(LAY_ML)
                    alloc_conv_bufs()
                    qT2 = [sb("qT", [128, 2, TB], BF16) for _ in range(2)]; qTb2 = [Buf("qT") for _ in range(2)]
                    kT2 = [sb("kT", [128, 2, TB], BF16) for _ in range(2)]; kTb2 = [Buf("kT") for _ in range(2)]
                    k_tok2 = [sb("k_tok", [128, NT, 256], BF16) for _ in range(2)]
                    k_tokb2 = [[Buf("ktok%d" % i) for i in range(NT)] for _ in range(2)]
                    vext2 = [sb("vext", [128, NT, 258], BF16) for _ in range(2)]
                    vextb2 = [[Buf("vext%d" % i) for i in range(NT)] for _ in range(2)]
                    osig2 = [sb("osig", [128, NT, 256], BF16) for _ in range(2)]
                    osigb2 = [[Buf("osig%d" % i) for i in range(NT)] for _ in range(2)]
                    nwm2 = [sb("nwm", [128, 256]) for _ in range(2)]; nwmb2 = [Buf("nwm") for _ in range(2)]
                    Md = [sb("Md%d" % i, [128, 64]) for i in range(2)]
                    Dm = [sb("Dm%d" % i, [128, 64]) for i in range(2)]
                    Sx = [sb("Sx%d" % i, [128, 64], BF16) for i in range(2)]
                    vw = [sb("vw%d" % i, [128, 258], BF16) for i in range(2)]
                    mib = [Buf("mind%d" % i) for i in range(2)]
                    num2 = [sb("num", [128, 258]) for _ in range(2)]
                    hb2 = [sb("hb", [128, 256], BF16) for _ in range(2)]
                    msb2 = [Buf("mseq%d" % i) for i in range(2)]
                    for par_ in range(2):
                        vinit = Buf("vinit")
                        memset(vext2[par_][:, :, 256:257], 1.0, [vinit]); memset(vext2[par_][:, :, 257:258], 0.0, [vinit])
                        for b_ in vextb2[par_]:
                            b_.w = vinit.w

                    def sel(h):
                        p_ = h % 2
                        return (qT2[p_], qTb2[p_], kT2[p_], kTb2[p_], k_tok2[p_], k_tokb2[p_], vext2[p_], vextb2[p_],
                                osig2[p_], osigb2[p_], nwm2[p_], nwmb2[p_])

                    def ml_pre(h):
                        qT, qTb, kT, kTb, k_tok, k_tokb, vext, vextb, osig, osigb, nwm, nwmb = sel(h)
                        (wq, wk, wv, wo), wbh = next_slab()
                        wvo = slab_t[(slab_cur[0] - 1) % NSLAB][:, 4096:8192].rearrange("p (s k n) -> p s k n", s=2, k=8)
                        P.tag = "ml_conv"
                        P.dma("sp", lambda e, h=h: e.dma_start(out=nwm[:], in_=bc_row(vecs["mlstm_norm_w"], h * 256, 256)), writes=[nwmb])
                        for half in range(2):
                            fm_conv(wq, wbh, half * 128, 24 + h * 2 + half, qT[:, half, :], qTb)
                        for half in range(2):
                            fm_conv(wk, wbh, half * 128, 32 + h * 2 + half, kT[:, half, :], kTb)
                        for half in range(2):
                            transpose_blocks(kT[:, half, :], kTb, 1, k_tok[:, :, half * 128:(half + 1) * 128], k_tokb)
                        for i in range(NT):
                            ps, pb = PS("conv")
                            for k in range(8):
                                mm(ps[:, 0:512].rearrange("p (s n) -> p s n", s=2), uT[:, k, i * 128:(i + 1) * 128], wvo[:, :, k, :], k == 0, k == 7,
                                   [uTb[i], wbh], [pb], inc=(k in (3, 7)))
                            cp(vext[:, i, 0:256], ps[:, 0:256], [pb], [vextb[i]], eng="act")
                            act(osig[:, i, :], ps[:, 256:512], AF.Sigmoid, [pb], [osigb[i]])

                    def ml_core(h):
                        qT, qTb, kT, kTb, k_tok, k_tokb, vext, vextb, osig, osigb, nwm, nwmb = sel(h)

                        def mindep(i, h=h):
                            P.tag = "ml_indep"
                            q = i % 2
                            ib = mib[q]
                            ts(Md[q][:], Idm[:], gtok[:, i, 1, h:h + 1], None, ALU.mult, reads=[gtokb, cb], writes=[ib])
                            psM, pbM = PS("m64")
                            mm(psM[:, 0:64], Bd[:], Md[q][:], True, True, [ib, cb], [pbM])
                            act(Dm[q][:], psM[:, 0:64], AF.Exp, [pbM, gtokb], [ib], bias=gtok[:, i, 0, h:h + 1], scale=-1.0)
                            tt(Dm[q][:], Dm[q][:], tri[:], ALU.mult, [ib, cb], [ib])
                            psq, pbq = PS("q")
                            for half in range(2):
                                mm(psq[:, 0:128], kT[:, half, i * 128:(i + 1) * 128], qT[:, half, i * 128:(i + 1) * 128], half == 0, half == 1,
                                   [kTb, qTb], [pbq], inc=(half == 1))
                            tt(Sx[q][0:64, :], psq[0:64, 0:64], Dm[q][0:64, :], ALU.mult, [pbq, ib], [ib])
                            tt(Sx[q][64:128, :], psq[64:128, 64:128], Dm[q][64:128, :], ALU.mult, [pbq, ib], [ib])
                            ts(vw[q][:], vext[:, i, :], gtok[:, i, 4, h:h + 1], None, ALU.mult, reads=[vextb[i], gtokb], writes=[ib])

                        def mseq(i, h=h):
                            P.tag = "ml_seq"
                            q = i % 2
                            ib = mib[q]
                            tmpi = num2[q]; num = num2[q]; hn = num2[q]; hb = hb2[q]; msb = msb2[q]
                            psn, pbn = PS(); psi, pbi = PS()
                            for j in range(2):
                                sl = slice(64 * j, 64 * j + 64)
                                c = 2 * i + j
                                mm(psn[sl, 0:258], Sx[q][sl, :], vext[sl, i, :], True, True, [ib, vextb[i]], [pbn])
                                for half in range(2):
                                    mm(psi[sl, 0:258], qT[:, half, i * 128 + 64 * j:i * 128 + 64 * j + 64], Cbf[:, h, half, :], half == 0, half == 1,
                                       [qTb, Cbfb[h]], [pbi], inc=(half == 1))
                                psc, pbc = PS("c")
                                pcn, pbcn = PS("n")
                                for half in range(2):
                                    mm(psc[:, half * 256:(half + 1) * 256], k_tok[sl, i, half * 128:(half + 1) * 128], vw[q][sl, 0:256], True, True,
                                       [k_tokb[i], ib], [pbc], inc=(half == 1))
                                for half in range(2):
                                    mm(pcn[:, half * 2:half * 2 + 2], k_tok[sl, i, half * 128:(half + 1) * 128], vw[q][sl, 256:258], True, True,
                                       [k_tokb[i], ib], [pbcn], inc=(half == 1))
                                stt(Cst[:, h, :, 0:256], Cst[:, h, :, 0:256], aold_bc[:, h, c:c + 1], psc[:, :].rearrange("p (a b) -> p a b", a=2),
                                    ALU.mult, ALU.add, [Chb[h], aoldb, pbc], [Chb[h]])
                                stt(Cst[:, h, :, 256:258], Cst[:, h, :, 256:258], aold_bc[:, h, c:c + 1],
                                    pcn[:, 0:4].rearrange("p (a b) -> p a b", a=2), ALU.mult, ALU.add, [Chb[h], aoldb, pbcn], [Chb[h]])
                                cp(Cbf[:, h, :, :], Cst[:, h, :, :], [Chb[h]], [Cbfb[h]], eng="act")
                            act(tmpi[:], psi[:, 0:258], AF.Identity, [pbi, gtokb], [msb], scale=gtok[:, i, 2, h:h + 1])
                            tt(num[:], tmpi[:], psn[:, 0:258], ALU.add, [msb, pbn], [msb])
                            st, stbuf = STAT()
                            ts(st[:, 2:3], num[:, 256:257], -1.0, None, ALU.mult, reads=[msb], writes=[stbuf])
                            tt(st[:, 2:3], st[:, 2:3], num[:, 256:257], ALU.max, [msb, stbuf], [stbuf])
                            tt(st[:, 2:3], st[:, 2:3], gtok[:, i, 3, h:h + 1], ALU.max, [stbuf, gtokb], [stbuf])
                            P.op("dve", lambda e: e.reciprocal(st[:, 3:4], st[:, 2:3]), [stbuf], [stbuf])
                            jk, jkb = JUNK()
                            act(jk[:, 0:256], num[:, 0:256], AF.Square, [msb, stbuf], [stbuf, jkb], scale=st[:, 3:4], accum=st[:, 0:1])
                            rstd_from_ss(st, stbuf, 256.0, EPS)
                            tt(st[:, 2:3], st[:, 1:2], st[:, 3:4], ALU.mult, [stbuf], [stbuf])
                            stt(hn[:, 0:256], num[:, 0:256], st[:, 2:3], nwm[:], ALU.mult, ALU.mult, [msb, stbuf, nwmb], [msb])
                            tt(hb[:], hn[:, 0:256], osig[:, i, :], ALU.mult, [msb, osigb[i]], [msb])
                            pst, pbt = PS("t")
                            pbf = pst.bitcast(BF16)
                            for c2 in range(2):
                                tr(pbf[:, c2 * 128:(c2 + 1) * 128], hb[:, c2 * 128:(c2 + 1) * 128], identb[:], [msb, cb], [pbt], inc=(c2 == 1))
                            cp(big[:, h * 2:(h + 1) * 2, i * 128:(i + 1) * 128], pbf[:, 0:256].rearrange("p (c t) -> p c t", c=2),
                               [pbt], [bigb[i]], eng="act")

                        for i in range(NT + 1):
                            if i < NT:
                                mindep(i)
                            if i > 0:
                                mseq(i - 1)

                    ml_pre(0)
                    for h in range(4):
                        if h < 3:
                            ml_pre(h + 1)
                        ml_core(h)
                if blk == 0:
                    dump("hT", big[:, 0:8, :], bigb, [128, 8, TB], BF16)
            if stop_after == "ML":
                break

            set_rings(LAY_DEFAULT)
            P.tag = "post_ml"
            with Phase():
                h1 = sb("h1", [128, NT, 1024])
                alloc_norm_bufs()
                sgt = sb("sgt", [128, NT, 1024], BF16); sgtb = [Buf("sgt%d" % i) for i in range(NT)]
                tmpm = [sb("tmpm%d" % i, [128, 512]) for i in range(2)]; sgb = [Buf("tmpm%d" % i) for i in range(2)]
                (wg2,), wbg = next_slab()
                for i in range(NT):
                    for cbk in range(2):
                        psg, pbg = PS()
                        for k in range(8):
                            mm(psg[:], uT[:, k, i * 128:(i + 1) * 128], wg2[:, k, cbk * 512:(cbk + 1) * 512], k == 0, k == 7,
                               [uTb[i], wbg], [pbg], inc=(k == 7))
                        act(sgt[:, i, cbk * 512:(cbk + 1) * 512], psg[:], AF.Sigmoid, [pbg], [sgtb[i]])
                (wbm,), wbb = next_slab()
                cnt = 0
                for i in range(NT):
                    for cbk in range(2):
                        psr, pbr = PS()
                        for k in range(8):
                            mm(psr[:], big[:, k, i * 128:(i + 1) * 128], wbm[:, k, cbk * 512:(cbk + 1) * 512], k == 0, k == 7,
                               [bigb[i], wbb], [pbr], inc=(k == 7))
                        q = cnt % 2; cnt += 1
                        tt(tmpm[q][:], sgt[:, i, cbk * 512:(cbk + 1) * 512], psr[:], ALU.mult, [sgtb[i], pbr], [sgb[q]])
                        tt(mixed[:, i, cbk * 512:(cbk + 1) * 512], mixed[:, i, cbk * 512:(cbk + 1) * 512], tmpm[q][:], ALU.add,
                           [sgb[q], mixb[i]], [mixb[i]])
                if blk == 0:
                    dump("mixed", mixed[:], mixb, [128, NT, 1024], BF16)
                for i in range(NT):
                    ps, pb = PS()
                    pbf = ps.bitcast(BF16)
                    for k in range(8):
                        tr(pbf[:, k * 128:(k + 1) * 128], mixed[:, i, k * 128:(k + 1) * 128], identb[:], [mixb[i], cb], [pb], inc=(k == 7))
                    cp(big[:, 8:16, i * 128:(i + 1) * 128], pbf[:, 0:1024].rearrange("p (k t) -> p k t", k=8), [pb], [bigb[i]], eng="act")
                (wo_,), wbo = next_slab()
                for i in range(NT):
                    xt = M["xin"][i % 2]
                    src = x_d.ap()[t0 + i * 128:t0 + (i + 1) * 128, :]
                    P.dma("sp", lambda e, xt=xt, src=src: e.dma_start(out=xt[:], in_=src), writes=[xinb[i % 2]])
                    for cbk in range(2):
                        ps, pb = PS()
                        for k in range(8):
                            mm(ps[:], big[:, 8 + k, i * 128:(i + 1) * 128], wo_[:, k, cbk * 512:(cbk + 1) * 512], k == 0, k == 7,
                               [bigb[i], wbo], [pb], inc=(k == 7))
                        tt(h1[:, i, cbk * 512:(cbk + 1) * 512], ps[:], xt[:, cbk * 512:(cbk + 1) * 512], ALU.add, [pb, xinb[i % 2]], [h1b[i]])
                if blk == 0:
                    dump("h1", h1[:], h1b, [128, NT, 1024])
                P.tag = "mlp"
                load_nw("norm_mlp_w", 0, 1024)
                norm_to_uT(lambda i: h1[:, i, :], lambda i: h1b[i], t0)
                hidb2 = [[Buf("hid%d" % i) for i in range(TB // 512)] for _ in range(2)]
                rl = tmpm; rlb = sgb
                cnt = 0
                for qf in range(4):
                    (wu,), wbu = next_slab()
                    for e2 in range(2):
                        hidb = hidb2[e2]
                        for f4 in range(4):
                            fg = e2 * 4 + f4
                            for tb in range(TB // 512):
                                ps, pb = PS()
                                for k in range(8):
                                    mm(ps[:], wu[:, k, fg * 128:(fg + 1) * 128], uT[:, k, tb * 512:(tb + 1) * 512], k == 0, k == 7,
                                       [wbu] + uTb[tb * 4:(tb + 1) * 4], [pb], inc=(k == 7))
                                q = cnt % 2; cnt += 1
                                act(rl[q][:], ps[:], AF.Relu, [pb], [rlb[q]])
                                tt(sgt[:, fg, tb * 512:(tb + 1) * 512], rl[q][:], rl[q][:], ALU.mult, [rlb[q]], [hidb[tb]])
                    (wd,), wbd = next_slab()
                    for e2 in range(2):
                        hidb = hidb2[e2]
                        for i in range(NT):
                            for cbk in range(2):
                                ps, pb = PS()
                                for c4 in range(4):
                                    c = e2 * 4 + c4
                                    mm(ps[:], sgt[:, c, i * 128:(i + 1) * 128], wd[:, c, cbk * 512:(cbk + 1) * 512], c4 == 0, c4 == 3,
                                       [hidb[i // 4], wbd], [pb], inc=(c4 == 3))
                                tt(h1[:, i, cbk * 512:(cbk + 1) * 512], h1[:, i, cbk * 512:(cbk + 1) * 512], ps[:], ALU.add, [pb, h1b[i]], [h1b[i]])
                if blk == 0:
                    dump("h2", h1[:], h1b, [128, NT, 1024])
                load_nw("norm_final_w", 0, 1024)
                for i in range(NT):
                    st, stbuf = STAT()
                    jk, jkb = JUNK()
                    act(jk[:], h1[:, i, :], AF.Square, [h1b[i]], [stbuf, jkb], accum=st[:, 0:1])
                    rstd_from_ss(st, stbuf, 1024.0, EPS)
                    ot = M["xin"][i % 2]
                    stt(ot[:], h1[:, i, :], st[:, 1:2], M["nw"][:], ALU.mult, ALU.mult, [h1b[i], stbuf, nwb], [xinb[i % 2]])
                    dst = y_d.ap()[t0 + i * 128:t0 + (i + 1) * 128, :]
                    tok = P.dma("sp", lambda e, ot=ot, dst=dst: e.dma_start(out=dst, in_=ot[:]), reads=[xinb[i % 2]])
                    outtoks.append(tok)
            phA.__exit__()

        P.wait_all("sp", dumptoks + outtoks)
        P.emit(sems)
    return nc, dump_t


def make_in_map(inputs, b):
    m = {"x": np.ascontiguousarray(np.asarray(inputs["x"])[b], dtype=np.float32)}
    for n in ("w_in", "w_br_ssd", "w_br_mlstm", "w_out", "w_up", "w_down", "conv_ssd_w", "conv_qk_w"):
        m[n] = np.ascontiguousarray(np.asarray(inputs[n])[0], dtype=np.float32)
    for n in ("conv_ssd_b", "conv_qk_b", "norm_mix_w", "norm_mlp_w", "ssd_norm_w", "mlstm_norm_w", "dt_bias",
              "a_log", "d_skip", "i_bias", "f_bias"):
        m[n] = np.ascontiguousarray(np.asarray(inputs[n]).reshape(1, -1), dtype=np.float32)
    m["norm_final_w"] = np.ascontiguousarray(np.asarray(inputs["norm_final_w"]).reshape(1, -1), dtype=np.float32)
    return m


def kernel(**inputs):
    nc, _ = build()
    in_maps = [make_in_map(inputs, b) for b in range(8)]
    res = run_bass_kernel_spmd(nc, in_maps, core_ids=list(range(8)))
    return np.stack([np.asarray(r["y"], dtype=np.float32) for r in res.results], axis=0)
```
